# Optimizing a Trainium2 kernel written in Bass

```python
import jax, jax.numpy as jnp
from jax import lax
import numpy as np

D_MODEL = 1024
BATCH = 4
SEQ = 4096
DEPTH = 4

HEAD_DIM = 64
GROUP_HEADS = 4
GROUP_WIDTH = GROUP_HEADS * HEAD_DIM
N_GROUPS = 4
MIX_WIDTH = N_GROUPS * GROUP_WIDTH
BLOCK = 128
NEG_INF = -1e30

FOX_H = GROUP_HEADS
FOX_BIAS_INIT = 3.0
MLA_H = GROUP_HEADS
MLA_Q_RANK = D_MODEL // 4
MLA_KV_RANK = D_MODEL // 8
MLA_NOPE = HEAD_DIM
MLA_ROPE = HEAD_DIM // 2
MLA_V = HEAD_DIM
ROPE_THETA = 10000.0
SB_H = GROUP_HEADS
SWA_H = GROUP_HEADS
SWA_KV_H = 2
WINDOW = 128
D_FF = ((8 * D_MODEL + 767) // 768) * 256
ALPHA = (2.0 * DEPTH) ** 0.25
BETA = (8.0 * DEPTH) ** -0.25

SPLIT_SIZES = (
    FOX_H * HEAD_DIM, FOX_H * HEAD_DIM, FOX_H * HEAD_DIM, FOX_H,
    MLA_Q_RANK, MLA_KV_RANK, MLA_ROPE,
    SB_H * HEAD_DIM, SB_H * HEAD_DIM, SB_H * HEAD_DIM,
    SWA_H * HEAD_DIM, SWA_KV_H * HEAD_DIM, SWA_KV_H * HEAD_DIM,
)
IN_WIDTH = sum(SPLIT_SIZES)

kernel_name = 'hybrid_fox_mla_stickbreak_swa_deepnorm'


def _layernorm(x, g, b, eps=1e-5):
    xf = x.astype(jnp.float32)
    mu = jnp.mean(xf, axis=-1, keepdims=True)
    var = jnp.mean(jnp.square(xf - mu), axis=-1, keepdims=True)
    return ((xf - mu) * lax.rsqrt(var + eps) * g + b).astype(x.dtype)


def _rmsnorm(x, g, eps=1e-6):
    xf = x.astype(jnp.float32)
    return (xf * lax.rsqrt(jnp.mean(jnp.square(xf), axis=-1, keepdims=True) + eps) * g).astype(x.dtype)


def _group_rmsnorm(mix, g, eps=1e-6):
    B, S, _ = mix.shape
    xf = mix.astype(jnp.float32).reshape(B, S, N_GROUPS, GROUP_WIDTH)
    xf = xf * lax.rsqrt(jnp.mean(jnp.square(xf), axis=-1, keepdims=True) + eps)
    return (xf.reshape(B, S, MIX_WIDTH) * g).astype(mix.dtype)


def _heads(t, n):
    B, S, _ = t.shape
    return t.reshape(B, S, n, -1)


def _rope_tables(S):
    pos = jnp.arange(S, dtype=jnp.float32)
    inv = ROPE_THETA ** (-jnp.arange(0, MLA_ROPE, 2, dtype=jnp.float32) / MLA_ROPE)
    ang = pos[:, None] * inv[None, :]
    return jnp.cos(ang), jnp.sin(ang)


def _rope(x, cos, sin):
    x1, x2 = jnp.split(x.astype(jnp.float32), 2, axis=-1)
    c = cos[None, :, None, :]
    s = sin[None, :, None, :]
    return jnp.concatenate([x1 * c - x2 * s, x1 * s + x2 * c], axis=-1).astype(x.dtype)


def _alibi_slopes(n):
    return jnp.exp2(-8.0 * jnp.arange(1, n + 1, dtype=jnp.float32) / n)


def _causal_softmax_blocks(q, k, v, scale, cum_log_f=None):
    B, S, H, _ = q.shape
    nb = S // BLOCK
    qb = q.reshape(B, nb, BLOCK, H, -1).swapaxes(0, 1)
    kpos = jnp.arange(S)

    def one_block(args):
        i, qi = args
        s = jnp.einsum('bqhd,bkhd->bhqk', qi, k).astype(jnp.float32) * scale
        qpos = i * BLOCK + jnp.arange(BLOCK)
        if cum_log_f is not None:
            cq = lax.dynamic_slice_in_dim(cum_log_f, i * BLOCK, BLOCK, axis=2)
            s = s + cq[..., :, None] - cum_log_f[..., None, :]
        s = jnp.where(kpos[None, :] <= qpos[:, None], s, NEG_INF)
        p = jax.nn.softmax(s, axis=-1).astype(v.dtype)
        return jnp.einsum('bhqk,bkhd->bqhd', p, v)

    out = lax.map(one_block, (jnp.arange(nb), qb))
    return out.swapaxes(0, 1).reshape(B, S, H, -1)


def _stick_breaking_blocks(q, k, v):
    B, S, H, D = q.shape
    nb = S // BLOCK
    scale = D ** -0.5
    qb = q.reshape(B, nb, BLOCK, H, D).swapaxes(0, 1)
    kpos = jnp.arange(S)

    def one_block(args):
        i, qi = args
        z = jnp.einsum('bqhd,bkhd->bhqk', qi, k).astype(jnp.float32) * scale
        qpos = i * BLOCK + jnp.arange(BLOCK)
        strict = kpos[None, :] < qpos[:, None]
        log_1mb = jnp.where(strict, jax.nn.log_sigmoid(-z), 0.0)
        between = lax.cumsum(log_1mb, axis=3, reverse=True) - log_1mb
        a = jnp.where(strict, jnp.exp(jax.nn.log_sigmoid(z) + between), 0.0)
        return jnp.einsum('bhqk,bkhd->bqhd', a.astype(v.dtype), v)

    out = lax.map(one_block, (jnp.arange(nb), qb))
    return out.swapaxes(0, 1).reshape(B, S, H, D)


def _mla(cq, ckv, kr, g_q, g_kv, w_uq, w_ukv, cos, sin):
    B, S, _ = cq.shape
    q = (_rmsnorm(cq, g_q) @ w_uq).reshape(B, S, MLA_H, MLA_NOPE + MLA_ROPE)
    kv = (_rmsnorm(ckv, g_kv) @ w_ukv).reshape(B, S, MLA_H, MLA_NOPE + MLA_V)
    k_nope, v = kv[..., :MLA_NOPE], kv[..., MLA_NOPE:]
    q = jnp.concatenate([q[..., :MLA_NOPE], _rope(q[..., MLA_NOPE:], cos, sin)], axis=-1)
    k_rope = jnp.broadcast_to(_rope(kr[:, :, None, :], cos, sin), (B, S, MLA_H, MLA_ROPE))
    k = jnp.concatenate([k_nope, k_rope.astype(k_nope.dtype)], axis=-1)
    return _causal_softmax_blocks(q.astype(k.dtype), k, v, (MLA_NOPE + MLA_ROPE) ** -0.5)


def _swa_sink_alibi(q, k, v, sinks, slopes):
    B, S, Hq, D = q.shape
    Hkv = k.shape[2]
    G = Hq // Hkv
    nb = S // BLOCK
    qb = q.reshape(B, nb, BLOCK, Hkv, G, D)
    pad = jnp.zeros((B, BLOCK, Hkv, D), k.dtype)
    kp = jnp.concatenate([pad, k], axis=1).reshape(B, nb + 1, BLOCK, Hkv, D)
    vp = jnp.concatenate([pad.astype(v.dtype), v], axis=1).reshape(B, nb + 1, BLOCK, Hkv, D)
    kb = jnp.concatenate([kp[:, :-1], kp[:, 1:]], axis=2)
    vb = jnp.concatenate([vp[:, :-1], vp[:, 1:]], axis=2)
    s = jnp.einsum('bnqhgd,bnkhd->bnhgqk', qb, kb).astype(jnp.float32) * (D ** -0.5)
    dist = jnp.arange(BLOCK)[:, None] + BLOCK - jnp.arange(2 * BLOCK)[None, :]
    band = (dist >= 0) & (dist < WINDOW)
    kpos = (jnp.arange(nb)[:, None] - 1) * BLOCK + jnp.arange(2 * BLOCK)[None, :]
    valid = band[None, :, :] & (kpos >= 0)[:, None, :]
    s = s - slopes.reshape(Hkv, G)[:, :, None, None] * dist.astype(jnp.float32)
    s = jnp.where(valid[None, :, None, None], s, NEG_INF)
    sink = jnp.broadcast_to(sinks.astype(jnp.float32).reshape(Hkv, G)[None, None, :, :, None, None],
                            s.shape[:-1] + (1,))
    p = jax.nn.softmax(jnp.concatenate([s, sink], axis=-1), axis=-1)[..., :-1]
    out = jnp.einsum('bnhgqk,bnkhd->bnqhgd', p.astype(v.dtype), vb)
    return out.reshape(B, S, Hq, D)


def setup_inputs(seed: int = 0) -> dict:
    key = jax.random.key(seed)
    ks = jax.random.split(key, 17)
    L = DEPTH

    def nrm(k, shape, scale):
        return jax.random.normal(k, shape, jnp.float32) * scale

    def gain(k, shape):
        return 1.0 + 0.02 * jax.random.normal(k, shape, jnp.float32)

    return {
        'x': nrm(ks[0], (BATCH, SEQ, D_MODEL), 1.0),
        'w_in': nrm(ks[1], (L, D_MODEL, IN_WIDTH), D_MODEL ** -0.5),
        'fox_b_f': FOX_BIAS_INIT + nrm(ks[2], (L, FOX_H), 0.1),
        'mla_g_q': gain(ks[3], (L, MLA_Q_RANK)),
        'mla_g_kv': gain(ks[4], (L, MLA_KV_RANK)),
        'mla_w_uq': nrm(ks[5], (L, MLA_Q_RANK, MLA_H * (MLA_NOPE + MLA_ROPE)), MLA_Q_RANK ** -0.5),
        'mla_w_ukv': nrm(ks[6], (L, MLA_KV_RANK, MLA_H * (MLA_NOPE + MLA_V)), MLA_KV_RANK ** -0.5),
        'swa_sinks': nrm(ks[7], (L, SWA_H), 0.5),
        'mix_g': gain(ks[8], (L, MIX_WIDTH)),
        'w_o': nrm(ks[9], (L, MIX_WIDTH, D_MODEL), BETA * MIX_WIDTH ** -0.5),
        'ln1_g': gain(ks[10], (L, D_MODEL)),
        'ln1_b': nrm(ks[11], (L, D_MODEL), 0.02),
        'w_gate': nrm(ks[12], (L, D_MODEL, D_FF), D_MODEL ** -0.5),
        'w_up': nrm(ks[13], (L, D_MODEL, D_FF), D_MODEL ** -0.5),
        'w_down': nrm(ks[14], (L, D_FF, D_MODEL), BETA * D_FF ** -0.5),
        'ln2_g': gain(ks[15], (L, D_MODEL)),
        'ln2_b': nrm(ks[16], (L, D_MODEL), 0.02),
    }


def reference(x, w_in, fox_b_f, mla_g_q, mla_g_kv, mla_w_uq, mla_w_ukv, swa_sinks, mix_g, w_o,
              ln1_g, ln1_b, w_gate, w_up, w_down, ln2_g, ln2_b):
    B, S, _ = x.shape
    cos, sin = _rope_tables(S)
    slopes = _alibi_slopes(SWA_H)
    points = [int(p) for p in np.cumsum(SPLIT_SIZES)[:-1]]
    for l in range(DEPTH):
        h = x @ w_in[l]
        (fq, fk, fv, fgate, cq, ckv, kr, sq, sk, sv, wq, wk, wv) = jnp.split(h, points, axis=-1)
        log_f = jax.nn.log_sigmoid(fgate.astype(jnp.float32) + fox_b_f[l].astype(jnp.float32))
        cum = jnp.cumsum(log_f, axis=1).transpose(0, 2, 1)
        out_a = _causal_softmax_blocks(_heads(fq, FOX_H), _heads(fk, FOX_H), _heads(fv, FOX_H),
                                       HEAD_DIM ** -0.5, cum)
        out_b = _mla(cq, ckv, kr, mla_g_q[l], mla_g_kv[l], mla_w_uq[l], mla_w_ukv[l], cos, sin)
        out_c = _stick_breaking_blocks(_heads(sq, SB_H), _heads(sk, SB_H), _heads(sv, SB_H))
        out_d = _swa_sink_alibi(_heads(wq, SWA_H), _heads(wk, SWA_KV_H), _heads(wv, SWA_KV_H),
                                swa_sinks[l], slopes)
        mix = jnp.concatenate([out_a.reshape(B, S, GROUP_WIDTH), out_b.reshape(B, S, GROUP_WIDTH),
                               out_c.reshape(B, S, GROUP_WIDTH), out_d.reshape(B, S, GROUP_WIDTH)],
                              axis=-1).astype(x.dtype)
        y = _group_rmsnorm(mix, mix_g[l]) @ w_o[l]
        x = _layernorm(ALPHA * x + y, ln1_g[l], ln1_b[l])
        f = (jax.nn.silu(x @ w_gate[l]) * (x @ w_up[l])) @ w_down[l]
        x = _layernorm(ALPHA * x + f, ln2_g[l], ln2_b[l])
    return x
```

```python
import numpy as np
import concourse.bass as bass
import concourse.mybir as mybir
from concourse.bass_utils import run_bass_kernel_spmd
from contextlib import ExitStack

F32 = mybir.dt.float32
BF16 = mybir.dt.bfloat16
AF = mybir.ActivationFunctionType
ALU = mybir.AluOpType
AX = mybir.AxisListType

ENG = ("pe", "act", "dve", "pool", "sp")
NDMASEM = 8


class Op:
    __slots__ = ("eng", "fn", "deps", "is_dma", "signal", "cnt", "semi", "val", "waits", "gid", "is_cc")

    def __init__(self, eng, fn, is_dma):
        self.eng = eng
        self.fn = fn
        self.deps = []
        self.is_dma = is_dma
        self.signal = False
        self.cnt = 0
        self.semi = -1
        self.val = 0
        self.waits = []
        self.is_cc = False


class Prog:
    def __init__(self, nc, stack):
        self.nc = nc
        self.stack = stack
        self.all = []
        self.res = {}
        self.ntens = 0
        self._bar_from = 0

    def sb(self, shape, dt, name=None):
        self.ntens += 1
        return self.stack.enter_context(self.nc.sbuf_tensor("s_" + (name or f"sb{self.ntens}"), list(shape), dt))

    def psum(self, shape, dt, name=None):
        self.ntens += 1
        return self.stack.enter_context(self.nc.psum_tensor(name or f"ps{self.ntens}", list(shape), dt))

    def op(self, eng, fn, reads=(), writes=(), dma=False):
        o = Op(eng, fn, dma)
        deps = o.deps
        res = self.res
        for r in reads:
            st = res.get(r)
            if st is None:
                st = res[r] = [None, []]
            if st[0] is not None:
                deps.append(st[0])
        for w in writes:
            st = res.get(w)
            if st is None:
                st = res[w] = [None, []]
            if st[0] is not None:
                deps.append(st[0])
            deps.extend(st[1])
        for r in reads:
            res[r][1].append(o)
        for w in writes:
            st = res[w]
            st[0] = o
            st[1] = []
        self.all.append(o)
        return o

    def dma(self, eng, out, in_, reads=(), writes=(), **kw):
        return self.op(eng, lambda e: e.dma_start(out=out, in_=in_, **kw), reads, writes, dma=True)

    def cc(self, fn, reads=(), writes=()):
        o = self.op("pool", fn, reads, writes, dma=True)
        o.is_cc = True
        return o

    def wait_all(self, eng, ops):
        o = Op(eng, None, False)
        o.deps = list(ops)
        self.all.append(o)
        return o


    def barrier(self):
        deps = []
        last = {}
        for o in self.all[self._bar_from:]:
            if o.fn is None:
                continue
            if o.is_dma:
                deps.append(o)
            else:
                last[o.eng] = o
        deps.extend(last.values())
        self._bar_from = len(self.all)
        for e in ENG:
            w = Op(e, None, False)
            w.deps = list(deps)
            self.all.append(w)

    def finalize(self):
        nc = self.nc
        for o in self.all:
            nd = []
            seen = set()
            for d in o.deps:
                if id(d) in seen:
                    continue
                seen.add(id(d))
                if d is o:
                    continue
                if not d.is_dma and d.eng == "pe" and o.eng == "pe" and not o.is_dma:
                    continue
                nd.append(d)
                if not d.is_dma:
                    d.signal = True
            o.deps = nd
        cnt = {e: 0 for e in ENG}
        for o in self.all:
            if o.is_dma or o.fn is None:
                continue
            if o.signal:
                cnt[o.eng] += 1
            o.cnt = cnt[o.eng]
        known = {e: {} for e in ENG}
        pool_next = {e: 0 for e in ENG}
        pool_last = {e: [0] * NDMASEM for e in ENG}
        self.nwaits = 0
        ncc = 0
        for o in self.all:
            E = o.eng
            kn = known[E]
            need = {}
            if o.is_cc:
                o.semi = ("cc", ncc)
                o.val = 1
                ncc += 1
            elif o.is_dma:
                s = pool_next[E]
                pool_next[E] = (s + 1) % NDMASEM
                prev = pool_last[E][s]
                key = ("d", E, s)
                if prev > 0 and kn.get(key, 0) < prev:
                    need[key] = prev
                o.semi = (E, s)
                o.val = prev + 16
                pool_last[E][s] = o.val
            for d in o.deps:
                if d.is_dma:
                    key = ("d",) + d.semi
                    v = d.val
                else:
                    key = ("c", d.eng)
                    v = d.cnt
                if kn.get(key, 0) < v and need.get(key, 0) < v:
                    need[key] = v
            for key, v in need.items():
                kn[key] = v
            o.waits = list(need.items())
            self.nwaits += len(need)
        st = self.stack
        self.csem = {e: st.enter_context(nc.semaphore(f"c_{e}")) for e in ENG}
        self.dsem = {(e, s): st.enter_context(nc.semaphore(f"d_{e}{s}")) for e in ("sp", "pool", "act") for s in range(NDMASEM)}
        self.ccsem = [st.enter_context(nc.semaphore(f"cc{i}")) for i in range(ncc)]
        self.byeng = {e: [o for o in self.all if o.eng == e] for e in ENG}

    def _sem(self, key):
        if key[0] == "c":
            return self.csem[key[1]]
        if key[1] == "cc":
            return self.ccsem[key[2]]
        return self.dsem[(key[1], key[2])]

    def emit_engine(self, E, e):
        for o in self.byeng[E]:
            for key, v in o.waits:
                e.wait_ge(self._sem(key), v)
            if o.fn is None:
                continue
            ins = o.fn(e)
            if o.is_cc:
                ins.then_inc(self.ccsem[o.semi[1]])
            elif o.is_dma:
                ins.then_inc(self.dsem[o.semi], 16)
            elif o.signal:
                ins.then_inc(self.csem[E], 1)

    def emit(self):
        nc = self.nc
        self.finalize()
        with nc.Block() as block:
            @block.tensor
            def _(e):
                self.emit_engine("pe", e)

            @block.scalar
            def _(e):
                self.emit_engine("act", e)

            @block.vector
            def _(e):
                self.emit_engine("dve", e)

            @block.gpsimd
            def _(e):
                self.emit_engine("pool", e)

            @block.sync
            def _(e):
                self.emit_engine("sp", e)

D = 1024
T = 4096
NB = 32
NT = 8
TS = 512
DFF = 2816
NFC = 22
INW = 2468
NL = 4
ALPHA = float((2.0 * NL) ** 0.25)
GBASE = {"A": 0, "B": 772, "C": 1188, "D": 1956}
GW = {"A": 772, "B": 416, "C": 768, "D": 512}
MASKV = -30000.0
NS = 4
TQ = NS * 512


def nat_tile(p):
    return 2 * (p % 4) + p // 4


def pos_tile(nt):
    return (nt % 2) * 4 + nt // 2


def pb_nat(n):
    return pos_tile(n // 4) * 4 + n % 4
STAGE = 99
KVF = 0
KVT = 8
VENG = 'dve'
S2 = 99
AMODE = 0
P2T = 8


def make_consts(rank):
    c = {}
    c["identf"] = np.eye(128, dtype=np.float32)
    p = np.arange(128)[:, None]
    xx = np.arange(896)[None, :]
    ms = np.where((xx - 384) < p, MASKV, 0.0)
    mx = np.where((xx - 384) <= p, MASKV, 0.0)
    full = np.full((128, 896), MASKV)
    zero = np.zeros((128, 896))
    mb = np.zeros((2, 2, 128, 896), np.float32)
    for ti, st in enumerate((ms, mx)):
        if rank == 0:
            mb[ti, 0] = st
            mb[ti, 1] = full
        else:
            mb[ti, 0] = zero
            mb[ti, 1] = st
    c["mbase"] = mb
    sel = np.zeros((128, 2), np.float32)
    sel[:, rank] = 1.0
    c["sel"] = sel
    s = np.arange(128)[:, None]
    t = np.arange(128)[None, :]
    tri = np.zeros((4, 128, 128), np.float32)
    tri[0] = (s <= t)
    tri[1] = 1.0
    tri[2] = -(s > t).astype(np.float32)
    tri[3] = -(s <= t).astype(np.float32)
    c["tri"] = tri
    pos = np.arange(T, dtype=np.float32)
    inv = (np.float32(10000.0) ** (-np.arange(0, 32, 2, dtype=np.float32) / np.float32(32))).astype(np.float32)
    ang = (pos[:, None] * inv[None, :]).astype(np.float32)
    cs = np.cos(ang).astype(np.float32).T
    sn = np.sin(ang).astype(np.float32).T
    rope = np.zeros((2, 32, T), np.float32)
    rope[0, :16] = cs
    rope[0, 16:] = cs
    rope[1, :16] = -sn
    rope[1, 16:] = sn
    r8 = rope.reshape(2, 32, 8, 512)
    c["rope"] = np.ascontiguousarray(r8[:, :, [nat_tile(pp_) for pp_ in range(8)], :]).reshape(2, 32, T)
    c["ropeq"] = np.ascontiguousarray(r8[:, :, [2 * j_ + rank for j_ in range(4)], :]).reshape(2, 32, TQ)
    bd = np.zeros((4, 128, 256), np.float32)
    pp = np.arange(128)[:, None].astype(np.float32)
    cc = np.arange(128)[None, :].astype(np.float32)
    for h in range(4):
        m = np.float32(2.0 ** (-8.0 * (h + 1) / 4.0))
        d1 = 128.0 + cc - pp
        bd[h, :, 0:128] = np.where(cc < pp, -m * d1, MASKV)
        d2 = cc - pp
        bd[h, :, 128:256] = np.where(cc >= pp, -m * d2, MASKV)
    c["biasd"] = bd
    ind = np.zeros((128, 8, 4), np.float32)
    for ch in range(8):
        ind[:, ch, ch // 2] = 1.0
    c["ind"] = ind
    return c


WNAMES = [("w_in", [NL, D, INW]), ("fox_b_f", [NL, 4]), ("mla_g_q", [NL, 256]), ("mla_g_kv", [NL, 128]),
          ("mla_w_uq", [NL, 256, 384]), ("mla_w_ukv", [NL, 128, 512]), ("swa_sinks", [NL, 4]),
          ("mix_g", [NL, D]), ("w_o", [NL, D, D]), ("ln1_g", [NL, D]), ("ln1_b", [NL, D]),
          ("w_gate", [NL, D, DFF]), ("w_up", [NL, D, DFF]), ("w_down", [NL, DFF, D]),
          ("ln2_g", [NL, D]), ("ln2_b", [NL, D])]
CNAMES = [("identf", [128, 128]), ("mbase", [2, 2, 128, 896]), ("tri", [4, 128, 128]), ("rope", [2, 32, T]),
          ("ropeq", [2, 32, TQ]), ("sel", [128, 2]), ("biasd", [4, 128, 256]), ("ind", [128, 8, 4])]


class Arena:
    def __init__(self, P, nbytes, name):
        self.t = P.sb([128, nbytes // 2], BF16, name)
        self.n = nbytes
        self.off = 0

    def reset(self):
        self.off = 0

    def get(self, shape, dt):
        esz = 4 if dt == F32 else 2
        ne = int(np.prod(shape[1:]))
        n = (ne * esz + 63) // 64 * 64
        a = self.t[:, self.off // 2:(self.off + n) // 2]
        self.off += n
        assert self.off <= self.n, ("arena overflow", self.off, self.n)
        if dt == F32:
            a = a.bitcast(F32)
        a = a[:, 0:ne]
        if len(shape) == 3:
            a = a.rearrange("p (a b) -> p a b", a=shape[1])
        elif len(shape) == 4:
            a = a.rearrange("p (a b c) -> p a b c", a=shape[1], b=shape[2])
        return a


def build_program(n_layers=NL, groups="ABCD", debug=False, phase2=True, n_cores=8):
    nc = bass.Bass("TRN2", target_bir_lowering=False)
    dd = {}
    dd["x"] = nc.dram_tensor("x", [T, D], F32, kind="ExternalInput").ap()
    for nm, shp in WNAMES + CNAMES:
        dd[nm] = nc.dram_tensor(nm, list(shp), F32, kind="ExternalInput").ap()
    dd["xq"] = nc.dram_tensor("xq", [TQ, D], F32, kind="ExternalInput").ap()
    out_d = nc.dram_tensor("out", [TQ, D], F32, kind="ExternalOutput").ap()
    xs_d = nc.dram_tensor("xs", [TQ, D], F32, kind="Internal").ap()
    mixT_d = nc.dram_tensor("mixT", [D, TQ], BF16, kind="ExternalOutput" if debug else "Internal").ap()
    xtm_f = [nc.dram_tensor(f"xtm{j_}", [D, TS // 2], F32, kind="Internal").ap() for j_ in range(NS)]
    xta_f = [nc.dram_tensor(f"xta{j_}", [2 * D, TS // 2], F32, kind="Internal").ap() for j_ in range(NS)]
    rgroups = [[2 * g_, 2 * g_ + 1] for g_ in range(n_cores // 2)]

    with ExitStack() as st:
        P = Prog(nc, st)

        def mm(out, lhsT, rhs, start, stop, r, w, **kw):
            P.op("pe", lambda e: e.matmul(out, lhsT=lhsT, rhs=rhs, start=start, stop=stop, **kw), r, w)

        def tr(out, in_, r, w):
            P.op("pe", lambda e: e.transpose(out=out, in_=in_, identity=identf), list(r) + ["const"], w)

        def act(out, in_, func, r, w, scale=None, bias=None):
            kw = {}
            if scale is not None:
                kw["scale"] = scale
            if bias is not None:
                kw["bias"] = bias
            P.op("act", lambda e: e.activation(out=out, in_=in_, func=func, **kw), r, w)

        def tt(eng, out, in0, in1, op, r, w):
            P.op(eng, lambda e: e.tensor_tensor(out=out, in0=in0, in1=in1, op=op), r, w)

        def ts(eng, out, in0, s1, op0, r, w, s2=None, op1=None):
            if op1 is None:
                P.op(eng, lambda e: e.tensor_scalar(out=out, in0=in0, scalar1=s1, scalar2=None, op0=op0), r, w)
            else:
                P.op(eng, lambda e: e.tensor_scalar(out=out, in0=in0, scalar1=s1, scalar2=s2, op0=op0, op1=op1), r, w)

        def stt(out, in0, scalar, in1, op0, op1, r, w):
            P.op("dve", lambda e: e.scalar_tensor_tensor(out=out, in0=in0, scalar=scalar, in1=in1, op0=op0, op1=op1), r, w)

        def cp(eng, out, in_, r, w):
            if eng == "act":
                P.op("act", lambda e: e.activation(out=out, in_=in_, func=AF.Copy), r, w)
            else:
                P.op(eng, lambda e: e.tensor_copy(out=out, in_=in_), r, w)

        def rsqrt(out, in_, eps, r, w):
            act(out, in_, AF.Ln, r, w, bias=eps)
            act(out, out, AF.Exp, list(w), w, scale=-0.5)

        def recip(out, in_, r, w):
            P.op("dve", lambda e: e.reciprocal(out=out, in_=in_), r, w)

        def dma(eng, out, in_, r, w, **kw):
            return P.dma(eng, out, in_, r, w, **kw)

        ps = [P.psum([128, 512], F32, f"bank{i}")[:] for i in range(8)]
        rot = {}

        def bank(pool, lst):
            i = rot.get(pool, 0)
            rot[pool] = i + 1
            b = lst[i % len(lst)]
            return b, ("ps", b)

        XT = P.sb([128, 8, T], BF16, "XT")[:]
        AR1 = Arena(P, 49152, "AR1")
        AR2 = Arena(P, 80 * 1024, "AR2")
        identf = P.sb([128, 128], F32, "identf")[:]
        identb = P.sb([128, 128], BF16, "identb")[:]
        onesb = P.sb([128, 128], BF16, "onesb")[:]
        MB = P.sb([128, 4, 896], BF16, "MB")[:]
        TRI = P.sb([128, 4, 128], F32, "TRI")[:]
        BIASD = P.sb([128, 4, 256], F32, "BIASD")[:]
        IND = P.sb([128, 8, 4], BF16, "IND")[:]
        SMALL = P.sb([128, 64], F32, "SMALL")[:]
        BF_ = SMALL[:, 0:4]
        ES = SMALL[:, 4:8]
        GQ = SMALL[:, 8:10]
        GKV = SMALL[:, 10:11]
        MG = SMALL[:, 16:24]
        SEL = SMALL[:, 24:26]
        CARRYX = P.sb([128, NB, 4], F32, "CARRYX")[:]

        dma("sp", identf, dd["identf"], [], ["const"])
        dma("pool", identb, dd["identf"], [], ["const"])
        dma("pool", onesb, dd["tri"][1], [], ["const"])
        dma("pool", MB, dd["mbase"].rearrange("t w p c -> p (t w) c"), [], ["const"])
        dma("sp", SEL, dd["sel"], [], ["const"])
        dma("sp", TRI, dd["tri"].rearrange("i p c -> p i c"), [], ["const"])
        dma("sp", BIASD, dd["biasd"].rearrange("h p c -> p h c"), [], ["const"])
        dma("pool", IND, dd["ind"], [], ["const"])
        P.op("dve", lambda e: e.memset(CARRYX[:, 0, :], 0.0), [], ["carry0"])

        all_dma_out = []

        def build_xt0():
            AR2.reset()
            XB = [AR2.get([128, D], F32) for _ in range(2)]
            for blk in range(NB):
                xb = XB[blk % 2]
                k = ("XB0", blk % 2)
                dma("sp", xb, dd["x"][blk * 128:(blk + 1) * 128, :], [], [k])
                for half in range(2):
                    b, bk = bank("M", list(range(8)))
                    for j in range(4):
                        c = half * 4 + j
                        tr(ps[b][:, j * 128:(j + 1) * 128], xb[:, c * 128:(c + 1) * 128], [k], [bk])
                    cp("act" if half else "dve", XT[:, half * 4:half * 4 + 4, blk * 128:(blk + 1) * 128],
                       ps[b].rearrange("p (a b) -> p a b", a=4), [bk], [("XT", blk // 4)])

        def phase1(l):
            AR1.reset()
            AR2.reset()
            KT = [AR1.get([128, T], BF16) for _ in range(4)]
            V = AR1.get([128, NB, 256], BF16)
            QT = [[AR2.get([128, TS], BF16) for _ in range(2)] for _ in range(4)]
            WIN = AR2.get([128, 8, 772], BF16)
            PT = [AR2.get([128, TS], BF16) for _ in range(3)]
            CT = [[AR2.get([128, TS], F32) for _ in range(2)] for _ in range(2)]
            RC = AR2.get([128, TS], F32)
            MO = [AR2.get([128, TS], BF16) for _ in range(2)]
            GATE = AR2.get([128, NB, 4], F32)
            TOTs = AR2.get([128, NB, 4], F32)
            CNEG = AR2.get([128, NB, 4], F32)
            BT = [AR2.get([128, NB, 4], F32) for _ in range(2)]
            WUQ = AR2.get([128, 2, 384], BF16)
            WUQS = AR2.get([128, 2, 384], BF16)
            WUKV = AR2.get([128, 512], BF16)
            WKRS = AR2.get([128, 8, 96], BF16)
            SQ = AR2.get([128, 2, TS], BF16)
            RSB = AR2.get([128, TS], F32)
            CQN = AR2.get([128, 2, TS], BF16)
            CKVN = AR2.get([128, TS], BF16)
            ROPE = AR2.get([128, 2, TS], F32)
            RT1 = AR2.get([128, TS], F32)
            RT2 = AR2.get([128, TS], F32)
            SBD = [AR2.get([128, 256], F32) for _ in range(2)]
            PD = [AR2.get([128, 256], BF16) for _ in range(2)]
            XTQ = AR2.get([128, 8, TS], BF16)
            MO2 = AR2.get([128, TS], BF16)

            def make_xtq(j):
                ts("dve", XTQ, XT[:, :, j * TS:(j + 1) * TS], SEL[:, 0:1], ALU.mult, [("XT", j), "const"], ["XTQ"])
                stt(XTQ, XT[:, :, (4 + j) * TS:(5 + j) * TS], SEL[:, 1:2], XTQ, ALU.mult, ALU.add,
                    [("XT", 4 + j), "const", "XTQ"], ["XTQ"])

            w_in_l = dd["w_in"][l].rearrange("(k p) n -> p k n", p=128)
            ALLB = list(range(8))
            mo_cnt = [0]

            def load_win(g):
                dma("pool", WIN[:, :, 0:GW[g]], w_in_l[:, :, GBASE[g]:GBASE[g] + GW[g]], [], ["WIN"])

            def xt_keys(tile):
                return [("XT", tile)]

            def proj_fm(dst, dk, col0, tile, eng, wkeys, dkeys, pool="M", banks=None, lw=None, own=False):
                b, bk = bank(pool, banks or [7])
                W = WIN if lw is None else lw
                for k in range(8):
                    if own:
                        mm(ps[b][0:dk, :], W[:, k, col0:col0 + dk], XTQ[:, k, :], k == 0, k == 7, ["XTQ"] + wkeys, [bk])
                    else:
                        mm(ps[b][0:dk, :], W[:, k, col0:col0 + dk], XT[:, k, tile * TS:(tile + 1) * TS], k == 0, k == 7,
                           xt_keys(tile) + wkeys, [bk])
                if dst is not None:
                    cp(eng, dst, ps[b][0:dk, :], [bk], dkeys)
                return b, bk

            def store_mix(g, h, qt, src_bank, bk, rc_ap, extra_r):
                i = mo_cnt[0] % 2
                mo_cnt[0] += 1
                mo = MO[i]
                mk = ("MO", i)
                if rc_ap is None:
                    cp("dve", mo[0:64, :], ps[src_bank][0:64, :], [bk], [mk])
                else:
                    tt("dve", mo[0:64, :], ps[src_bank][0:64, :], rc_ap, ALU.mult, [bk] + extra_r, [mk])
                row = "ABCD".index(g) * 256 + h * 64
                o = dma("sp", mixT_d[row:row + 64, qt * TS:(qt + 1) * TS], mo[0:64, :], [mk], [("mixT", g, h, qt)])

            def softmax_unit(g, h, qt, dk, scale, qap, qkeys, bias_fn, bias_keys, u):
                kbl = [4 * p_ + i_ for p_ in list(range(qt + 1)) + list(range(4, 4 + qt + 1)) for i_ in range(4)]
                nkb = len(kbl)
                ob = [3, 4][u % 2]
                db = [5, 6][u % 2]
                obk, dbk = ("ps", ob), ("ps", db)
                LA = 2
                sb_of = {}
                for step in range(nkb + LA):
                    if step < nkb:
                        kb = kbl[step]
                        b, bk = bank("S", [0, 1, 2])
                        sb_of[step] = (b, bk)
                        p_, i = kb // 4, kb % 4
                        wh = 0 if p_ == qt else (1 if p_ == 4 + qt else -1)
                        mm(ps[b], KT[h][0:dk, kb * 128:(kb + 1) * 128], qap, True, wh < 0,
                           [("KT", h, kb // 4)] + qkeys, [bk])
                        if wh >= 0:
                            mm(ps[b], identb, MB[:, wh, 384 - 128 * i:384 - 128 * i + 512], False, True, ["const"], [bk])
                    s2 = step - LA
                    if 0 <= s2 < nkb:
                        kb = kbl[s2]
                        b, bk = sb_of.pop(s2)
                        pi = s2 % 3
                        pt, pk = PT[pi], ("PT", pi)
                        act(pt, ps[b], AF.Exp, [bk] + bias_keys, [pk], scale=scale,
                            bias=(bias_fn(kb) if bias_fn else None))
                        mm(ps[ob][0:64, :], V[:, kb, h * 64:(h + 1) * 64], pt, s2 == 0, s2 == nkb - 1,
                           [pk, ("V", kb // 4)], [obk])
                        mm(ps[db][0:64, :], onesb[:, 0:64], pt, s2 == 0, s2 == nkb - 1, [pk, "const"], [dbk])
                recip(RC[0:64, :], ps[db][0:64, :], [dbk], ["RC"])
                store_mix(g, h, qt, ob, obk, RC[0:64, :], ["RC"])

            def sb_units(hs, qt, qaps, qkeys):
                order = [pos_tile(nt_) * 4 + i_ for nt_ in reversed(range(2 * qt + 2)) for i_ in reversed(range(4))]
                nkb = len(order)
                scale = 0.125
                zb = [[0, 1], [4, 5]]
                accb = [2, 6]
                obb = [3, 7]
                LA = 1
                z_of = {}
                for step in range(nkb + LA):
                    for si, h in enumerate(hs):
                        if step < nkb:
                            kb = order[step]
                            b = zb[si][step % 2]
                            bk = ("ps", b)
                            z_of[(si, step)] = (b, bk)
                            p_, i = kb // 4, kb % 4
                            wh = 0 if p_ == qt else (1 if p_ == 4 + qt else -1)
                            mm(ps[b], KT[h][0:64, kb * 128:(kb + 1) * 128], qaps[si], True, wh < 0,
                               [("KT", h, kb // 4)] + qkeys[si], [bk])
                            if wh >= 0:
                                mm(ps[b], identb, MB[:, 2 + wh, 384 - 128 * i:384 - 128 * i + 512], False, True,
                                   ["const"], [bk])
                    for si, h in enumerate(hs):
                        s2 = step - LA
                        if 0 <= s2 < nkb:
                            kb = order[s2]
                            b, bk = z_of.pop((si, s2))
                            SP, T1 = CT[si]
                            spk, t1k = ("SP", si), ("T1", si)
                            ab, abk = accb[si], ("ps", accb[si])
                            ob, obk = obb[si], ("ps", obb[si])
                            act(SP, ps[b], AF.Exp, [bk], [spk], scale=scale)
                            act(SP, SP, AF.Ln, [spk], [spk], bias=1.0)
                            mm(ps[ab], TRI[:, 2, :], SP, s2 == 0, True, [spk, "const"], [abk], skip_group_check=True)
                            stt(T1, ps[b], scale, SP, ALU.mult, ALU.subtract, [bk, spk], [t1k])
                            tt("dve", T1, ps[ab], T1, ALU.add, [abk, t1k], [t1k])
                            if s2 < nkb - 1:
                                mm(ps[ab], TRI[:, 3, :], SP, False, True, [spk, "const"], [abk], skip_group_check=True)
                            pi = (s2 * 2 + si) % 3
                            pt, pk = PT[pi], ("PT", pi)
                            act(pt, T1, AF.Exp, [t1k], [pk])
                            mm(ps[ob][0:64, :], V[:, kb, h * 64:(h + 1) * 64], pt, s2 == 0, s2 == nkb - 1,
                               [pk, ("V", kb // 4)], [obk])
                for si, h in enumerate(hs):
                    store_mix("C", h, qt, obb[si], ("ps", obb[si]), None, [])

            def kv_pass(g, kcols, vcol0, nv, gates):
                KVB = list(range(7)) if gates else ALLB
                for tile in range(NT):
                    for h, c0 in enumerate(kcols):
                        proj_fm(KT[h][0:64, tile * TS:(tile + 1) * TS], 64, c0, tile, "dve", ["WIN"],
                                [("KT", h, tile)], banks=KVB)
                    for j in range(4):
                        blk = tile * 4 + j
                        b, bk = bank("M", KVB)
                        n = nv
                        for k in range(8):
                            mm(ps[b][:, 0:n], XT[:, k, blk * 128:(blk + 1) * 128], WIN[:, k, vcol0:vcol0 + n],
                               k == 0, k == 7, xt_keys(tile) + ["WIN"], [bk])
                        cp("dve", V[:, blk, 0:nv], ps[b][:, 0:nv], [bk], [("V", tile)])
                        if gates:
                            for k in range(8):
                                mm(ps[7][:, blk * 4:(blk + 1) * 4], XT[:, k, blk * 128:(blk + 1) * 128],
                                   WIN[:, k, vcol0 + nv:vcol0 + nv + 4], k == 0, k == 7, xt_keys(tile) + ["WIN"],
                                   [("ps", 7)], skip_group_check=True)
                if gates:
                    cp("dve", GATE.rearrange("p a b -> p (a b)"), ps[7][:, 0:128], [("ps", 7)], ["GATE"])

            if "A" in groups:
                load_win("A")
                dma("sp", BF_, dd["fox_b_f"][l].partition_broadcast(128), [], ["BF"])
                kv_pass("A", [256, 320, 384, 448], 512, 256, AMODE != 1)
                if AMODE in (0, 3):
                    for h in range(4):
                        ts("dve", GATE[:, :, h], GATE[:, :, h], BF_[:, h:h + 1], ALU.add, ["GATE", "BF"], ["GATE"])
                    GF = GATE.rearrange("p a b -> p (a b)")
                    act(GF, GF, AF.Exp, ["GATE"], ["GATE"], scale=-1.0)
                    act(GF, GF, AF.Ln, ["GATE"], ["GATE"], bias=1.0)
                    if False:
                        pass
                    b1, bk1 = bank("M", ALLB)
                    mm(ps[b1][:, 0:128], TRI[:, 1, :], GF, True, True, ["GATE", "const"], [bk1])
                    b2, bk2 = bank("M", ALLB)
                    mm(ps[b2][:, 0:128], TRI[:, 0, :], GF, True, True, ["GATE", "const"], [bk2])
                    cp("dve", TOTs.rearrange("p a b -> p (a b)"), ps[b1][:, 0:128], [bk1], ["TOTs"])
                    if False:
                        pass
                    for n_ in range(NB - 1):
                        tt("dve", CARRYX[:, pb_nat(n_ + 1), :], CARRYX[:, pb_nat(n_), :], TOTs[:, pb_nat(n_), :], ALU.add,
                           ["TOTs", "carry0", "CARRYX"], ["CARRYX"])
                    tt("dve", CNEG, ps[b2][:, 0:128].rearrange("p (a b) -> p a b", a=NB), CARRYX, ALU.add,
                       [bk2, "CARRYX", "carry0"], ["CNEG"])
                    if False:
                        pass
                    u = 0
                    for qt in range(NS):
                        bt = BT[qt % 2]
                        btk = ("BT", qt % 2)
                        for h in range(4):
                            ts("dve", bt[:, :, h], CNEG[:, :, h], CARRYX[:, (4 + qt) * 4, h:h + 1], ALU.subtract,
                               ["CNEG", "CARRYX", "carry0"], [btk])
                        make_xtq(qt)
                        for h in range(4):
                            qb = u % 2
                            qk = ("QT", h, qb)
                            proj_fm(QT[h][qb][0:64, :], 64, h * 64, qt, "dve", ["WIN"], [qk], own=True)
                            softmax_unit("A", h, qt, 64, 0.125, QT[h][qb][0:64, :], [qk],
                                         (lambda kb, hh=h, bb=bt: bb[:, kb, hh:hh + 1]), [btk], u)
                            u += 1

            if "B" in groups:
                P.barrier()
                load_win("B")
                wb = GBASE["B"]
                uq = dd["mla_w_uq"][l].rearrange("(c p) n -> p c n", p=128)
                dma("pool", WUQ, uq, [], ["WUQ"])
                uq4 = uq.rearrange("p c (h x) -> p c h x", h=4)
                wq4 = WUQS.rearrange("p c (h x) -> p c h x", h=4)
                for c in range(2):
                    dma("pool", wq4[:, c, :, 0:64], uq4[:, c, :, 0:64], [], ["WUQS"])
                    dma("pool", wq4[:, c, :, 64:80], uq4[:, c, :, 80:96], [], ["WUQS"])
                    dma("pool", wq4[:, c, :, 80:96], uq4[:, c, :, 64:80], [], ["WUQS"])
                dma("pool", WUKV, dd["mla_w_ukv"][l], [], ["WUKV"])
                dma("pool", WKRS[:, :, 0:64], w_in_l[:, :, wb + 320:wb + 384], [], ["WKRS"])
                dma("pool", WKRS[:, :, 64:80], w_in_l[:, :, wb + 400:wb + 416], [], ["WKRS"])
                dma("pool", WKRS[:, :, 80:96], w_in_l[:, :, wb + 384:wb + 400], [], ["WKRS"])
                dma("sp", GQ, dd["mla_g_q"][l].rearrange("(c p) -> p c", p=128), [], ["GQ"], allow_slow_non_contiguous=True)
                dma("sp", GKV, dd["mla_g_kv"][l].rearrange("(c p) -> p c", p=128), [], ["GKV"], allow_slow_non_contiguous=True)
                ts("dve", GQ, GQ, 16.0, ALU.mult, ["GQ"], ["GQ"])
                ts("dve", GKV, GKV, float(np.sqrt(128.0)), ALU.mult, ["GKV"], ["GKV"])
                WUKV4 = WUKV.rearrange("p (h x) -> p h x", h=4)

                def load_rope(tile, own=False):
                    src_ = dd["ropeq"] if own else dd["rope"]
                    dma("sp", ROPE[64:96, 0, :], src_[0, :, tile * TS:(tile + 1) * TS], [], ["ROPE"])
                    dma("sp", ROPE[64:96, 1, :], src_[1, :, tile * TS:(tile + 1) * TS], [], ["ROPE"])

                def rope_apply(ba, bka, bb, bkb, dsts, dkeys):
                    tt("dve", RT1[64:96, :], ps[ba][64:96, :], ROPE[64:96, 0, :], ALU.mult, [bka, "ROPE"], ["RT1"])
                    tt("dve", RT2[64:96, :], ps[bb][64:96, :], ROPE[64:96, 1, :], ALU.mult, [bkb, "ROPE"], ["RT2"])
                    for dst, dk_ in zip(dsts, dkeys):
                        tt("dve", dst, RT1[64:96, :], RT2[64:96, :], ALU.add, ["RT1", "RT2"], [dk_])

                for tile in range(NT):
                    ba, bka = proj_fm(None, 128, 256, tile, None, ["WIN"], None, banks=ALLB)
                    act(SQ[:, 0, :], ps[ba], AF.Square, [bka], ["SQ"])
                    bs, bks = bank("M", ALLB)
                    mm(ps[bs], onesb, SQ[:, 0, :], True, True, ["SQ", "const"], [bks])
                    rsqrt(RSB, ps[bs], 128.0 * 1e-6, [bks], ["RSB"])
                    stt(CKVN, ps[ba], GKV[:, 0:1], RSB, ALU.mult, ALU.mult, [bka, "RSB", "GKV"], ["CKVN"])
                    for h in range(4):
                        b, bk = bank("M", ALLB)
                        mm(ps[b][0:64, :], WUKV[:, h * 128:h * 128 + 64], CKVN, True, True, ["CKVN", "WUKV"], [bk])
                        cp("dve", KT[h][0:64, tile * TS:(tile + 1) * TS], ps[b][0:64, :], [bk], [("KT", h, tile)])
                    for j in range(4):
                        blk = tile * 4 + j
                        b, bk = bank("M", ALLB)
                        mm(ps[b][:, 0:256].rearrange("p (h x) -> p h x", h=4), CKVN[:, j * 128:(j + 1) * 128],
                           WUKV4[:, :, 64:128], True, True, ["CKVN", "WUKV"], [bk])
                        cp("dve", V[:, blk, :], ps[b][:, 0:256], [bk], [("V", tile)])
                    bc, bkc = proj_fm(None, 96, 320, tile, None, ["WIN"], None, banks=ALLB)
                    bd_, bkd = proj_fm(None, 96, 0, tile, None, ["WKRS"], None, banks=ALLB, lw=WKRS)
                    load_rope(tile)
                    rope_apply(bc, bkc, bd_, bkd, [KT[h][64:96, tile * TS:(tile + 1) * TS] for h in range(4)],
                               [("KT", h, tile) for h in range(4)])
                u = 0
                sc_b = float(96.0 ** -0.5)
                for qt in range(NS):
                    make_xtq(qt)
                    bq = []
                    for c in range(2):
                        bq.append(proj_fm(None, 128, c * 128, qt, None, ["WIN"], None, banks=[7, 0, 1, 2], own=True))
                        act(SQ[:, c, :], ps[bq[c][0]], AF.Square, [bq[c][1]], ["SQ"])
                    bs, bks = bank("M", [7, 0, 1, 2])
                    for c in range(2):
                        mm(ps[bs], onesb, SQ[:, c, :], c == 0, c == 1, ["SQ", "const"], [bks])
                    rsqrt(RSB, ps[bs], 256.0 * 1e-6, [bks], ["RSB"])
                    for c in range(2):
                        stt(CQN[:, c, :], ps[bq[c][0]], GQ[:, c:c + 1], RSB, ALU.mult, ALU.mult,
                            [bq[c][1], "RSB", "GQ"], ["CQN"])
                    load_rope(qt, own=True)
                    for h in range(4):
                        qb = u % 2
                        qk = ("QT", h, qb)
                        be, bke = bank("M", [7, 0, 1, 2])
                        for c in range(2):
                            mm(ps[be][0:96, :], WUQ[:, c, h * 96:(h + 1) * 96], CQN[:, c, :], c == 0, c == 1,
                               ["CQN", "WUQ"], [bke])
                        bf_, bkf = bank("M", [7, 0, 1, 2])
                        for c in range(2):
                            mm(ps[bf_][0:96, :], WUQS[:, c, h * 96:(h + 1) * 96], CQN[:, c, :], c == 0, c == 1,
                               ["CQN", "WUQS"], [bkf])
                        cp("dve", QT[h][qb][0:64, :], ps[be][0:64, :], [bke], [qk])
                        rope_apply(be, bke, bf_, bkf, [QT[h][qb][64:96, :]], [qk])
                        softmax_unit("B", h, qt, 96, sc_b, QT[h][qb][0:96, :], [qk], None, [], u)
                        u += 1

            if "C" in groups:
                P.barrier()
                load_win("C")
                kv_pass("C", [256, 320, 384, 448], 512, 256, False)
                for qt in range(NS):
                    make_xtq(qt)
                    for pair in range(2):
                        hs = [2 * pair, 2 * pair + 1]
                        qaps, qkeys = [], []
                        for h in hs:
                            qb = qt % 2
                            qk = ("QT", h, qb)
                            proj_fm(QT[h][qb][0:64, :], 64, h * 64, qt, "dve", ["WIN"], [qk], banks=[0, 1, 4, 5], own=True)
                            qaps.append(QT[h][qb][0:64, :])
                            qkeys.append([qk])
                        sb_units(hs, qt, qaps, qkeys)

            if "D" in groups:
                P.barrier()
                load_win("D")
                dma("sp", ES, dd["swa_sinks"][l].partition_broadcast(128), [], ["ES"])
                act(ES, ES, AF.Exp, ["ES"], ["ES"])
                kv_pass("D", [256, 320], 384, 128, False)
                u = 0
                for js in range(NS):
                    for h in range(4):
                        kvh = h // 2
                        for half in range(2):
                            nt = 2 * js + half
                            pp = pos_tile(nt)
                            qb = u % 2
                            qk = ("QT", h, qb)
                            proj_fm(QT[h][qb][0:64, :], 64, h * 64, pp, "dve", ["WIN"], [qk])
                            ob = [3, 4][u % 2]
                            db = [5, 6][u % 2]
                            obk, dbk = ("ps", ob), ("ps", db)
                            for j in range(4):
                                n = 4 * nt + j
                                pn = pb_nat(n)
                                pm = pb_nat(n - 1) if n > 0 else 0
                                lo = 0 if n > 0 else 128
                                b, bk = bank("S", [0, 1, 2])
                                qa = QT[h][qb][0:64, j * 128:(j + 1) * 128]
                                if n > 0:
                                    mm(ps[b][:, 0:128], KT[kvh][0:64, pm * 128:(pm + 1) * 128], qa, True, True,
                                       [("KT", kvh, pm // 4), qk], [bk])
                                mm(ps[b][:, 128:256], KT[kvh][0:64, pn * 128:(pn + 1) * 128], qa, True, True,
                                   [("KT", kvh, pn // 4), qk], [bk])
                                si = (u * 4 + j) % 2
                                sbd, pd = SBD[si], PD[si]
                                stt(sbd[:, lo:256], ps[b][:, lo:256], 0.125, BIASD[:, h, lo:256], ALU.mult, ALU.add,
                                    [bk, "const"], [("SBD", si)])
                                act(pd[:, lo:256], sbd[:, lo:256], AF.Exp, [("SBD", si)], [("PD", si)])
                                oc = slice(j * 128, (j + 1) * 128)
                                if n > 0:
                                    mm(ps[ob][0:64, oc], V[:, pm, kvh * 64:(kvh + 1) * 64], pd[:, 0:128], True, False,
                                       [("PD", si), ("V", pm // 4)], [obk])
                                    mm(ps[db][0:64, oc], onesb[:, 0:64], pd[:, 0:128], True, False, [("PD", si), "const"], [dbk])
                                mm(ps[ob][0:64, oc], V[:, pn, kvh * 64:(kvh + 1) * 64], pd[:, 128:256], n == 0, True,
                                   [("PD", si), ("V", pn // 4)], [obk])
                                mm(ps[db][0:64, oc], onesb[:, 0:64], pd[:, 128:256], n == 0, True, [("PD", si), "const"], [dbk])
                            ts("dve", RC[0:64, :], ps[db][0:64, :], ES[0:64, h:h + 1], ALU.add, [dbk, "ES"], ["RC"])
                            recip(RC[0:64, :], RC[0:64, :], ["RC"], ["RC"])
                            tt("dve", MO[half][0:64, :], ps[ob][0:64, :], RC[0:64, :], ALU.mult, [obk, "RC"], [("MO", half)])
                            u += 1
                        ts("dve", MO2[0:64, :], MO[0][0:64, :], SEL[0:64, 0:1], ALU.mult, [("MO", 0), "const"], ["MO2"])
                        stt(MO2[0:64, :], MO[1][0:64, :], SEL[0:64, 1:2], MO2[0:64, :], ALU.mult, ALU.add,
                            [("MO", 1), "const", "MO2"], ["MO2"])
                        row = 3 * 256 + h * 64
                        dma("sp", mixT_d[row:row + 64, js * TS:(js + 1) * TS], MO2[0:64, :], ["MO2"], [("mixT", "D", h, js)])

        def phase2_(l, last):
            AR1.reset()
            AR2.reset()
            AT = AR1.get([128, NFC, TS], BF16)
            MX = AR1.get([128, 8, TS], BF16)
            X1 = AR1.get([128, 4, D], F32)
            WOS = AR2.get([128, 8, D], BF16)
            WST = AR2.get([128, D], F32)
            LNV = [AR2.get([128, D], F32) for _ in range(2)]
            WG = [AR2.get([128, 8, 256], BF16) for _ in range(2)]
            WU = [AR2.get([128, 8, 256], BF16) for _ in range(2)]
            WD = [AR2.get([128, 2, TS], BF16) for _ in range(2)]
            X1T = AR2.get([128, 8, TS], BF16)
            SQ2 = [AR2.get([128, TS], BF16) for _ in range(2)]
            ACCY = AR2.get([128, D], F32)
            U = AR2.get([128, D], F32)
            XB = AR2.get([128, D], F32)
            SG = [AR2.get([128, TS], F32) for _ in range(2)]
            RS = AR2.get([128, 4, 4], F32)
            STAT = AR2.get([128, 2, 6], F32)
            MV = AR2.get([128, 4], F32)

            dma("sp", MG, dd["mix_g"][l].rearrange("(c p) -> p c", p=128), [], ["MG"], allow_slow_non_contiguous=True)
            wo_l = dd["w_o"][l].rearrange("(c p) n -> p c n", p=128)
            for c in range(8):
                dma("sp", WST, wo_l[:, c, :], [], ["WST"])
                ts("dve", WOS[:, c, :], WST, MG[:, c:c + 1], ALU.mult, ["WST", "MG"], ["WOS"])
            wg_l = dd["w_gate"][l].rearrange("(k p) n -> p k n", p=128)
            wu_l = dd["w_up"][l].rearrange("(k p) n -> p k n", p=128)
            wd_l = dd["w_down"][l].rearrange("(c p) n -> p c n", p=128)
            mixT_v = mixT_d.rearrange("(c p) t -> p c t", p=128)
            src_x = dd["xq"] if l == 0 else xs_d
            dst_x = out_d if last else xs_d
            lnvi = [0]

            def load_vec(name):
                i = lnvi[0] % 2
                lnvi[0] += 1
                dma("sp", LNV[i], dd[name][l].partition_broadcast(128), [], [("LNV", i)])
                return LNV[i], ("LNV", i)

            def layernorm(src, srck, dst, dstk, gk, bk_):
                g_ap, gkey = gk
                b_ap, bkey = bk_
                for hf in range(2):
                    P.op("dve", (lambda hh: (lambda e: e.bn_stats(out=STAT[:, hh, :], in_=src[:, hh * 512:(hh + 1) * 512])))(hf),
                         [srck], ["STAT"])
                P.op("dve", lambda e: e.bn_aggr(out=MV[:, 0:2], in_=STAT.rearrange("p a b -> p (a b)")), ["STAT"], ["MV"])
                rsqrt(MV[:, 2:3], MV[:, 1:2], 1e-5, ["MV"], ["MV"])
                ts("dve", dst, src, MV[:, 0:1], ALU.subtract, [srck, "MV"], [dstk])
                ts("dve", dst, dst, MV[:, 2:3], ALU.mult, [dstk, "MV"], [dstk])
                tt("pool", dst, dst, g_ap, ALU.mult, [dstk, gkey], [dstk])
                tt("pool", dst, dst, b_ap, ALU.add, [dstk, bkey], [dstk])

            def transpose_to(src, srck, dstT, col0, dstk):
                for half in range(2):
                    b, bk = bank("M2", [4, 5, 6, 7])
                    for j in range(4):
                        c = half * 4 + j
                        tr(ps[b][:, j * 128:(j + 1) * 128], src[:, c * 128:(c + 1) * 128], [srck], [bk])
                    cp("act", dstT[:, half * 4:half * 4 + 4, col0:col0 + 128],
                       ps[b].rearrange("p (a b) -> p a b", a=4), [bk], [dstk])

            if S2 <= 1:
                return
            for tile in range(NS):
                mxk = "MX"
                dma("sp", MX, mixT_v[:, :, tile * TS:(tile + 1) * TS],
                    [("mixT", g, h, tile) for g in "ABCD" for h in range(4)], [mxk])
                g1 = load_vec("ln1_g")
                b1 = load_vec("ln1_b")
                for c in range(8):
                    sq = SQ2[c % 2]
                    sqk = ("SQ2", c % 2)
                    tt("pool", sq, MX[:, c, :], MX[:, c, :], ALU.mult, [mxk], [sqk])
                    for j in range(4):
                        mm(ps[4 + j][:, 0:4], sq[:, j * 128:(j + 1) * 128], IND[:, c, :], c == 0, c == 7,
                           [sqk, "const"], [("ps", 4 + j)])
                for j in range(4):
                    ts("dve", RS[:, j, :], ps[4 + j][:, 0:4], 1.0 / 256.0, ALU.mult, [("ps", 4 + j)], ["RS"],
                       s2=1e-6, op1=ALU.add)
                rsqrt(RS.rearrange("p a b -> p (a b)"), RS.rearrange("p a b -> p (a b)"), 0.0, ["RS"], ["RS"])
                if S2 <= 2:
                    continue
                for j in range(4):
                    blk = tile * 4 + j
                    dma("sp", XB, src_x[blk * 128:(blk + 1) * 128, :], [("xs", blk)], ["XB"])
                    for half in range(2):
                        ybs = []
                        for g in range(4):
                            b, bk = bank("Y", [0, 1, 2, 3])
                            ybs.append((b, bk))
                            for c2 in range(2):
                                c = 2 * g + c2
                                mm(ps[b], MX[:, c, j * 128:(j + 1) * 128], WOS[:, c, half * 512:(half + 1) * 512],
                                   c2 == 0, c2 == 1, [mxk, "WOS"], [bk])
                        acc = ACCY[:, half * 512:(half + 1) * 512]
                        ts("dve", acc, ps[ybs[0][0]], RS[:, j, 0:1], ALU.mult, [ybs[0][1], "RS"], ["ACCY"])
                        for g in range(1, 4):
                            stt(acc, ps[ybs[g][0]], RS[:, j, g:g + 1], acc, ALU.mult, ALU.add, [ybs[g][1], "RS", "ACCY"], ["ACCY"])
                    stt(U, XB, ALPHA, ACCY, ALU.mult, ALU.add, ["XB", "ACCY"], ["U"])
                    layernorm(U, "U", X1[:, j, :], ("X1", j), g1, b1)
                    transpose_to(X1[:, j, :], ("X1", j), X1T, j * 128, "X1T")
                if S2 <= 3:
                    continue
                g2 = load_vec("ln2_g")
                b2 = load_vec("ln2_b")
                for sc in range(11):
                    wi = sc % 2
                    dma("pool", WG[wi], wg_l[:, :, sc * 256:(sc + 1) * 256], [], [("WG", wi)])
                    dma("pool", WU[wi], wu_l[:, :, sc * 256:(sc + 1) * 256], [], [("WU", wi)])
                    for j2 in range(2):
                        fc = sc * 2 + j2
                        bg, bkg = bank("G", [0, 1])
                        bu, bku = bank("Uu", [2, 3])
                        for k in range(8):
                            mm(ps[bg], WG[wi][:, k, j2 * 128:(j2 + 1) * 128], X1T[:, k, :], k == 0, k == 7,
                               [("WG", wi), "X1T"], [bkg])
                        for k in range(8):
                            mm(ps[bu], WU[wi][:, k, j2 * 128:(j2 + 1) * 128], X1T[:, k, :], k == 0, k == 7,
                               [("WU", wi), "X1T"], [bku])
                        sg = SG[fc % 2]
                        sgk = ("SG", fc % 2)
                        act(sg, ps[bg], AF.Silu, [bkg], [sgk])
                        tt("dve", AT[:, fc, :], sg, ps[bu], ALU.mult, [sgk, bku], [("AT", fc)])
                if S2 <= 4:
                    continue
                for half in range(2):
                    fb = [4, 5, 6, 7]
                    for sc in range(11):
                        wi = (half * 11 + sc) % 2
                        dma("pool", WD[wi], wd_l[:, sc * 2:sc * 2 + 2, half * 512:(half + 1) * 512], [], [("WD", wi)])
                        for j2 in range(2):
                            fc = sc * 2 + j2
                            for j in range(4):
                                mm(ps[fb[j]], AT[:, fc, j * 128:(j + 1) * 128], WD[wi][:, j2, :], fc == 0, fc == NFC - 1,
                                   [("AT", fc), ("WD", wi)], [("ps", fb[j])])
                    for j in range(4):
                        stt(X1[:, j, half * 512:(half + 1) * 512], X1[:, j, half * 512:(half + 1) * 512], ALPHA,
                            ps[fb[j]], ALU.mult, ALU.add, [("ps", fb[j]), ("X1", j)], [("X1", j)])
                for j in range(4):
                    blk = tile * 4 + j
                    layernorm(X1[:, j, :], ("X1", j), U, "U", g2, b2)
                    o = dma("sp", dst_x[blk * 128:(blk + 1) * 128, :], U, ["U"], [("xs", blk)])
                    if last:
                        all_dma_out.append(o)
                    else:
                        transpose_to(U, "U", X1T, j * 128, "X1T")
                if not last:
                    dma("sp", xtm_f[tile].bitcast(BF16).rearrange("(c p) t -> p c t", p=128), X1T, ["X1T"], [("xtm", tile)])
                    P.cc((lambda jj: (lambda e: e.collective_compute("AllGather", ALU.bypass, replica_groups=rgroups,
                                                                     ins=[xtm_f[jj]], outs=[xta_f[jj]])))(tile),
                         [("xtm", tile)], [("xta", tile)])
                    xta_v = xta_f[tile].bitcast(BF16).rearrange("(r k p) t -> p k r t", r=2, k=8, p=128)
                    for r_ in range(2):
                        dma("sp", XT[:, :, (r_ * 4 + tile) * TS:(r_ * 4 + tile + 1) * TS], xta_v[:, :, r_, :],
                            [("xta", tile)], [("XT", r_ * 4 + tile)])

        build_xt0()
        for l in range(n_layers):
            P.barrier()
            phase1(l)
            P.barrier()
            if phase2:
                phase2_(l, l == n_layers - 1)
        if debug and not phase2:
            pass
        P.barrier()
        P.emit()
    return nc


def kernel(**inputs):
    x = np.ascontiguousarray(np.asarray(inputs["x"], dtype=np.float32))
    nc = build_program()
    base = {nm: np.ascontiguousarray(np.asarray(inputs[nm], dtype=np.float32)) for nm, _ in WNAMES}
    cst = [make_consts(0), make_consts(1)]
    in_maps = []
    for core in range(8):
        bq, r = core // 2, core % 2
        xt = x[bq].reshape(8, 512, D)
        m = dict(base)
        m.update(cst[r])
        m["x"] = np.ascontiguousarray(xt[[nat_tile(p) for p in range(8)]]).reshape(T, D)
        m["xq"] = np.ascontiguousarray(xt[[2 * j + r for j in range(4)]]).reshape(TQ, D)
        in_maps.append(m)
    res = run_bass_kernel_spmd(nc, in_maps, core_ids=list(range(8)))
    out = np.zeros((4, 8, 512, D), np.float32)
    for core in range(8):
        bq, r = core // 2, core % 2
        o = np.asarray(res.results[core]["out"], dtype=np.float32).reshape(4, 512, D)
        for j in range(4):
            out[bq, 2 * j + r] = o[j]
    return out.reshape(4, T, D)
```

```python
import numpy as np
import concourse.bass as bass
import concourse.mybir as mybir
from concourse.bass_utils import run_bass_kernel_spmd
from contextlib import ExitStack

F32 = mybir.dt.float32
BF16 = mybir.dt.bfloat16
AF = mybir.ActivationFunctionType
ALU = mybir.AluOpType
AX = mybir.AxisListType

ENG = ("pe", "act", "dve", "pool", "sp")
NDMASEM = 8


class Op:
    __slots__ = ("eng", "fn", "deps", "is_dma", "signal", "cnt", "semi", "val", "waits", "gid", "is_cc")

    def __init__(self, eng, fn, is_dma):
        self.eng = eng
        self.fn = fn
        self.deps = []
        self.is_dma = is_dma
        self.signal = False
        self.cnt = 0
        self.semi = -1
        self.val = 0
        self.waits = []
        self.is_cc = False


class Prog:
    def __init__(self, nc, stack):
        self.nc = nc
        self.stack = stack
        self.all = []
        self.res = {}
        self.ntens = 0
        self._bar_from = 0

    def sb(self, shape, dt, name=None):
        self.ntens += 1
        return self.stack.enter_context(self.nc.sbuf_tensor("s_" + (name or f"sb{self.ntens}"), list(shape), dt))

    def psum(self, shape, dt, name=None):
        self.ntens += 1
        return self.stack.enter_context(self.nc.psum_tensor(name or f"ps{self.ntens}", list(shape), dt))

    def op(self, eng, fn, reads=(), writes=(), dma=False):
        o = Op(eng, fn, dma)
        deps = o.deps
        res = self.res
        for r in reads:
            st = res.get(r)
            if st is None:
                st = res[r] = [None, []]
            if st[0] is not None:
                deps.append(st[0])
        for w in writes:
            st = res.get(w)
            if st is None:
                st = res[w] = [None, []]
            if st[0] is not None:
                deps.append(st[0])
            deps.extend(st[1])
        for r in reads:
            res[r][1].append(o)
        for w in writes:
            st = res[w]
            st[0] = o
            st[1] = []
        self.all.append(o)
        return o

    def dma(self, eng, out, in_, reads=(), writes=(), **kw):
        return self.op(eng, lambda e: e.dma_start(out=out, in_=in_, **kw), reads, writes, dma=True)

    def cc(self, fn, reads=(), writes=()):
        o = self.op("pool", fn, reads, writes, dma=True)
        o.is_cc = True
        return o

    def wait_all(self, eng, ops):
        o = Op(eng, None, False)
        o.deps = list(ops)
        self.all.append(o)
        return o


    def barrier(self):
        deps = []
        last = {}
        for o in self.all[self._bar_from:]:
            if o.fn is None:
                continue
            if o.is_dma:
                deps.append(o)
            else:
                last[o.eng] = o
        deps.extend(last.values())
        self._bar_from = len(self.all)
        for e in ENG:
            w = Op(e, None, False)
            w.deps = list(deps)
            self.all.append(w)

    def finalize(self):
        nc = self.nc
        for o in self.all:
            nd = []
            seen = set()
            for d in o.deps:
                if id(d) in seen:
                    continue
                seen.add(id(d))
                if d is o:
                    continue
                if not d.is_dma and d.eng == "pe" and o.eng == "pe" and not o.is_dma:
                    continue
                nd.append(d)
                if not d.is_dma:
                    d.signal = True
            o.deps = nd
        cnt = {e: 0 for e in ENG}
        for o in self.all:
            if o.is_dma or o.fn is None:
                continue
            if o.signal:
                cnt[o.eng] += 1
            o.cnt = cnt[o.eng]
        known = {e: {} for e in ENG}
        pool_next = {e: 0 for e in ENG}
        pool_last = {e: [0] * NDMASEM for e in ENG}
        self.nwaits = 0
        ncc = 0
        for o in self.all:
            E = o.eng
            kn = known[E]
            need = {}
            if o.is_cc:
                o.semi = ("cc", ncc)
                o.val = 1
                ncc += 1
            elif o.is_dma:
                s = pool_next[E]
                pool_next[E] = (s + 1) % NDMASEM
                prev = pool_last[E][s]
                key = ("d", E, s)
                if prev > 0 and kn.get(key, 0) < prev:
                    need[key] = prev
                o.semi = (E, s)
                o.val = prev + 16
                pool_last[E][s] = o.val
            for d in o.deps:
                if d.is_dma:
                    key = ("d",) + d.semi
                    v = d.val
                else:
                    key = ("c", d.eng)
                    v = d.cnt
                if kn.get(key, 0) < v and need.get(key, 0) < v:
                    need[key] = v
            for key, v in need.items():
                kn[key] = v
            o.waits = list(need.items())
            self.nwaits += len(need)
        st = self.stack
        self.csem = {e: st.enter_context(nc.semaphore(f"c_{e}")) for e in ENG}
        self.dsem = {(e, s): st.enter_context(nc.semaphore(f"d_{e}{s}")) for e in ("sp", "pool", "act") for s in range(NDMASEM)}
        self.ccsem = [st.enter_context(nc.semaphore(f"cc{i}")) for i in range(ncc)]
        self.byeng = {e: [o for o in self.all if o.eng == e] for e in ENG}

    def _sem(self, key):
        if key[0] == "c":
            return self.csem[key[1]]
        if key[1] == "cc":
            return self.ccsem[key[2]]
        return self.dsem[(key[1], key[2])]

    def emit_engine(self, E, e):
        for o in self.byeng[E]:
            for key, v in o.waits:
                e.wait_ge(self._sem(key), v)
            if o.fn is None:
                continue
            ins = o.fn(e)
            if o.is_cc:
                ins.then_inc(self.ccsem[o.semi[1]])
            elif o.is_dma:
                ins.then_inc(self.dsem[o.semi], 16)
            elif o.signal:
                ins.then_inc(self.csem[E], 1)

    def emit(self):
        nc = self.nc
        self.finalize()
        with nc.Block() as block:
            @block.tensor
            def _(e):
                self.emit_engine("pe", e)

            @block.scalar
            def _(e):
                self.emit_engine("act", e)

            @block.vector
            def _(e):
                self.emit_engine("dve", e)

            @block.gpsimd
            def _(e):
                self.emit_engine("pool", e)

            @block.sync
            def _(e):
                self.emit_engine("sp", e)

D = 1024
T = 4096
NB = 32
NT = 8
TS = 512
DFF = 2816
NFC = 22
INW = 2468
NL = 4
ALPHA = float((2.0 * NL) ** 0.25)
GBASE = {"A": 0, "B": 772, "C": 1188, "D": 1956}
GW = {"A": 772, "B": 416, "C": 768, "D": 512}
MASKV = -30000.0
NS = 4
TQ = NS * 512


def nat_tile(p):
    return 2 * (p % 4) + p // 4


def pos_tile(nt):
    return (nt % 2) * 4 + nt // 2


def pb_nat(n):
    return pos_tile(n // 4) * 4 + n % 4
STAGE = 99
KVF = 0
KVT = 8
VENG = 'dve'
S2 = 99
AMODE = 0
P2T = 8


def make_consts(rank):
    c = {}
    c["identf"] = np.eye(128, dtype=np.float32)
    p = np.arange(128)[:, None]
    xx = np.arange(896)[None, :]
    ms = np.where((xx - 384) < p, MASKV, 0.0)
    mx = np.where((xx - 384) <= p, MASKV, 0.0)
    full = np.full((128, 896), MASKV)
    zero = np.zeros((128, 896))
    mb = np.zeros((2, 2, 128, 896), np.float32)
    for ti, st in enumerate((ms, mx)):
        if rank == 0:
            mb[ti, 0] = st
            mb[ti, 1] = full
        else:
            mb[ti, 0] = zero
            mb[ti, 1] = st
    c["mbase"] = mb
    sel = np.zeros((128, 2), np.float32)
    sel[:, rank] = 1.0
    c["sel"] = sel
    s = np.arange(128)[:, None]
    t = np.arange(128)[None, :]
    tri = np.zeros((4, 128, 128), np.float32)
    tri[0] = (s <= t)
    tri[1] = 1.0
    tri[2] = -(s > t).astype(np.float32)
    tri[3] = -(s <= t).astype(np.float32)
    c["tri"] = tri
    pos = np.arange(T, dtype=np.float32)
    inv = (np.float32(10000.0) ** (-np.arange(0, 32, 2, dtype=np.float32) / np.float32(32))).astype(np.float32)
    ang = (pos[:, None] * inv[None, :]).astype(np.float32)
    cs = np.cos(ang).astype(np.float32).T
    sn = np.sin(ang).astype(np.float32).T
    rope = np.zeros((2, 32, T), np.float32)
    rope[0, :16] = cs
    rope[0, 16:] = cs
    rope[1, :16] = -sn
    rope[1, 16:] = sn
    r8 = rope.reshape(2, 32, 8, 512)
    c["rope"] = np.ascontiguousarray(r8[:, :, [nat_tile(pp_) for pp_ in range(8)], :]).reshape(2, 32, T)
    c["ropeq"] = np.ascontiguousarray(r8[:, :, [2 * j_ + rank for j_ in range(4)], :]).reshape(2, 32, TQ)
    bd = np.zeros((4, 128, 256), np.float32)
    pp = np.arange(128)[:, None].astype(np.float32)
    cc = np.arange(128)[None, :].astype(np.float32)
    for h in range(4):
        m = np.float32(2.0 ** (-8.0 * (h + 1) / 4.0))
        d1 = 128.0 + cc - pp
        bd[h, :, 0:128] = np.where(cc < pp, -m * d1, MASKV)
        d2 = cc - pp
        bd[h, :, 128:256] = np.where(cc >= pp, -m * d2, MASKV)
    c["biasd"] = bd
    ind = np.zeros((128, 8, 4), np.float32)
    for ch in range(8):
        ind[:, ch, ch // 2] = 1.0
    c["ind"] = ind
    return c


WNAMES = [("w_in", [NL, D, INW]), ("fox_b_f", [NL, 4]), ("mla_g_q", [NL, 256]), ("mla_g_kv", [NL, 128]),
          ("mla_w_uq", [NL, 256, 384]), ("mla_w_ukv", [NL, 128, 512]), ("swa_sinks", [NL, 4]),
          ("mix_g", [NL, D]), ("w_o", [NL, D, D]), ("ln1_g", [NL, D]), ("ln1_b", [NL, D]),
          ("w_gate", [NL, D, DFF]), ("w_up", [NL, D, DFF]), ("w_down", [NL, DFF, D]),
          ("ln2_g", [NL, D]), ("ln2_b", [NL, D])]
CNAMES = [("identf", [128, 128]), ("mbase", [2, 2, 128, 896]), ("tri", [4, 128, 128]), ("rope", [2, 32, T]),
          ("ropeq", [2, 32, TQ]), ("sel", [128, 2]), ("biasd", [4, 128, 256]), ("ind", [128, 8, 4])]


class Arena:
    def __init__(self, P, nbytes, name):
        self.t = P.sb([128, nbytes // 2], BF16, name)
        self.n = nbytes
        self.off = 0

    def reset(self):
        self.off = 0

    def get(self, shape, dt):
        esz = 4 if dt == F32 else 2
        ne = int(np.prod(shape[1:]))
        n = (ne * esz + 63) // 64 * 64
        a = self.t[:, self.off // 2:(self.off + n) // 2]
        self.off += n
        assert self.off <= self.n, ("arena overflow", self.off, self.n)
        if dt == F32:
            a = a.bitcast(F32)
        a = a[:, 0:ne]
        if len(shape) == 3:
            a = a.rearrange("p (a b) -> p a b", a=shape[1])
        elif len(shape) == 4:
            a = a.rearrange("p (a b c) -> p a b c", a=shape[1], b=shape[2])
        return a


def build_program(n_layers=NL, groups="ABCD", debug=False, phase2=True, n_cores=8):
    nc = bass.Bass("TRN2", target_bir_lowering=False)
    dd = {}
    dd["x"] = nc.dram_tensor("x", [T, D], F32, kind="ExternalInput").ap()
    for nm, shp in WNAMES + CNAMES:
        dd[nm] = nc.dram_tensor(nm, list(shp), F32, kind="ExternalInput").ap()
    dd["xq"] = nc.dram_tensor("xq", [TQ, D], F32, kind="ExternalInput").ap()
    out_d = nc.dram_tensor("out", [TQ, D], F32, kind="ExternalOutput").ap()
    xs_d = nc.dram_tensor("xs", [TQ, D], F32, kind="Internal").ap()
    mixT_d = nc.dram_tensor("mixT", [D, TQ], BF16, kind="ExternalOutput" if debug else "Internal").ap()
    xtm_f = [nc.dram_tensor(f"xtm{j_}", [D, TS // 2], F32, kind="Internal").ap() for j_ in range(NS)]
    xta_f = [nc.dram_tensor(f"xta{j_}", [2 * D, TS // 2], F32, kind="Internal").ap() for j_ in range(NS)]
    rgroups = [[2 * g_, 2 * g_ + 1] for g_ in range(n_cores // 2)]

    with ExitStack() as st:
        P = Prog(nc, st)

        def mm(out, lhsT, rhs, start, stop, r, w, **kw):
            P.op("pe", lambda e: e.matmul(out, lhsT=lhsT, rhs=rhs, start=start, stop=stop, **kw), r, w)

        def tr(out, in_, r, w):
            P.op("pe", lambda e: e.transpose(out=out, in_=in_, identity=identf), list(r) + ["const"], w)

        def act(out, in_, func, r, w, scale=None, bias=None):
            kw = {}
            if scale is not None:
                kw["scale"] = scale
            if bias is not None:
                kw["bias"] = bias
            P.op("act", lambda e: e.activation(out=out, in_=in_, func=func, **kw), r, w)

        def tt(eng, out, in0, in1, op, r, w):
            P.op(eng, lambda e: e.tensor_tensor(out=out, in0=in0, in1=in1, op=op), r, w)

        def ts(eng, out, in0, s1, op0, r, w, s2=None, op1=None):
            if op1 is None:
                P.op(eng, lambda e: e.tensor_scalar(out=out, in0=in0, scalar1=s1, scalar2=None, op0=op0), r, w)
            else:
                P.op(eng, lambda e: e.tensor_scalar(out=out, in0=in0, scalar1=s1, scalar2=s2, op0=op0, op1=op1), r, w)

        def stt(out, in0, scalar, in1, op0, op1, r, w):
            P.op("dve", lambda e: e.scalar_tensor_tensor(out=out, in0=in0, scalar=scalar, in1=in1, op0=op0, op1=op1), r, w)

        def cp(eng, out, in_, r, w):
            if eng == "act":
                P.op("act", lambda e: e.activation(out=out, in_=in_, func=AF.Copy), r, w)
            else:
                P.op(eng, lambda e: e.tensor_copy(out=out, in_=in_), r, w)

        def rsqrt(out, in_, eps, r, w):
            act(out, in_, AF.Ln, r, w, bias=eps)
            act(out, out, AF.Exp, list(w), w, scale=-0.5)

        def recip(out, in_, r, w):
            P.op("dve", lambda e: e.reciprocal(out=out, in_=in_), r, w)

        def dma(eng, out, in_, r, w, **kw):
            return P.dma(eng, out, in_, r, w, **kw)

        ps = [P.psum([128, 512], F32, f"bank{i}")[:] for i in range(8)]
        rot = {}

        def bank(pool, lst):
            i = rot.get(pool, 0)
            rot[pool] = i + 1
            b = lst[i % len(lst)]
            return b, ("ps", b)

        XT = P.sb([128, 8, T], BF16, "XT")[:]
        AR1 = Arena(P, 49152, "AR1")
        AR2 = Arena(P, 80 * 1024, "AR2")
        identf = P.sb([128, 128], F32, "identf")[:]
        identb = P.sb([128, 128], BF16, "identb")[:]
        onesb = P.sb([128, 128], BF16, "onesb")[:]
        MB = P.sb([128, 4, 896], BF16, "MB")[:]
        TRI = P.sb([128, 4, 128], F32, "TRI")[:]
        TRIB = P.sb([128, 2, 128], BF16, "TRIB")[:]
        BIASD = P.sb([128, 4, 256], F32, "BIASD")[:]
        IND = P.sb([128, 8, 4], BF16, "IND")[:]
        SMALL = P.sb([128, 64], F32, "SMALL")[:]
        BF_ = SMALL[:, 0:4]
        ES = SMALL[:, 4:8]
        GQ = SMALL[:, 8:10]
        GKV = SMALL[:, 10:11]
        MG = SMALL[:, 16:24]
        SEL = SMALL[:, 24:26]
        CARRYX = P.sb([128, NB, 4], F32, "CARRYX")[:]

        dma("sp", identf, dd["identf"], [], ["const"])
        dma("pool", identb, dd["identf"], [], ["const"])
        dma("pool", onesb, dd["tri"][1], [], ["const"])
        dma("pool", MB, dd["mbase"].rearrange("t w p c -> p (t w) c"), [], ["const"])
        dma("sp", SEL, dd["sel"], [], ["const"])
        dma("sp", TRI, dd["tri"].rearrange("i p c -> p i c"), [], ["const"])
        dma("pool", TRIB, dd["tri"][2:4].rearrange("i p c -> p i c"), [], ["const"])
        dma("sp", BIASD, dd["biasd"].rearrange("h p c -> p h c"), [], ["const"])
        dma("pool", IND, dd["ind"], [], ["const"])
        P.op("dve", lambda e: e.memset(CARRYX[:, 0, :], 0.0), [], ["carry0"])

        all_dma_out = []

        def build_xt0():
            AR2.reset()
            XB = [AR2.get([128, D], F32) for _ in range(2)]
            for blk in range(NB):
                xb = XB[blk % 2]
                k = ("XB0", blk % 2)
                dma("sp", xb, dd["x"][blk * 128:(blk + 1) * 128, :], [], [k])
                for half in range(2):
                    b, bk = bank("M", list(range(8)))
                    for j in range(4):
                        c = half * 4 + j
                        tr(ps[b][:, j * 128:(j + 1) * 128], xb[:, c * 128:(c + 1) * 128], [k], [bk])
                    cp("act" if half else "dve", XT[:, half * 4:half * 4 + 4, blk * 128:(blk + 1) * 128],
                       ps[b].rearrange("p (a b) -> p a b", a=4), [bk], [("XT", blk // 4)])

        def phase1(l):
            AR1.reset()
            AR2.reset()
            KT = [AR1.get([128, T], BF16) for _ in range(4)]
            V = AR1.get([128, NB, 256], BF16)
            QT = [[AR2.get([128, TS], BF16) for _ in range(2)] for _ in range(4)]
            WIN = AR2.get([128, 8, 772], BF16)
            PT = [AR2.get([128, TS], BF16) for _ in range(3)]
            CT = [[AR2.get([128, TS], F32) for _ in range(2)] for _ in range(2)]
            RC = AR2.get([128, TS], F32)
            MO = [AR2.get([128, TS], BF16) for _ in range(2)]
            GATE = AR2.get([128, NB, 4], F32)
            TOTs = AR2.get([128, NB, 4], F32)
            CNEG = AR2.get([128, NB, 4], F32)
            BT = [AR2.get([128, NB, 4], F32) for _ in range(2)]
            WUQ = AR2.get([128, 2, 384], BF16)
            WUQS = AR2.get([128, 2, 384], BF16)
            WUKV = AR2.get([128, 512], BF16)
            WKRS = AR2.get([128, 8, 96], BF16)
            SQ = AR2.get([128, 2, TS], BF16)
            RSB = AR2.get([128, TS], F32)
            CQN = AR2.get([128, 2, TS], BF16)
            CKVN = AR2.get([128, TS], BF16)
            ROPE = AR2.get([128, 2, TS], F32)
            RT1 = AR2.get([128, TS], F32)
            RT2 = AR2.get([128, TS], F32)
            SBD = [AR2.get([128, 256], F32) for _ in range(2)]
            PD = [AR2.get([128, 256], BF16) for _ in range(2)]
            XTQ = AR2.get([128, 8, TS], BF16)
            SPB = [AR2.get([128, TS], BF16) for _ in range(2)]
            MO2 = AR2.get([128, TS], BF16)

            def make_xtq(j):
                ts("dve", XTQ, XT[:, :, j * TS:(j + 1) * TS], SEL[:, 0:1], ALU.mult, [("XT", j), "const"], ["XTQ"])
                stt(XTQ, XT[:, :, (4 + j) * TS:(5 + j) * TS], SEL[:, 1:2], XTQ, ALU.mult, ALU.add,
                    [("XT", 4 + j), "const", "XTQ"], ["XTQ"])

            w_in_l = dd["w_in"][l].rearrange("(k p) n -> p k n", p=128)
            ALLB = list(range(8))
            mo_cnt = [0]

            def load_win(g):
                dma("pool", WIN[:, :, 0:GW[g]], w_in_l[:, :, GBASE[g]:GBASE[g] + GW[g]], [], ["WIN"])

            def xt_keys(tile):
                return [("XT", tile)]

            def proj_fm(dst, dk, col0, tile, eng, wkeys, dkeys, pool="M", banks=None, lw=None, own=False):
                b, bk = bank(pool, banks or [7])
                W = WIN if lw is None else lw
                for k in range(8):
                    if own:
                        mm(ps[b][0:dk, :], W[:, k, col0:col0 + dk], XTQ[:, k, :], k == 0, k == 7, ["XTQ"] + wkeys, [bk])
                    else:
                        mm(ps[b][0:dk, :], W[:, k, col0:col0 + dk], XT[:, k, tile * TS:(tile + 1) * TS], k == 0, k == 7,
                           xt_keys(tile) + wkeys, [bk])
                if dst is not None:
                    cp(eng, dst, ps[b][0:dk, :], [bk], dkeys)
                return b, bk

            def store_mix(g, h, qt, src_bank, bk, rc_ap, extra_r):
                i = mo_cnt[0] % 2
                mo_cnt[0] += 1
                mo = MO[i]
                mk = ("MO", i)
                if rc_ap is None:
                    cp("dve", mo[0:64, :], ps[src_bank][0:64, :], [bk], [mk])
                else:
                    tt("dve", mo[0:64, :], ps[src_bank][0:64, :], rc_ap, ALU.mult, [bk] + extra_r, [mk])
                row = "ABCD".index(g) * 256 + h * 64
                o = dma("sp", mixT_d[row:row + 64, qt * TS:(qt + 1) * TS], mo[0:64, :], [mk], [("mixT", g, h, qt)])

            def softmax_unit(g, h, qt, dk, scale, qap, qkeys, bias_fn, bias_keys, u):
                kbl = [4 * p_ + i_ for p_ in list(range(qt + 1)) + list(range(4, 4 + qt + 1)) for i_ in range(4)]
                nkb = len(kbl)
                ob = [3, 4][u % 2]
                db = [5, 6][u % 2]
                obk, dbk = ("ps", ob), ("ps", db)
                LA = 2
                sb_of = {}
                for step in range(nkb + LA):
                    if step < nkb:
                        kb = kbl[step]
                        b, bk = bank("S", [0, 1, 2])
                        sb_of[step] = (b, bk)
                        p_, i = kb // 4, kb % 4
                        wh = 0 if p_ == qt else (1 if p_ == 4 + qt else -1)
                        mm(ps[b], KT[h][0:dk, kb * 128:(kb + 1) * 128], qap, True, wh < 0,
                           [("KT", h, kb // 4)] + qkeys, [bk])
                        if wh >= 0:
                            mm(ps[b], identb, MB[:, wh, 384 - 128 * i:384 - 128 * i + 512], False, True, ["const"], [bk])
                    s2 = step - LA
                    if 0 <= s2 < nkb:
                        kb = kbl[s2]
                        b, bk = sb_of.pop(s2)
                        pi = s2 % 3
                        pt, pk = PT[pi], ("PT", pi)
                        act(pt, ps[b], AF.Exp, [bk] + bias_keys, [pk], scale=scale,
                            bias=(bias_fn(kb) if bias_fn else None))
                        mm(ps[ob][0:64, :], V[:, kb, h * 64:(h + 1) * 64], pt, s2 == 0, s2 == nkb - 1,
                           [pk, ("V", kb // 4)], [obk])
                        mm(ps[db][0:64, :], onesb[:, 0:64], pt, s2 == 0, s2 == nkb - 1, [pk, "const"], [dbk])
                recip(RC[0:64, :], ps[db][0:64, :], [dbk], ["RC"])
                store_mix(g, h, qt, ob, obk, RC[0:64, :], ["RC"])

            def sb_units(hs, qt, qaps, qkeys):
                order = [pos_tile(nt_) * 4 + i_ for nt_ in reversed(range(2 * qt + 2)) for i_ in reversed(range(4))]
                nkb = len(order)
                scale = 0.125
                zb = [[0, 1], [4, 5]]
                accb = [2, 6]
                obb = [3, 7]
                LA = 1
                z_of = {}
                for step in range(nkb + LA):
                    for si, h in enumerate(hs):
                        if step < nkb:
                            kb = order[step]
                            b = zb[si][step % 2]
                            bk = ("ps", b)
                            z_of[(si, step)] = (b, bk)
                            p_, i = kb // 4, kb % 4
                            wh = 0 if p_ == qt else (1 if p_ == 4 + qt else -1)
                            mm(ps[b], KT[h][0:64, kb * 128:(kb + 1) * 128], qaps[si], True, wh < 0,
                               [("KT", h, kb // 4)] + qkeys[si], [bk])
                            if wh >= 0:
                                mm(ps[b], identb, MB[:, 2 + wh, 384 - 128 * i:384 - 128 * i + 512], False, True,
                                   ["const"], [bk])
                    for si, h in enumerate(hs):
                        s2 = step - LA
                        if 0 <= s2 < nkb:
                            kb = order[s2]
                            b, bk = z_of.pop((si, s2))
                            SP, T1 = CT[si]
                            spk, t1k = ("SP", si), ("T1", si)
                            ab, abk = accb[si], ("ps", accb[si])
                            ob, obk = obb[si], ("ps", obb[si])
                            spb, spbk = SPB[si], ("SPB", si)
                            act(SP, ps[b], AF.Exp, [bk], [spk], scale=scale)
                            act(spb, SP, AF.Ln, [spk], [spbk], bias=1.0)
                            mm(ps[ab], TRIB[:, 0, :], spb, s2 == 0, True, [spbk, "const"], [abk], skip_group_check=True)
                            stt(T1, ps[b], scale, spb, ALU.mult, ALU.subtract, [bk, spbk], [t1k])
                            tt("dve", T1, ps[ab], T1, ALU.add, [abk, t1k], [t1k])
                            if s2 < nkb - 1:
                                mm(ps[ab], TRIB[:, 1, :], spb, False, True, [spbk, "const"], [abk], skip_group_check=True)
                            pi = (s2 * 2 + si) % 3
                            pt, pk = PT[pi], ("PT", pi)
                            act(pt, T1, AF.Exp, [t1k], [pk])
                            mm(ps[ob][0:64, :], V[:, kb, h * 64:(h + 1) * 64], pt, s2 == 0, s2 == nkb - 1,
                               [pk, ("V", kb // 4)], [obk])
                for si, h in enumerate(hs):
                    store_mix("C", h, qt, obb[si], ("ps", obb[si]), None, [])

            def kv_pass(g, kcols, vcol0, nv, gates):
                KVB = list(range(7)) if gates else ALLB
                for tile in range(NT):
                    for h, c0 in enumerate(kcols):
                        proj_fm(KT[h][0:64, tile * TS:(tile + 1) * TS], 64, c0, tile, "dve", ["WIN"],
                                [("KT", h, tile)], banks=KVB)
                    for j in range(4):
                        blk = tile * 4 + j
                        b, bk = bank("M", KVB)
                        n = nv
                        for k in range(8):
                            mm(ps[b][:, 0:n], XT[:, k, blk * 128:(blk + 1) * 128], WIN[:, k, vcol0:vcol0 + n],
                               k == 0, k == 7, xt_keys(tile) + ["WIN"], [bk])
                        cp("dve", V[:, blk, 0:nv], ps[b][:, 0:nv], [bk], [("V", tile)])
                        if gates:
                            for k in range(8):
                                mm(ps[7][:, blk * 4:(blk + 1) * 4], XT[:, k, blk * 128:(blk + 1) * 128],
                                   WIN[:, k, vcol0 + nv:vcol0 + nv + 4], k == 0, k == 7, xt_keys(tile) + ["WIN"],
                                   [("ps", 7)], skip_group_check=True)
                if gates:
                    cp("dve", GATE.rearrange("p a b -> p (a b)"), ps[7][:, 0:128], [("ps", 7)], ["GATE"])

            if "A" in groups:
                load_win("A")
                dma("sp", BF_, dd["fox_b_f"][l].partition_broadcast(128), [], ["BF"])
                kv_pass("A", [256, 320, 384, 448], 512, 256, AMODE != 1)
                if AMODE in (0, 3):
                    for h in range(4):
                        ts("dve", GATE[:, :, h], GATE[:, :, h], BF_[:, h:h + 1], ALU.add, ["GATE", "BF"], ["GATE"])
                    GF = GATE.rearrange("p a b -> p (a b)")
                    act(GF, GF, AF.Exp, ["GATE"], ["GATE"], scale=-1.0)
                    act(GF, GF, AF.Ln, ["GATE"], ["GATE"], bias=1.0)
                    if False:
                        pass
                    b1, bk1 = bank("M", ALLB)
                    mm(ps[b1][:, 0:128], TRI[:, 1, :], GF, True, True, ["GATE", "const"], [bk1])
                    b2, bk2 = bank("M", ALLB)
                    mm(ps[b2][:, 0:128], TRI[:, 0, :], GF, True, True, ["GATE", "const"], [bk2])
                    cp("dve", TOTs.rearrange("p a b -> p (a b)"), ps[b1][:, 0:128], [bk1], ["TOTs"])
                    if False:
                        pass
                    for n_ in range(NB - 1):
                        tt("dve", CARRYX[:, pb_nat(n_ + 1), :], CARRYX[:, pb_nat(n_), :], TOTs[:, pb_nat(n_), :], ALU.add,
                           ["TOTs", "carry0", "CARRYX"], ["CARRYX"])
                    tt("dve", CNEG, ps[b2][:, 0:128].rearrange("p (a b) -> p a b", a=NB), CARRYX, ALU.add,
                       [bk2, "CARRYX", "carry0"], ["CNEG"])
                    if False:
                        pass
                    u = 0
                    for qt in range(NS):
                        bt = BT[qt % 2]
                        btk = ("BT", qt % 2)
                        for h in range(4):
                            ts("dve", bt[:, :, h], CNEG[:, :, h], CARRYX[:, (4 + qt) * 4, h:h + 1], ALU.subtract,
                               ["CNEG", "CARRYX", "carry0"], [btk])
                        make_xtq(qt)
                        for h in range(4):
                            qb = u % 2
                            qk = ("QT", h, qb)
                            proj_fm(QT[h][qb][0:64, :], 64, h * 64, qt, "dve", ["WIN"], [qk], own=True)
                            softmax_unit("A", h, qt, 64, 0.125, QT[h][qb][0:64, :], [qk],
                                         (lambda kb, hh=h, bb=bt: bb[:, kb, hh:hh + 1]), [btk], u)
                            u += 1

            if "B" in groups:
                P.barrier()
                load_win("B")
                wb = GBASE["B"]
                uq = dd["mla_w_uq"][l].rearrange("(c p) n -> p c n", p=128)
                dma("pool", WUQ, uq, [], ["WUQ"])
                uq4 = uq.rearrange("p c (h x) -> p c h x", h=4)
                wq4 = WUQS.rearrange("p c (h x) -> p c h x", h=4)
                for c in range(2):
                    dma("pool", wq4[:, c, :, 0:64], uq4[:, c, :, 0:64], [], ["WUQS"])
                    dma("pool", wq4[:, c, :, 64:80], uq4[:, c, :, 80:96], [], ["WUQS"])
                    dma("pool", wq4[:, c, :, 80:96], uq4[:, c, :, 64:80], [], ["WUQS"])
                dma("pool", WUKV, dd["mla_w_ukv"][l], [], ["WUKV"])
                dma("pool", WKRS[:, :, 0:64], w_in_l[:, :, wb + 320:wb + 384], [], ["WKRS"])
                dma("pool", WKRS[:, :, 64:80], w_in_l[:, :, wb + 400:wb + 416], [], ["WKRS"])
                dma("pool", WKRS[:, :, 80:96], w_in_l[:, :, wb + 384:wb + 400], [], ["WKRS"])
                dma("sp", GQ, dd["mla_g_q"][l].rearrange("(c p) -> p c", p=128), [], ["GQ"], allow_slow_non_contiguous=True)
                dma("sp", GKV, dd["mla_g_kv"][l].rearrange("(c p) -> p c", p=128), [], ["GKV"], allow_slow_non_contiguous=True)
                ts("dve", GQ, GQ, 16.0, ALU.mult, ["GQ"], ["GQ"])
                ts("dve", GKV, GKV, float(np.sqrt(128.0)), ALU.mult, ["GKV"], ["GKV"])
                WUKV4 = WUKV.rearrange("p (h x) -> p h x", h=4)

                def load_rope(tile, own=False):
                    src_ = dd["ropeq"] if own else dd["rope"]
                    dma("sp", ROPE[64:96, 0, :], src_[0, :, tile * TS:(tile + 1) * TS], [], ["ROPE"])
                    dma("sp", ROPE[64:96, 1, :], src_[1, :, tile * TS:(tile + 1) * TS], [], ["ROPE"])

                def rope_apply(ba, bka, bb, bkb, dsts, dkeys):
                    tt("dve", RT1[64:96, :], ps[ba][64:96, :], ROPE[64:96, 0, :], ALU.mult, [bka, "ROPE"], ["RT1"])
                    tt("dve", RT2[64:96, :], ps[bb][64:96, :], ROPE[64:96, 1, :], ALU.mult, [bkb, "ROPE"], ["RT2"])
                    for dst, dk_ in zip(dsts, dkeys):
                        tt("dve", dst, RT1[64:96, :], RT2[64:96, :], ALU.add, ["RT1", "RT2"], [dk_])

                for tile in range(NT):
                    ba, bka = proj_fm(None, 128, 256, tile, None, ["WIN"], None, banks=ALLB)
                    act(SQ[:, 0, :], ps[ba], AF.Square, [bka], ["SQ"])
                    bs, bks = bank("M", ALLB)
                    mm(ps[bs], onesb, SQ[:, 0, :], True, True, ["SQ", "const"], [bks])
                    rsqrt(RSB, ps[bs], 128.0 * 1e-6, [bks], ["RSB"])
                    stt(CKVN, ps[ba], GKV[:, 0:1], RSB, ALU.mult, ALU.mult, [bka, "RSB", "GKV"], ["CKVN"])
                    for h in range(4):
                        b, bk = bank("M", ALLB)
                        mm(ps[b][0:64, :], WUKV[:, h * 128:h * 128 + 64], CKVN, True, True, ["CKVN", "WUKV"], [bk])
                        cp("dve", KT[h][0:64, tile * TS:(tile + 1) * TS], ps[b][0:64, :], [bk], [("KT", h, tile)])
                    for j in range(4):
                        blk = tile * 4 + j
                        b, bk = bank("M", ALLB)
                        mm(ps[b][:, 0:256].rearrange("p (h x) -> p h x", h=4), CKVN[:, j * 128:(j + 1) * 128],
                           WUKV4[:, :, 64:128], True, True, ["CKVN", "WUKV"], [bk])
                        cp("dve", V[:, blk, :], ps[b][:, 0:256], [bk], [("V", tile)])
                    bc, bkc = proj_fm(None, 96, 320, tile, None, ["WIN"], None, banks=ALLB)
                    bd_, bkd = proj_fm(None, 96, 0, tile, None, ["WKRS"], None, banks=ALLB, lw=WKRS)
                    load_rope(tile)
                    rope_apply(bc, bkc, bd_, bkd, [KT[h][64:96, tile * TS:(tile + 1) * TS] for h in range(4)],
                               [("KT", h, tile) for h in range(4)])
                u = 0
                sc_b = float(96.0 ** -0.5)
                for qt in range(NS):
                    make_xtq(qt)
                    bq = []
                    for c in range(2):
                        bq.append(proj_fm(None, 128, c * 128, qt, None, ["WIN"], None, banks=[7, 0, 1, 2], own=True))
                        act(SQ[:, c, :], ps[bq[c][0]], AF.Square, [bq[c][1]], ["SQ"])
                    bs, bks = bank("M", [7, 0, 1, 2])
                    for c in range(2):
                        mm(ps[bs], onesb, SQ[:, c, :], c == 0, c == 1, ["SQ", "const"], [bks])
                    rsqrt(RSB, ps[bs], 256.0 * 1e-6, [bks], ["RSB"])
                    for c in range(2):
                        stt(CQN[:, c, :], ps[bq[c][0]], GQ[:, c:c + 1], RSB, ALU.mult, ALU.mult,
                            [bq[c][1], "RSB", "GQ"], ["CQN"])
                    load_rope(qt, own=True)
                    for h in range(4):
                        qb = u % 2
                        qk = ("QT", h, qb)
                        be, bke = bank("M", [7, 0, 1, 2])
                        for c in range(2):
                            mm(ps[be][0:96, :], WUQ[:, c, h * 96:(h + 1) * 96], CQN[:, c, :], c == 0, c == 1,
                               ["CQN", "WUQ"], [bke])
                        bf_, bkf = bank("M", [7, 0, 1, 2])
                        for c in range(2):
                            mm(ps[bf_][0:96, :], WUQS[:, c, h * 96:(h + 1) * 96], CQN[:, c, :], c == 0, c == 1,
                               ["CQN", "WUQS"], [bkf])
                        cp("dve", QT[h][qb][0:64, :], ps[be][0:64, :], [bke], [qk])
                        rope_apply(be, bke, bf_, bkf, [QT[h][qb][64:96, :]], [qk])
                        softmax_unit("B", h, qt, 96, sc_b, QT[h][qb][0:96, :], [qk], None, [], u)
                        u += 1

            if "C" in groups:
                P.barrier()
                load_win("C")
                kv_pass("C", [256, 320, 384, 448], 512, 256, False)
                for qt in range(NS):
                    make_xtq(qt)
                    for pair in range(2):
                        hs = [2 * pair, 2 * pair + 1]
                        qaps, qkeys = [], []
                        for h in hs:
                            qb = qt % 2
                            qk = ("QT", h, qb)
                            proj_fm(QT[h][qb][0:64, :], 64, h * 64, qt, "dve", ["WIN"], [qk], banks=[0, 1, 4, 5], own=True)
                            qaps.append(QT[h][qb][0:64, :])
                            qkeys.append([qk])
                        sb_units(hs, qt, qaps, qkeys)

            if "D" in groups:
                P.barrier()
                load_win("D")
                dma("sp", ES, dd["swa_sinks"][l].partition_broadcast(128), [], ["ES"])
                act(ES, ES, AF.Exp, ["ES"], ["ES"])
                kv_pass("D", [256, 320], 384, 128, False)
                u = 0
                for js in range(NS):
                    for h in range(4):
                        kvh = h // 2
                        for half in range(2):
                            nt = 2 * js + half
                            pp = pos_tile(nt)
                            qb = u % 2
                            qk = ("QT", h, qb)
                            proj_fm(QT[h][qb][0:64, :], 64, h * 64, pp, "dve", ["WIN"], [qk])
                            ob = [3, 4][u % 2]
                            db = [5, 6][u % 2]
                            obk, dbk = ("ps", ob), ("ps", db)
                            for j in range(4):
                                n = 4 * nt + j
                                pn = pb_nat(n)
                                pm = pb_nat(n - 1) if n > 0 else 0
                                lo = 0 if n > 0 else 128
                                b, bk = bank("S", [0, 1, 2])
                                qa = QT[h][qb][0:64, j * 128:(j + 1) * 128]
                                if n > 0:
                                    mm(ps[b][:, 0:128], KT[kvh][0:64, pm * 128:(pm + 1) * 128], qa, True, True,
                                       [("KT", kvh, pm // 4), qk], [bk])
                                mm(ps[b][:, 128:256], KT[kvh][0:64, pn * 128:(pn + 1) * 128], qa, True, True,
                                   [("KT", kvh, pn // 4), qk], [bk])
                                si = (u * 4 + j) % 2
                                sbd, pd = SBD[si], PD[si]
                                stt(sbd[:, lo:256], ps[b][:, lo:256], 0.125, BIASD[:, h, lo:256], ALU.mult, ALU.add,
                                    [bk, "const"], [("SBD", si)])
                                act(pd[:, lo:256], sbd[:, lo:256], AF.Exp, [("SBD", si)], [("PD", si)])
                                oc = slice(j * 128, (j + 1) * 128)
                                if n > 0:
                                    mm(ps[ob][0:64, oc], V[:, pm, kvh * 64:(kvh + 1) * 64], pd[:, 0:128], True, False,
                                       [("PD", si), ("V", pm // 4)], [obk])
                                    mm(ps[db][0:64, oc], onesb[:, 0:64], pd[:, 0:128], True, False, [("PD", si), "const"], [dbk])
                                mm(ps[ob][0:64, oc], V[:, pn, kvh * 64:(kvh + 1) * 64], pd[:, 128:256], n == 0, True,
                                   [("PD", si), ("V", pn // 4)], [obk])
                                mm(ps[db][0:64, oc], onesb[:, 0:64], pd[:, 128:256], n == 0, True, [("PD", si), "const"], [dbk])
                            ts("dve", RC[0:64, :], ps[db][0:64, :], ES[0:64, h:h + 1], ALU.add, [dbk, "ES"], ["RC"])
                            recip(RC[0:64, :], RC[0:64, :], ["RC"], ["RC"])
                            tt("dve", MO[half][0:64, :], ps[ob][0:64, :], RC[0:64, :], ALU.mult, [obk, "RC"], [("MO", half)])
                            u += 1
                        ts("dve", MO2[0:64, :], MO[0][0:64, :], SEL[0:64, 0:1], ALU.mult, [("MO", 0), "const"], ["MO2"])
                        stt(MO2[0:64, :], MO[1][0:64, :], SEL[0:64, 1:2], MO2[0:64, :], ALU.mult, ALU.add,
                            [("MO", 1), "const", "MO2"], ["MO2"])
                        row = 3 * 256 + h * 64
                        dma("sp", mixT_d[row:row + 64, js * TS:(js + 1) * TS], MO2[0:64, :], ["MO2"], [("mixT", "D", h, js)])

        def phase2_(l, last):
            AR1.reset()
            AR2.reset()
            AT = AR1.get([128, NFC, TS], BF16)
            MX = AR1.get([128, 8, TS], BF16)
            X1 = AR1.get([128, 4, D], F32)
            WOS = AR2.get([128, 8, D], BF16)
            WST = AR2.get([128, D], F32)
            LNV = [AR2.get([128, D], F32) for _ in range(2)]
            WG = [AR2.get([128, 8, 256], BF16) for _ in range(2)]
            WU = [AR2.get([128, 8, 256], BF16) for _ in range(2)]
            WD = [AR2.get([128, 2, TS], BF16) for _ in range(2)]
            X1T = AR2.get([128, 8, TS], BF16)
            SQ2 = [AR2.get([128, TS], BF16) for _ in range(2)]
            ACCY = AR2.get([128, D], F32)
            U = AR2.get([128, D], F32)
            XB = AR2.get([128, D], F32)
            SG = [AR2.get([128, TS], F32) for _ in range(2)]
            RS = AR2.get([128, 4, 4], F32)
            STAT = AR2.get([128, 2, 6], F32)
            MV = AR2.get([128, 4], F32)

            dma("sp", MG, dd["mix_g"][l].rearrange("(c p) -> p c", p=128), [], ["MG"], allow_slow_non_contiguous=True)
            wo_l = dd["w_o"][l].rearrange("(c p) n -> p c n", p=128)
            for c in range(8):
                dma("sp", WST, wo_l[:, c, :], [], ["WST"])
                ts("dve", WOS[:, c, :], WST, MG[:, c:c + 1], ALU.mult, ["WST", "MG"], ["WOS"])
            wg_l = dd["w_gate"][l].rearrange("(k p) n -> p k n", p=128)
            wu_l = dd["w_up"][l].rearrange("(k p) n -> p k n", p=128)
            wd_l = dd["w_down"][l].rearrange("(c p) n -> p c n", p=128)
            mixT_v = mixT_d.rearrange("(c p) t -> p c t", p=128)
            src_x = dd["xq"] if l == 0 else xs_d
            dst_x = out_d if last else xs_d
            lnvi = [0]

            def load_vec(name):
                i = lnvi[0] % 2
                lnvi[0] += 1
                dma("sp", LNV[i], dd[name][l].partition_broadcast(128), [], [("LNV", i)])
                return LNV[i], ("LNV", i)

            def layernorm(src, srck, dst, dstk, gk, bk_):
                g_ap, gkey = gk
                b_ap, bkey = bk_
                for hf in range(2):
                    P.op("dve", (lambda hh: (lambda e: e.bn_stats(out=STAT[:, hh, :], in_=src[:, hh * 512:(hh + 1) * 512])))(hf),
                         [srck], ["STAT"])
                P.op("dve", lambda e: e.bn_aggr(out=MV[:, 0:2], in_=STAT.rearrange("p a b -> p (a b)")), ["STAT"], ["MV"])
                rsqrt(MV[:, 2:3], MV[:, 1:2], 1e-5, ["MV"], ["MV"])
                ts("dve", dst, src, MV[:, 0:1], ALU.subtract, [srck, "MV"], [dstk])
                ts("dve", dst, dst, MV[:, 2:3], ALU.mult, [dstk, "MV"], [dstk])
                tt("pool", dst, dst, g_ap, ALU.mult, [dstk, gkey], [dstk])
                tt("pool", dst, dst, b_ap, ALU.add, [dstk, bkey], [dstk])

            def transpose_to(src, srck, dstT, col0, dstk):
                for half in range(2):
                    b, bk = bank("M2", [4, 5, 6, 7])
                    for j in range(4):
                        c = half * 4 + j
                        tr(ps[b][:, j * 128:(j + 1) * 128], src[:, c * 128:(c + 1) * 128], [srck], [bk])
                    cp("act", dstT[:, half * 4:half * 4 + 4, col0:col0 + 128],
                       ps[b].rearrange("p (a b) -> p a b", a=4), [bk], [dstk])

            if S2 <= 1:
                return
            for tile in range(NS):
                mxk = "MX"
                dma("sp", MX, mixT_v[:, :, tile * TS:(tile + 1) * TS],
                    [("mixT", g, h, tile) for g in "ABCD" for h in range(4)], [mxk])
                g1 = load_vec("ln1_g")
                b1 = load_vec("ln1_b")
                for c in range(8):
                    sq = SQ2[c % 2]
                    sqk = ("SQ2", c % 2)
                    tt("pool", sq, MX[:, c, :], MX[:, c, :], ALU.mult, [mxk], [sqk])
                    for j in range(4):
                        mm(ps[4 + j][:, 0:4], sq[:, j * 128:(j + 1) * 128], IND[:, c, :], c == 0, c == 7,
                           [sqk, "const"], [("ps", 4 + j)])
                for j in range(4):
                    ts("dve", RS[:, j, :], ps[4 + j][:, 0:4], 1.0 / 256.0, ALU.mult, [("ps", 4 + j)], ["RS"],
                       s2=1e-6, op1=ALU.add)
                rsqrt(RS.rearrange("p a b -> p (a b)"), RS.rearrange("p a b -> p (a b)"), 0.0, ["RS"], ["RS"])
                if S2 <= 2:
                    continue
                for j in range(4):
                    blk = tile * 4 + j
                    dma("sp", XB, src_x[blk * 128:(blk + 1) * 128, :], [("xs", blk)], ["XB"])
                    for half in range(2):
                        ybs = []
                        for g in range(4):
                            b, bk = bank("Y", [0, 1, 2, 3])
                            ybs.append((b, bk))
                            for c2 in range(2):
                                c = 2 * g + c2
                                mm(ps[b], MX[:, c, j * 128:(j + 1) * 128], WOS[:, c, half * 512:(half + 1) * 512],
                                   c2 == 0, c2 == 1, [mxk, "WOS"], [bk])
                        acc = ACCY[:, half * 512:(half + 1) * 512]
                        ts("dve", acc, ps[ybs[0][0]], RS[:, j, 0:1], ALU.mult, [ybs[0][1], "RS"], ["ACCY"])
                        for g in range(1, 4):
                            stt(acc, ps[ybs[g][0]], RS[:, j, g:g + 1], acc, ALU.mult, ALU.add, [ybs[g][1], "RS", "ACCY"], ["ACCY"])
                    stt(U, XB, ALPHA, ACCY, ALU.mult, ALU.add, ["XB", "ACCY"], ["U"])
                    layernorm(U, "U", X1[:, j, :], ("X1", j), g1, b1)
                    transpose_to(X1[:, j, :], ("X1", j), X1T, j * 128, "X1T")
                if S2 <= 3:
                    continue
                g2 = load_vec("ln2_g")
                b2 = load_vec("ln2_b")
                for sc in range(11):
                    wi = sc % 2
                    dma("pool", WG[wi], wg_l[:, :, sc * 256:(sc + 1) * 256], [], [("WG", wi)])
                    dma("pool", WU[wi], wu_l[:, :, sc * 256:(sc + 1) * 256], [], [("WU", wi)])
                    for j2 in range(2):
                        fc = sc * 2 + j2
                        bg, bkg = bank("G", [0, 1])
                        bu, bku = bank("Uu", [2, 3])
                        for k in range(8):
                            mm(ps[bg], WG[wi][:, k, j2 * 128:(j2 + 1) * 128], X1T[:, k, :], k == 0, k == 7,
                               [("WG", wi), "X1T"], [bkg])
                        for k in range(8):
                            mm(ps[bu], WU[wi][:, k, j2 * 128:(j2 + 1) * 128], X1T[:, k, :], k == 0, k == 7,
                               [("WU", wi), "X1T"], [bku])
                        sg = SG[fc % 2]
                        sgk = ("SG", fc % 2)
                        act(sg, ps[bg], AF.Silu, [bkg], [sgk])
                        tt("dve", AT[:, fc, :], sg, ps[bu], ALU.mult, [sgk, bku], [("AT", fc)])
                if S2 <= 4:
                    continue
                for half in range(2):
                    fb = [4, 5, 6, 7]
                    for sc in range(11):
                        wi = (half * 11 + sc) % 2
                        dma("pool", WD[wi], wd_l[:, sc * 2:sc * 2 + 2, half * 512:(half + 1) * 512], [], [("WD", wi)])
                        for j2 in range(2):
                            fc = sc * 2 + j2
                            for j in range(4):
                                mm(ps[fb[j]], AT[:, fc, j * 128:(j + 1) * 128], WD[wi][:, j2, :], fc == 0, fc == NFC - 1,
                                   [("AT", fc), ("WD", wi)], [("ps", fb[j])])
                    for j in range(4):
                        stt(X1[:, j, half * 512:(half + 1) * 512], X1[:, j, half * 512:(half + 1) * 512], ALPHA,
                            ps[fb[j]], ALU.mult, ALU.add, [("ps", fb[j]), ("X1", j)], [("X1", j)])
                for j in range(4):
                    blk = tile * 4 + j
                    layernorm(X1[:, j, :], ("X1", j), U, "U", g2, b2)
                    o = dma("sp", dst_x[blk * 128:(blk + 1) * 128, :], U, ["U"], [("xs", blk)])
                    if last:
                        all_dma_out.append(o)
                    else:
                        transpose_to(U, "U", X1T, j * 128, "X1T")
                if not last:
                    dma("sp", xtm_f[tile].bitcast(BF16).rearrange("(c p) t -> p c t", p=128), X1T, ["X1T"], [("xtm", tile)])
                    P.cc((lambda jj: (lambda e: e.collective_compute("AllGather", ALU.bypass, replica_groups=rgroups,
                                                                     ins=[xtm_f[jj]], outs=[xta_f[jj]])))(tile),
                         [("xtm", tile)], [("xta", tile)])
                    xta_v = xta_f[tile].bitcast(BF16).rearrange("(r k p) t -> p k r t", r=2, k=8, p=128)
                    for r_ in range(2):
                        dma("sp", XT[:, :, (r_ * 4 + tile) * TS:(r_ * 4 + tile + 1) * TS], xta_v[:, :, r_, :],
                            [("xta", tile)], [("XT", r_ * 4 + tile)])

        build_xt0()
        for l in range(n_layers):
            P.barrier()
            phase1(l)
            P.barrier()
            if phase2:
                phase2_(l, l == n_layers - 1)
        if debug and not phase2:
            pass
        P.barrier()
        P.emit()
    return nc


def kernel(**inputs):
    x = np.ascontiguousarray(np.asarray(inputs["x"], dtype=np.float32))
    nc = build_program()
    base = {nm: np.ascontiguousarray(np.asarray(inputs[nm], dtype=np.float32)) for nm, _ in WNAMES}
    cst = [make_consts(0), make_consts(1)]
    in_maps = []
    for core in range(8):
        bq, r = core // 2, core % 2
        xt = x[bq].reshape(8, 512, D)
        m = dict(base)
        m.update(cst[r])
        m["x"] = np.ascontiguousarray(xt[[nat_tile(p) for p in range(8)]]).reshape(T, D)
        m["xq"] = np.ascontiguousarray(xt[[2 * j + r for j in range(4)]]).reshape(TQ, D)
        in_maps.append(m)
    res = run_bass_kernel_spmd(nc, in_maps, core_ids=list(range(8)))
    out = np.zeros((4, 8, 512, D), np.float32)
    for core in range(8):
        bq, r = core // 2, core % 2
        o = np.asarray(res.results[core]["out"], dtype=np.float32).reshape(4, 512, D)
        for j in range(4):
            out[bq, 2 * j + r] = o[j]
    return out.reshape(4, T, D)
```

```python
import numpy as np
import concourse.bass as bass
import concourse.mybir as mybir
from concourse.bass_utils import run_bass_kernel_spmd
from contextlib import ExitStack

F32 = mybir.dt.float32
BF16 = mybir.dt.bfloat16
AF = mybir.ActivationFunctionType
ALU = mybir.AluOpType
AX = mybir.AxisListType

ENG = ("pe", "act", "dve", "pool", "sp")
NDMASEM = 8


class Op:
    __slots__ = ("eng", "fn", "deps", "is_dma", "signal", "cnt", "semi", "val", "waits", "gid", "is_cc")

    def __init__(self, eng, fn, is_dma):
        self.eng = eng
        self.fn = fn
        self.deps = []
        self.is_dma = is_dma
        self.signal = False
        self.cnt = 0
        self.semi = -1
        self.val = 0
        self.waits = []
        self.is_cc = False


class Prog:
    def __init__(self, nc, stack):
        self.nc = nc
        self.stack = stack
        self.all = []
        self.res = {}
        self.ntens = 0
        self._bar_from = 0

    def sb(self, shape, dt, name=None):
        self.ntens += 1
        return self.stack.enter_context(self.nc.sbuf_tensor("s_" + (name or f"sb{self.ntens}"), list(shape), dt))

    def psum(self, shape, dt, name=None):
        self.ntens += 1
        return self.stack.enter_context(self.nc.psum_tensor(name or f"ps{self.ntens}", list(shape), dt))

    def op(self, eng, fn, reads=(), writes=(), dma=False):
        o = Op(eng, fn, dma)
        deps = o.deps
        res = self.res
        for r in reads:
            st = res.get(r)
            if st is None:
                st = res[r] = [None, []]
            if st[0] is not None:
                deps.append(st[0])
        for w in writes:
            st = res.get(w)
            if st is None:
                st = res[w] = [None, []]
            if st[0] is not None:
                deps.append(st[0])
            deps.extend(st[1])
        for r in reads:
            res[r][1].append(o)
        for w in writes:
            st = res[w]
            st[0] = o
            st[1] = []
        self.all.append(o)
        return o

    def dma(self, eng, out, in_, reads=(), writes=(), **kw):
        return self.op(eng, lambda e: e.dma_start(out=out, in_=in_, **kw), reads, writes, dma=True)

    def cc(self, fn, reads=(), writes=()):
        o = self.op("pool", fn, reads, writes, dma=True)
        o.is_cc = True
        return o

    def wait_all(self, eng, ops):
        o = Op(eng, None, False)
        o.deps = list(ops)
        self.all.append(o)
        return o


    def barrier(self):
        deps = []
        last = {}
        for o in self.all[self._bar_from:]:
            if o.fn is None:
                continue
            if o.is_dma:
                deps.append(o)
            else:
                last[o.eng] = o
        deps.extend(last.values())
        self._bar_from = len(self.all)
        for e in ENG:
            w = Op(e, None, False)
            w.deps = list(deps)
            self.all.append(w)

    def finalize(self):
        nc = self.nc
        for o in self.all:
            nd = []
            seen = set()
            for d in o.deps:
                if id(d) in seen:
                    continue
                seen.add(id(d))
                if d is o:
                    continue
                if not d.is_dma and d.eng == "pe" and o.eng == "pe" and not o.is_dma:
                    continue
                nd.append(d)
                if not d.is_dma:
                    d.signal = True
            o.deps = nd
        cnt = {e: 0 for e in ENG}
        for o in self.all:
            if o.is_dma or o.fn is None:
                continue
            if o.signal:
                cnt[o.eng] += 1
            o.cnt = cnt[o.eng]
        known = {e: {} for e in ENG}
        pool_next = {e: 0 for e in ENG}
        pool_last = {e: [0] * NDMASEM for e in ENG}
        self.nwaits = 0
        ncc = 0
        for o in self.all:
            E = o.eng
            kn = known[E]
            need = {}
            if o.is_cc:
                o.semi = ("cc", ncc)
                o.val = 1
                ncc += 1
            elif o.is_dma:
                s = pool_next[E]
                pool_next[E] = (s + 1) % NDMASEM
                prev = pool_last[E][s]
                key = ("d", E, s)
                if prev > 0 and kn.get(key, 0) < prev:
                    need[key] = prev
                o.semi = (E, s)
                o.val = prev + 16
                pool_last[E][s] = o.val
            for d in o.deps:
                if d.is_dma:
                    key = ("d",) + d.semi
                    v = d.val
                else:
                    key = ("c", d.eng)
                    v = d.cnt
                if kn.get(key, 0) < v and need.get(key, 0) < v:
                    need[key] = v
            for key, v in need.items():
                kn[key] = v
            o.waits = list(need.items())
            self.nwaits += len(need)
        st = self.stack
        self.csem = {e: st.enter_context(nc.semaphore(f"c_{e}")) for e in ENG}
        self.dsem = {(e, s): st.enter_context(nc.semaphore(f"d_{e}{s}")) for e in ("sp", "pool", "act") for s in range(NDMASEM)}
        self.ccsem = [st.enter_context(nc.semaphore(f"cc{i}")) for i in range(ncc)]
        self.byeng = {e: [o for o in self.all if o.eng == e] for e in ENG}

    def _sem(self, key):
        if key[0] == "c":
            return self.csem[key[1]]
        if key[1] == "cc":
            return self.ccsem[key[2]]
        return self.dsem[(key[1], key[2])]

    def emit_engine(self, E, e):
        for o in self.byeng[E]:
            for key, v in o.waits:
                e.wait_ge(self._sem(key), v)
            if o.fn is None:
                continue
            ins = o.fn(e)
            if o.is_cc:
                ins.then_inc(self.ccsem[o.semi[1]])
            elif o.is_dma:
                ins.then_inc(self.dsem[o.semi], 16)
            elif o.signal:
                ins.then_inc(self.csem[E], 1)

    def emit(self):
        nc = self.nc
        self.finalize()
        with nc.Block() as block:
            @block.tensor
            def _(e):
                self.emit_engine("pe", e)

            @block.scalar
            def _(e):
                self.emit_engine("act", e)

            @block.vector
            def _(e):
                self.emit_engine("dve", e)

            @block.gpsimd
            def _(e):
                self.emit_engine("pool", e)

            @block.sync
            def _(e):
                self.emit_engine("sp", e)

D = 1024
T = 4096
NB = 32
NT = 8
TS = 512
DFF = 2816
NFC = 22
INW = 2468
NL = 4
ALPHA = float((2.0 * NL) ** 0.25)
GBASE = {"A": 0, "B": 772, "C": 1188, "D": 1956}
GW = {"A": 772, "B": 416, "C": 768, "D": 512}
MASKV = -30000.0
NS = 4
TQ = NS * 512


def nat_tile(p):
    return 2 * (p % 4) + p // 4


def pos_tile(nt):
    return (nt % 2) * 4 + nt // 2


def pb_nat(n):
    return pos_tile(n // 4) * 4 + n % 4
STAGE = 99
KVF = 0
KVT = 8
VENG = 'dve'
S2 = 99
AMODE = 0
P2T = 8


def make_consts(rank):
    c = {}
    c["identf"] = np.eye(128, dtype=np.float32)
    p = np.arange(128)[:, None]
    xx = np.arange(896)[None, :]
    ms = np.where((xx - 384) < p, MASKV, 0.0)
    mx = np.where((xx - 384) <= p, MASKV, 0.0)
    full = np.full((128, 896), MASKV)
    zero = np.zeros((128, 896))
    mb = np.zeros((2, 2, 128, 896), np.float32)
    for ti, st in enumerate((ms, mx)):
        if rank == 0:
            mb[ti, 0] = st
            mb[ti, 1] = full
        else:
            mb[ti, 0] = zero
            mb[ti, 1] = st
    c["mbase"] = mb
    sel = np.zeros((128, 2), np.float32)
    sel[:, rank] = 1.0
    c["sel"] = sel
    s = np.arange(128)[:, None]
    t = np.arange(128)[None, :]
    tri = np.zeros((4, 128, 128), np.float32)
    tri[0] = (s <= t)
    tri[1] = 1.0
    tri[2] = -(s > t).astype(np.float32)
    tri[3] = -(s <= t).astype(np.float32)
    c["tri"] = tri
    pos = np.arange(T, dtype=np.float32)
    inv = (np.float32(10000.0) ** (-np.arange(0, 32, 2, dtype=np.float32) / np.float32(32))).astype(np.float32)
    ang = (pos[:, None] * inv[None, :]).astype(np.float32)
    cs = np.cos(ang).astype(np.float32).T
    sn = np.sin(ang).astype(np.float32).T
    rope = np.zeros((2, 32, T), np.float32)
    rope[0, :16] = cs
    rope[0, 16:] = cs
    rope[1, :16] = -sn
    rope[1, 16:] = sn
    r8 = rope.reshape(2, 32, 8, 512)
    c["rope"] = np.ascontiguousarray(r8[:, :, [nat_tile(pp_) for pp_ in range(8)], :]).reshape(2, 32, T)
    c["ropeq"] = np.ascontiguousarray(r8[:, :, [2 * j_ + rank for j_ in range(4)], :]).reshape(2, 32, TQ)
    bd = np.zeros((4, 128, 256), np.float32)
    pp = np.arange(128)[:, None].astype(np.float32)
    cc = np.arange(128)[None, :].astype(np.float32)
    for h in range(4):
        m = np.float32(2.0 ** (-8.0 * (h + 1) / 4.0))
        d1 = 128.0 + cc - pp
        bd[h, :, 0:128] = np.where(cc < pp, -m * d1, MASKV)
        d2 = cc - pp
        bd[h, :, 128:256] = np.where(cc >= pp, -m * d2, MASKV)
    c["biasd"] = bd
    ind = np.zeros((128, 8, 4), np.float32)
    for ch in range(8):
        ind[:, ch, ch // 2] = 1.0
    c["ind"] = ind
    return c


WNAMES = [("w_in", [NL, D, INW]), ("fox_b_f", [NL, 4]), ("mla_g_q", [NL, 256]), ("mla_g_kv", [NL, 128]),
          ("mla_w_uq", [NL, 256, 384]), ("mla_w_ukv", [NL, 128, 512]), ("swa_sinks", [NL, 4]),
          ("mix_g", [NL, D]), ("w_o", [NL, D, D]), ("ln1_g", [NL, D]), ("ln1_b", [NL, D]),
          ("w_gate", [NL, D, DFF]), ("w_up", [NL, D, DFF]), ("w_down", [NL, DFF, D]),
          ("ln2_g", [NL, D]), ("ln2_b", [NL, D])]
CNAMES = [("identf", [128, 128]), ("mbase", [2, 2, 128, 896]), ("tri", [4, 128, 128]), ("rope", [2, 32, T]),
          ("ropeq", [2, 32, TQ]), ("sel", [128, 2]), ("biasd", [4, 128, 256]), ("ind", [128, 8, 4])]


class Arena:
    def __init__(self, P, nbytes, name):
        self.t = P.sb([128, nbytes // 2], BF16, name)
        self.n = nbytes
        self.off = 0

    def reset(self):
        self.off = 0

    def get(self, shape, dt):
        esz = 4 if dt == F32 else 2
        ne = int(np.prod(shape[1:]))
        n = (ne * esz + 63) // 64 * 64
        a = self.t[:, self.off // 2:(self.off + n) // 2]
        self.off += n
        assert self.off <= self.n, ("arena overflow", self.off, self.n)
        if dt == F32:
            a = a.bitcast(F32)
        a = a[:, 0:ne]
        if len(shape) == 3:
            a = a.rearrange("p (a b) -> p a b", a=shape[1])
        elif len(shape) == 4:
            a = a.rearrange("p (a b c) -> p a b c", a=shape[1], b=shape[2])
        return a


def build_program(n_layers=NL, groups="ABCD", debug=False, phase2=True, n_cores=8):
    nc = bass.Bass("TRN2", target_bir_lowering=False)
    dd = {}
    dd["x"] = nc.dram_tensor("x", [T, D], F32, kind="ExternalInput").ap()
    for nm, shp in WNAMES + CNAMES:
        dd[nm] = nc.dram_tensor(nm, list(shp), F32, kind="ExternalInput").ap()
    dd["xq"] = nc.dram_tensor("xq", [TQ, D], F32, kind="ExternalInput").ap()
    out_d = nc.dram_tensor("out", [TQ, D], F32, kind="ExternalOutput").ap()
    xs_d = nc.dram_tensor("xs", [TQ, D], F32, kind="Internal").ap()
    mixT_d = nc.dram_tensor("mixT", [D, TQ], BF16, kind="ExternalOutput" if debug else "Internal").ap()
    xtm_f = [nc.dram_tensor(f"xtm{j_}", [D, TS // 2], F32, kind="Internal").ap() for j_ in range(NS)]
    xta_f = [nc.dram_tensor(f"xta{j_}", [2 * D, TS // 2], F32, kind="Internal").ap() for j_ in range(NS)]
    rgroups = [[2 * g_, 2 * g_ + 1] for g_ in range(n_cores // 2)]

    with ExitStack() as st:
        P = Prog(nc, st)

        def mm(out, lhsT, rhs, start, stop, r, w, **kw):
            P.op("pe", lambda e: e.matmul(out, lhsT=lhsT, rhs=rhs, start=start, stop=stop, **kw), r, w)

        def tr(out, in_, r, w):
            P.op("pe", lambda e: e.transpose(out=out, in_=in_, identity=identf), list(r) + ["const"], w)

        def act(out, in_, func, r, w, scale=None, bias=None):
            kw = {}
            if scale is not None:
                kw["scale"] = scale
            if bias is not None:
                kw["bias"] = bias
            P.op("act", lambda e: e.activation(out=out, in_=in_, func=func, **kw), r, w)

        def tt(eng, out, in0, in1, op, r, w):
            P.op(eng, lambda e: e.tensor_tensor(out=out, in0=in0, in1=in1, op=op), r, w)

        def ts(eng, out, in0, s1, op0, r, w, s2=None, op1=None):
            if op1 is None:
                P.op(eng, lambda e: e.tensor_scalar(out=out, in0=in0, scalar1=s1, scalar2=None, op0=op0), r, w)
            else:
                P.op(eng, lambda e: e.tensor_scalar(out=out, in0=in0, scalar1=s1, scalar2=s2, op0=op0, op1=op1), r, w)

        def stt(out, in0, scalar, in1, op0, op1, r, w):
            P.op("dve", lambda e: e.scalar_tensor_tensor(out=out, in0=in0, scalar=scalar, in1=in1, op0=op0, op1=op1), r, w)

        def cp(eng, out, in_, r, w):
            if eng == "act":
                P.op("act", lambda e: e.activation(out=out, in_=in_, func=AF.Copy), r, w)
            else:
                P.op(eng, lambda e: e.tensor_copy(out=out, in_=in_), r, w)

        def rsqrt(out, in_, eps, r, w):
            act(out, in_, AF.Ln, r, w, bias=eps)
            act(out, out, AF.Exp, list(w), w, scale=-0.5)

        def recip(out, in_, r, w):
            P.op("dve", lambda e: e.reciprocal(out=out, in_=in_), r, w)

        def dma(eng, out, in_, r, w, **kw):
            return P.dma(eng, out, in_, r, w, **kw)

        ps = [P.psum([128, 512], F32, f"bank{i}")[:] for i in range(8)]
        rot = {}

        def bank(pool, lst):
            i = rot.get(pool, 0)
            rot[pool] = i + 1
            b = lst[i % len(lst)]
            return b, ("ps", b)

        XT = P.sb([128, 8, T], BF16, "XT")[:]
        AR1 = Arena(P, 49152, "AR1")
        AR2 = Arena(P, 80 * 1024, "AR2")
        identf = P.sb([128, 128], F32, "identf")[:]
        identb = P.sb([128, 128], BF16, "identb")[:]
        onesb = P.sb([128, 128], BF16, "onesb")[:]
        MB = P.sb([128, 4, 896], BF16, "MB")[:]
        TRI = P.sb([128, 4, 128], F32, "TRI")[:]
        TRIB = P.sb([128, 2, 128], BF16, "TRIB")[:]
        BIASD = P.sb([128, 4, 256], F32, "BIASD")[:]
        IND = P.sb([128, 8, 4], BF16, "IND")[:]
        SMALL = P.sb([128, 64], F32, "SMALL")[:]
        BF_ = SMALL[:, 0:4]
        ES = SMALL[:, 4:8]
        GQ = SMALL[:, 8:10]
        GKV = SMALL[:, 10:11]
        MG = SMALL[:, 16:24]
        SEL = SMALL[:, 24:26]
        CARRYX = P.sb([128, NB, 4], F32, "CARRYX")[:]

        dma("sp", identf, dd["identf"], [], ["const"])
        dma("pool", identb, dd["identf"], [], ["const"])
        dma("pool", onesb, dd["tri"][1], [], ["const"])
        dma("pool", MB, dd["mbase"].rearrange("t w p c -> p (t w) c"), [], ["const"])
        dma("sp", SEL, dd["sel"], [], ["const"])
        dma("sp", TRI, dd["tri"].rearrange("i p c -> p i c"), [], ["const"])
        dma("pool", TRIB, dd["tri"][2:4].rearrange("i p c -> p i c"), [], ["const"])
        dma("sp", BIASD, dd["biasd"].rearrange("h p c -> p h c"), [], ["const"])
        dma("pool", IND, dd["ind"], [], ["const"])
        P.op("dve", lambda e: e.memset(CARRYX[:, 0, :], 0.0), [], ["carry0"])

        all_dma_out = []

        def build_xt0():
            AR2.reset()
            XB = [AR2.get([128, D], F32) for _ in range(2)]
            for blk in range(NB):
                xb = XB[blk % 2]
                k = ("XB0", blk % 2)
                dma("sp", xb, dd["x"][blk * 128:(blk + 1) * 128, :], [], [k])
                for half in range(2):
                    b, bk = bank("M", list(range(8)))
                    for j in range(4):
                        c = half * 4 + j
                        tr(ps[b][:, j * 128:(j + 1) * 128], xb[:, c * 128:(c + 1) * 128], [k], [bk])
                    cp("act" if half else "dve", XT[:, half * 4:half * 4 + 4, blk * 128:(blk + 1) * 128],
                       ps[b].rearrange("p (a b) -> p a b", a=4), [bk], [("XT", blk // 4)])

        def phase1(l):
            AR1.reset()
            AR2.reset()
            KT = [AR1.get([128, T], BF16) for _ in range(4)]
            V = AR1.get([128, NB, 256], BF16)
            QT = [[AR2.get([128, TS], BF16) for _ in range(2)] for _ in range(4)]
            WIN = AR2.get([128, 8, 772], BF16)
            PT = [AR2.get([128, TS], BF16) for _ in range(3)]
            CT = [[AR2.get([128, TS], F32) for _ in range(2)] for _ in range(2)]
            RC = AR2.get([128, TS], F32)
            MO = [AR2.get([128, TS], BF16) for _ in range(2)]
            GATE = AR2.get([128, NB, 4], F32)
            TOTs = AR2.get([128, NB, 4], F32)
            CNEG = AR2.get([128, NB, 4], F32)
            BT = [AR2.get([128, NB, 4], F32) for _ in range(2)]
            WUQ = AR2.get([128, 2, 384], BF16)
            WUQS = AR2.get([128, 2, 384], BF16)
            WUKV = AR2.get([128, 512], BF16)
            WKRS = AR2.get([128, 8, 96], BF16)
            SQ = AR2.get([128, 2, TS], BF16)
            RSB = AR2.get([128, TS], F32)
            CQN = AR2.get([128, 2, TS], BF16)
            CKVN = AR2.get([128, TS], BF16)
            ROPE = AR2.get([128, 2, TS], F32)
            RT1 = AR2.get([128, TS], F32)
            RT2 = AR2.get([128, TS], F32)
            SBD = [AR2.get([128, 256], F32) for _ in range(2)]
            PD = [AR2.get([128, 256], BF16) for _ in range(2)]
            XTQ = AR2.get([128, 8, TS], BF16)
            SPB = [AR2.get([128, TS], BF16) for _ in range(2)]
            MO2 = AR2.get([128, TS], BF16)

            def make_xtq(j):
                ts("dve", XTQ, XT[:, :, j * TS:(j + 1) * TS], SEL[:, 0:1], ALU.mult, [("XT", j), "const"], ["XTQ"])
                stt(XTQ, XT[:, :, (4 + j) * TS:(5 + j) * TS], SEL[:, 1:2], XTQ, ALU.mult, ALU.add,
                    [("XT", 4 + j), "const", "XTQ"], ["XTQ"])

            w_in_l = dd["w_in"][l].rearrange("(k p) n -> p k n", p=128)
            ALLB = list(range(8))
            mo_cnt = [0]

            def load_win(g):
                dma("pool", WIN[:, :, 0:GW[g]], w_in_l[:, :, GBASE[g]:GBASE[g] + GW[g]], [], ["WIN"])

            def xt_keys(tile):
                return [("XT", tile)]

            def proj_fm(dst, dk, col0, tile, eng, wkeys, dkeys, pool="M", banks=None, lw=None, own=False):
                b, bk = bank(pool, banks or [7])
                W = WIN if lw is None else lw
                for k in range(8):
                    if own:
                        mm(ps[b][0:dk, :], W[:, k, col0:col0 + dk], XTQ[:, k, :], k == 0, k == 7, ["XTQ"] + wkeys, [bk])
                    else:
                        mm(ps[b][0:dk, :], W[:, k, col0:col0 + dk], XT[:, k, tile * TS:(tile + 1) * TS], k == 0, k == 7,
                           xt_keys(tile) + wkeys, [bk])
                if dst is not None:
                    cp(eng, dst, ps[b][0:dk, :], [bk], dkeys)
                return b, bk

            def store_mix(g, h, qt, src_bank, bk, rc_ap, extra_r):
                i = mo_cnt[0] % 2
                mo_cnt[0] += 1
                mo = MO[i]
                mk = ("MO", i)
                if rc_ap is None:
                    cp("dve", mo[0:64, :], ps[src_bank][0:64, :], [bk], [mk])
                else:
                    tt("dve", mo[0:64, :], ps[src_bank][0:64, :], rc_ap, ALU.mult, [bk] + extra_r, [mk])
                row = "ABCD".index(g) * 256 + h * 64
                o = dma("sp", mixT_d[row:row + 64, qt * TS:(qt + 1) * TS], mo[0:64, :], [mk], [("mixT", g, h, qt)])

            def softmax_unit(g, h, qt, dk, scale, qap, qkeys, bias_fn, bias_keys, u):
                kbl = [4 * p_ + i_ for p_ in list(range(qt + 1)) + list(range(4, 4 + qt + 1)) for i_ in range(4)]
                nkb = len(kbl)
                ob = [3, 4][u % 2]
                db = [5, 6][u % 2]
                obk, dbk = ("ps", ob), ("ps", db)
                LA = 2
                sb_of = {}
                for step in range(nkb + LA):
                    if step < nkb:
                        kb = kbl[step]
                        b, bk = bank("S", [0, 1, 2])
                        sb_of[step] = (b, bk)
                        p_, i = kb // 4, kb % 4
                        wh = 0 if p_ == qt else (1 if p_ == 4 + qt else -1)
                        mm(ps[b], KT[h][0:dk, kb * 128:(kb + 1) * 128], qap, True, wh < 0,
                           [("KT", h, kb // 4)] + qkeys, [bk])
                        if wh >= 0:
                            mm(ps[b], identb, MB[:, wh, 384 - 128 * i:384 - 128 * i + 512], False, True, ["const"], [bk])
                    s2 = step - LA
                    if 0 <= s2 < nkb:
                        kb = kbl[s2]
                        b, bk = sb_of.pop(s2)
                        pi = s2 % 3
                        pt, pk = PT[pi], ("PT", pi)
                        act(pt, ps[b], AF.Exp, [bk] + bias_keys, [pk], scale=scale,
                            bias=(bias_fn(kb) if bias_fn else None))
                        mm(ps[ob][0:64, :], V[:, kb, h * 64:(h + 1) * 64], pt, s2 == 0, s2 == nkb - 1,
                           [pk, ("V", kb // 4)], [obk])
                        mm(ps[db][0:64, :], onesb[:, 0:64], pt, s2 == 0, s2 == nkb - 1, [pk, "const"], [dbk])
                recip(RC[0:64, :], ps[db][0:64, :], [dbk], ["RC"])
                store_mix(g, h, qt, ob, obk, RC[0:64, :], ["RC"])

            def sb_units(hs, qt, qaps, qkeys):
                order = [pos_tile(nt_) * 4 + i_ for nt_ in reversed(range(2 * qt + 2)) for i_ in reversed(range(4))]
                nkb = len(order)
                scale = 0.125
                zb = [[0, 1], [4, 5]]
                accb = [2, 6]
                obb = [3, 7]
                LA = 1
                z_of = {}
                for step in range(nkb + LA):
                    for si, h in enumerate(hs):
                        if step < nkb:
                            kb = order[step]
                            b = zb[si][step % 2]
                            bk = ("ps", b)
                            z_of[(si, step)] = (b, bk)
                            p_, i = kb // 4, kb % 4
                            wh = 0 if p_ == qt else (1 if p_ == 4 + qt else -1)
                            mm(ps[b], KT[h][0:64, kb * 128:(kb + 1) * 128], qaps[si], True, wh < 0,
                               [("KT", h, kb // 4)] + qkeys[si], [bk])
                            if wh >= 0:
                                mm(ps[b], identb, MB[:, 2 + wh, 384 - 128 * i:384 - 128 * i + 512], False, True,
                                   ["const"], [bk])
                    for si, h in enumerate(hs):
                        s2 = step - LA
                        if 0 <= s2 < nkb:
                            kb = order[s2]
                            b, bk = z_of.pop((si, s2))
                            SP, T1 = CT[si]
                            spk, t1k = ("SP", si), ("T1", si)
                            ab, abk = accb[si], ("ps", accb[si])
                            ob, obk = obb[si], ("ps", obb[si])
                            spb, spbk = SPB[si], ("SPB", si)
                            act(SP, ps[b], AF.Exp, [bk], [spk], scale=scale)
                            act(spb, SP, AF.Ln, [spk], [spbk], bias=1.0)
                            mm(ps[ab], TRIB[:, 0, :], spb, s2 == 0, True, [spbk, "const"], [abk], skip_group_check=True)
                            stt(T1, ps[b], scale, spb, ALU.mult, ALU.subtract, [bk, spbk], [t1k])
                            tt("dve", T1, ps[ab], T1, ALU.add, [abk, t1k], [t1k])
                            if s2 < nkb - 1:
                                mm(ps[ab], TRIB[:, 1, :], spb, False, True, [spbk, "const"], [abk], skip_group_check=True)
                            pi = (s2 * 2 + si) % 3
                            pt, pk = PT[pi], ("PT", pi)
                            act(pt, T1, AF.Exp, [t1k], [pk])
                            mm(ps[ob][0:64, :], V[:, kb, h * 64:(h + 1) * 64], pt, s2 == 0, s2 == nkb - 1,
                               [pk, ("V", kb // 4)], [obk])
                for si, h in enumerate(hs):
                    store_mix("C", h, qt, obb[si], ("ps", obb[si]), None, [])

            def kv_pass(g, kcols, vcol0, nv, gates):
                KVB = list(range(7)) if gates else ALLB
                for tile in range(NT):
                    for h, c0 in enumerate(kcols):
                        proj_fm(KT[h][0:64, tile * TS:(tile + 1) * TS], 64, c0, tile, "dve", ["WIN"],
                                [("KT", h, tile)], banks=KVB)
                    for j in range(4):
                        blk = tile * 4 + j
                        b, bk = bank("M", KVB)
                        n = nv
                        for k in range(8):
                            mm(ps[b][:, 0:n], XT[:, k, blk * 128:(blk + 1) * 128], WIN[:, k, vcol0:vcol0 + n],
                               k == 0, k == 7, xt_keys(tile) + ["WIN"], [bk])
                        cp("dve", V[:, blk, 0:nv], ps[b][:, 0:nv], [bk], [("V", tile)])
                        if gates:
                            for k in range(8):
                                mm(ps[7][:, blk * 4:(blk + 1) * 4], XT[:, k, blk * 128:(blk + 1) * 128],
                                   WIN[:, k, vcol0 + nv:vcol0 + nv + 4], k == 0, k == 7, xt_keys(tile) + ["WIN"],
                                   [("ps", 7)], skip_group_check=True)
                if gates:
                    cp("dve", GATE.rearrange("p a b -> p (a b)"), ps[7][:, 0:128], [("ps", 7)], ["GATE"])

            if "A" in groups:
                load_win("A")
                dma("sp", BF_, dd["fox_b_f"][l].partition_broadcast(128), [], ["BF"])
                kv_pass("A", [256, 320, 384, 448], 512, 256, AMODE != 1)
                if AMODE in (0, 3):
                    for h in range(4):
                        ts("dve", GATE[:, :, h], GATE[:, :, h], BF_[:, h:h + 1], ALU.add, ["GATE", "BF"], ["GATE"])
                    GF = GATE.rearrange("p a b -> p (a b)")
                    act(GF, GF, AF.Exp, ["GATE"], ["GATE"], scale=-1.0)
                    act(GF, GF, AF.Ln, ["GATE"], ["GATE"], bias=1.0)
                    if False:
                        pass
                    b1, bk1 = bank("M", ALLB)
                    mm(ps[b1][:, 0:128], TRI[:, 1, :], GF, True, True, ["GATE", "const"], [bk1])
                    b2, bk2 = bank("M", ALLB)
                    mm(ps[b2][:, 0:128], TRI[:, 0, :], GF, True, True, ["GATE", "const"], [bk2])
                    cp("dve", TOTs.rearrange("p a b -> p (a b)"), ps[b1][:, 0:128], [bk1], ["TOTs"])
                    if False:
                        pass
                    for n_ in range(NB - 1):
                        tt("dve", CARRYX[:, pb_nat(n_ + 1), :], CARRYX[:, pb_nat(n_), :], TOTs[:, pb_nat(n_), :], ALU.add,
                           ["TOTs", "carry0", "CARRYX"], ["CARRYX"])
                    tt("dve", CNEG, ps[b2][:, 0:128].rearrange("p (a b) -> p a b", a=NB), CARRYX, ALU.add,
                       [bk2, "CARRYX", "carry0"], ["CNEG"])
                    if False:
                        pass
                    u = 0
                    for qt in range(NS):
                        bt = BT[qt % 2]
                        btk = ("BT", qt % 2)
                        for h in range(4):
                            ts("dve", bt[:, :, h], CNEG[:, :, h], CARRYX[:, (4 + qt) * 4, h:h + 1], ALU.subtract,
                               ["CNEG", "CARRYX", "carry0"], [btk])
                        make_xtq(qt)
                        for h in range(4):
                            qb = u % 2
                            qk = ("QT", h, qb)
                            proj_fm(QT[h][qb][0:64, :], 64, h * 64, qt, "dve", ["WIN"], [qk], own=True)
                            softmax_unit("A", h, qt, 64, 0.125, QT[h][qb][0:64, :], [qk],
                                         (lambda kb, hh=h, bb=bt: bb[:, kb, hh:hh + 1]), [btk], u)
                            u += 1

            if "B" in groups:
                P.barrier()
                load_win("B")
                wb = GBASE["B"]
                uq = dd["mla_w_uq"][l].rearrange("(c p) n -> p c n", p=128)
                dma("pool", WUQ, uq, [], ["WUQ"])
                uq4 = uq.rearrange("p c (h x) -> p c h x", h=4)
                wq4 = WUQS.rearrange("p c (h x) -> p c h x", h=4)
                for c in range(2):
                    dma("pool", wq4[:, c, :, 0:64], uq4[:, c, :, 0:64], [], ["WUQS"])
                    dma("pool", wq4[:, c, :, 64:80], uq4[:, c, :, 80:96], [], ["WUQS"])
                    dma("pool", wq4[:, c, :, 80:96], uq4[:, c, :, 64:80], [], ["WUQS"])
                dma("pool", WUKV, dd["mla_w_ukv"][l], [], ["WUKV"])
                dma("pool", WKRS[:, :, 0:64], w_in_l[:, :, wb + 320:wb + 384], [], ["WKRS"])
                dma("pool", WKRS[:, :, 64:80], w_in_l[:, :, wb + 400:wb + 416], [], ["WKRS"])
                dma("pool", WKRS[:, :, 80:96], w_in_l[:, :, wb + 384:wb + 400], [], ["WKRS"])
                dma("sp", GQ, dd["mla_g_q"][l].rearrange("(c p) -> p c", p=128), [], ["GQ"], allow_slow_non_contiguous=True)
                dma("sp", GKV, dd["mla_g_kv"][l].rearrange("(c p) -> p c", p=128), [], ["GKV"], allow_slow_non_contiguous=True)
                ts("dve", GQ, GQ, 16.0, ALU.mult, ["GQ"], ["GQ"])
                ts("dve", GKV, GKV, float(np.sqrt(128.0)), ALU.mult, ["GKV"], ["GKV"])
                WUKV4 = WUKV.rearrange("p (h x) -> p h x", h=4)

                def load_rope(tile, own=False):
                    src_ = dd["ropeq"] if own else dd["rope"]
                    dma("sp", ROPE[64:96, 0, :], src_[0, :, tile * TS:(tile + 1) * TS], [], ["ROPE"])
                    dma("sp", ROPE[64:96, 1, :], src_[1, :, tile * TS:(tile + 1) * TS], [], ["ROPE"])

                def rope_apply(ba, bka, bb, bkb, dsts, dkeys):
                    tt("dve", RT1[64:96, :], ps[ba][64:96, :], ROPE[64:96, 0, :], ALU.mult, [bka, "ROPE"], ["RT1"])
                    tt("dve", RT2[64:96, :], ps[bb][64:96, :], ROPE[64:96, 1, :], ALU.mult, [bkb, "ROPE"], ["RT2"])
                    for dst, dk_ in zip(dsts, dkeys):
                        tt("dve", dst, RT1[64:96, :], RT2[64:96, :], ALU.add, ["RT1", "RT2"], [dk_])

                for tile in range(NT):
                    ba, bka = proj_fm(None, 128, 256, tile, None, ["WIN"], None, banks=ALLB)
                    act(SQ[:, 0, :], ps[ba], AF.Square, [bka], ["SQ"])
                    bs, bks = bank("M", ALLB)
                    mm(ps[bs], onesb, SQ[:, 0, :], True, True, ["SQ", "const"], [bks])
                    rsqrt(RSB, ps[bs], 128.0 * 1e-6, [bks], ["RSB"])
                    stt(CKVN, ps[ba], GKV[:, 0:1], RSB, ALU.mult, ALU.mult, [bka, "RSB", "GKV"], ["CKVN"])
                    for h in range(4):
                        b, bk = bank("M", ALLB)
                        mm(ps[b][0:64, :], WUKV[:, h * 128:h * 128 + 64], CKVN, True, True, ["CKVN", "WUKV"], [bk])
                        cp("dve", KT[h][0:64, tile * TS:(tile + 1) * TS], ps[b][0:64, :], [bk], [("KT", h, tile)])
                    for j in range(4):
                        blk = tile * 4 + j
                        b, bk = bank("M", ALLB)
                        mm(ps[b][:, 0:256].rearrange("p (h x) -> p h x", h=4), CKVN[:, j * 128:(j + 1) * 128],
                           WUKV4[:, :, 64:128], True, True, ["CKVN", "WUKV"], [bk])
                        cp("dve", V[:, blk, :], ps[b][:, 0:256], [bk], [("V", tile)])
                    bc, bkc = proj_fm(None, 96, 320, tile, None, ["WIN"], None, banks=ALLB)
                    bd_, bkd = proj_fm(None, 96, 0, tile, None, ["WKRS"], None, banks=ALLB, lw=WKRS)
                    load_rope(tile)
                    rope_apply(bc, bkc, bd_, bkd, [KT[h][64:96, tile * TS:(tile + 1) * TS] for h in range(4)],
                               [("KT", h, tile) for h in range(4)])
                u = 0
                sc_b = float(96.0 ** -0.5)
                for qt in range(NS):
                    make_xtq(qt)
                    bq = []
                    for c in range(2):
                        bq.append(proj_fm(None, 128, c * 128, qt, None, ["WIN"], None, banks=[7, 0, 1, 2], own=True))
                        act(SQ[:, c, :], ps[bq[c][0]], AF.Square, [bq[c][1]], ["SQ"])
                    bs, bks = bank("M", [7, 0, 1, 2])
                    for c in range(2):
                        mm(ps[bs], onesb, SQ[:, c, :], c == 0, c == 1, ["SQ", "const"], [bks])
                    rsqrt(RSB, ps[bs], 256.0 * 1e-6, [bks], ["RSB"])
                    for c in range(2):
                        stt(CQN[:, c, :], ps[bq[c][0]], GQ[:, c:c + 1], RSB, ALU.mult, ALU.mult,
                            [bq[c][1], "RSB", "GQ"], ["CQN"])
                    load_rope(qt, own=True)
                    for h in range(4):
                        qb = u % 2
                        qk = ("QT", h, qb)
                        be, bke = bank("M", [7, 0, 1, 2])
                        for c in range(2):
                            mm(ps[be][0:96, :], WUQ[:, c, h * 96:(h + 1) * 96], CQN[:, c, :], c == 0, c == 1,
                               ["CQN", "WUQ"], [bke])
                        bf_, bkf = bank("M", [7, 0, 1, 2])
                        for c in range(2):
                            mm(ps[bf_][0:96, :], WUQS[:, c, h * 96:(h + 1) * 96], CQN[:, c, :], c == 0, c == 1,
                               ["CQN", "WUQS"], [bkf])
                        cp("dve", QT[h][qb][0:64, :], ps[be][0:64, :], [bke], [qk])
                        rope_apply(be, bke, bf_, bkf, [QT[h][qb][64:96, :]], [qk])
                        softmax_unit("B", h, qt, 96, sc_b, QT[h][qb][0:96, :], [qk], None, [], u)
                        u += 1

            if "C" in groups:
                P.barrier()
                load_win("C")
                kv_pass("C", [256, 320, 384, 448], 512, 256, False)
                for qt in range(NS):
                    make_xtq(qt)
                    for pair in range(2):
                        hs = [2 * pair, 2 * pair + 1]
                        qaps, qkeys = [], []
                        for h in hs:
                            qb = qt % 2
                            qk = ("QT", h, qb)
                            proj_fm(QT[h][qb][0:64, :], 64, h * 64, qt, "dve", ["WIN"], [qk], banks=[0, 1, 4, 5], own=True)
                            qaps.append(QT[h][qb][0:64, :])
                            qkeys.append([qk])
                        sb_units(hs, qt, qaps, qkeys)

            if "D" in groups:
                P.barrier()
                load_win("D")
                dma("sp", ES, dd["swa_sinks"][l].partition_broadcast(128), [], ["ES"])
                act(ES, ES, AF.Exp, ["ES"], ["ES"])
                kv_pass("D", [256, 320], 384, 128, False)
                u = 0
                for js in range(NS):
                    for h in range(4):
                        kvh = h // 2
                        for half in range(2):
                            nt = 2 * js + half
                            pp = pos_tile(nt)
                            qb = u % 2
                            qk = ("QT", h, qb)
                            proj_fm(QT[h][qb][0:64, :], 64, h * 64, pp, "dve", ["WIN"], [qk])
                            ob = [3, 4][u % 2]
                            db = [5, 6][u % 2]
                            obk, dbk = ("ps", ob), ("ps", db)
                            for j in range(4):
                                n = 4 * nt + j
                                pn = pb_nat(n)
                                pm = pb_nat(n - 1) if n > 0 else 0
                                lo = 0 if n > 0 else 128
                                b, bk = bank("S", [0, 1, 2])
                                qa = QT[h][qb][0:64, j * 128:(j + 1) * 128]
                                if n > 0:
                                    mm(ps[b][:, 0:128], KT[kvh][0:64, pm * 128:(pm + 1) * 128], qa, True, True,
                                       [("KT", kvh, pm // 4), qk], [bk])
                                mm(ps[b][:, 128:256], KT[kvh][0:64, pn * 128:(pn + 1) * 128], qa, True, True,
                                   [("KT", kvh, pn // 4), qk], [bk])
                                si = (u * 4 + j) % 2
                                sbd, pd = SBD[si], PD[si]
                                stt(sbd[:, lo:256], ps[b][:, lo:256], 0.125, BIASD[:, h, lo:256], ALU.mult, ALU.add,
                                    [bk, "const"], [("SBD", si)])
                                act(pd[:, lo:256], sbd[:, lo:256], AF.Exp, [("SBD", si)], [("PD", si)])
                                oc = slice(j * 128, (j + 1) * 128)
                                if n > 0:
                                    mm(ps[ob][0:64, oc], V[:, pm, kvh * 64:(kvh + 1) * 64], pd[:, 0:128], True, False,
                                       [("PD", si), ("V", pm // 4)], [obk])
                                    mm(ps[db][0:64, oc], onesb[:, 0:64], pd[:, 0:128], True, False, [("PD", si), "const"], [dbk])
                                mm(ps[ob][0:64, oc], V[:, pn, kvh * 64:(kvh + 1) * 64], pd[:, 128:256], n == 0, True,
                                   [("PD", si), ("V", pn // 4)], [obk])
                                mm(ps[db][0:64, oc], onesb[:, 0:64], pd[:, 128:256], n == 0, True, [("PD", si), "const"], [dbk])
                            ts("dve", RC[0:64, :], ps[db][0:64, :], ES[0:64, h:h + 1], ALU.add, [dbk, "ES"], ["RC"])
                            recip(RC[0:64, :], RC[0:64, :], ["RC"], ["RC"])
                            tt("dve", MO[half][0:64, :], ps[ob][0:64, :], RC[0:64, :], ALU.mult, [obk, "RC"], [("MO", half)])
                            u += 1
                        ts("dve", MO2[0:64, :], MO[0][0:64, :], SEL[0:64, 0:1], ALU.mult, [("MO", 0), "const"], ["MO2"])
                        stt(MO2[0:64, :], MO[1][0:64, :], SEL[0:64, 1:2], MO2[0:64, :], ALU.mult, ALU.add,
                            [("MO", 1), "const", "MO2"], ["MO2"])
                        row = 3 * 256 + h * 64
                        dma("sp", mixT_d[row:row + 64, js * TS:(js + 1) * TS], MO2[0:64, :], ["MO2"], [("mixT", "D", h, js)])

        def phase2_(l, last):
            AR1.reset()
            AR2.reset()
            AT = AR1.get([128, NFC, TS], BF16)
            MX = AR1.get([128, 8, TS], BF16)
            X1 = AR1.get([128, 4, D], F32)
            WOS = AR2.get([128, 8, D], BF16)
            WST = AR2.get([128, D], F32)
            LNV = [AR2.get([128, D], F32) for _ in range(2)]
            WG = [AR2.get([128, 8, 256], BF16) for _ in range(2)]
            WU = [AR2.get([128, 8, 256], BF16) for _ in range(2)]
            WD = [AR2.get([128, 2, TS], BF16) for _ in range(2)]
            X1T = AR2.get([128, 8, TS], BF16)
            SQ2 = [AR2.get([128, TS], BF16) for _ in range(2)]
            ACCY = AR2.get([128, D], F32)
            U = AR2.get([128, D], F32)
            XB = AR2.get([128, D], F32)
            SG = [AR2.get([128, TS], F32) for _ in range(2)]
            RS = AR2.get([128, 4, 4], F32)
            STAT = AR2.get([128, 2, 6], F32)
            MV = AR2.get([128, 4], F32)

            dma("sp", MG, dd["mix_g"][l].rearrange("(c p) -> p c", p=128), [], ["MG"], allow_slow_non_contiguous=True)
            wo_l = dd["w_o"][l].rearrange("(c p) n -> p c n", p=128)
            for c in range(8):
                dma("sp", WST, wo_l[:, c, :], [], ["WST"])
                ts("dve", WOS[:, c, :], WST, MG[:, c:c + 1], ALU.mult, ["WST", "MG"], ["WOS"])
            wg_l = dd["w_gate"][l].rearrange("(k p) n -> p k n", p=128)
            wu_l = dd["w_up"][l].rearrange("(k p) n -> p k n", p=128)
            wd_l = dd["w_down"][l].rearrange("(c p) n -> p c n", p=128)
            mixT_v = mixT_d.rearrange("(c p) t -> p c t", p=128)
            src_x = dd["xq"] if l == 0 else xs_d
            dst_x = out_d if last else xs_d
            lnvi = [0]

            def load_vec(name):
                i = lnvi[0] % 2
                lnvi[0] += 1
                dma("sp", LNV[i], dd[name][l].partition_broadcast(128), [], [("LNV", i)])
                return LNV[i], ("LNV", i)

            def layernorm(src, srck, dst, dstk, gk, bk_):
                g_ap, gkey = gk
                b_ap, bkey = bk_
                for hf in range(2):
                    P.op("dve", (lambda hh: (lambda e: e.bn_stats(out=STAT[:, hh, :], in_=src[:, hh * 512:(hh + 1) * 512])))(hf),
                         [srck], ["STAT"])
                P.op("dve", lambda e: e.bn_aggr(out=MV[:, 0:2], in_=STAT.rearrange("p a b -> p (a b)")), ["STAT"], ["MV"])
                rsqrt(MV[:, 2:3], MV[:, 1:2], 1e-5, ["MV"], ["MV"])
                ts("dve", MV[:, 3:4], MV[:, 0:1], MV[:, 2:3], ALU.mult, ["MV"], ["MV"], s2=-1.0, op1=ALU.mult)
                act(dst, src, AF.Identity, [srck, "MV"], [dstk], scale=MV[:, 2:3], bias=MV[:, 3:4])
                tt("dve", dst, dst, g_ap, ALU.mult, [dstk, gkey], [dstk])
                tt("dve", dst, dst, b_ap, ALU.add, [dstk, bkey], [dstk])

            def transpose_to(src, srck, dstT, col0, dstk):
                for half in range(2):
                    b, bk = bank("M2", [4, 5, 6, 7])
                    for j in range(4):
                        c = half * 4 + j
                        tr(ps[b][:, j * 128:(j + 1) * 128], src[:, c * 128:(c + 1) * 128], [srck], [bk])
                    cp("act", dstT[:, half * 4:half * 4 + 4, col0:col0 + 128],
                       ps[b].rearrange("p (a b) -> p a b", a=4), [bk], [dstk])

            if S2 <= 1:
                return
            for tile in range(NS):
                mxk = "MX"
                dma("sp", MX, mixT_v[:, :, tile * TS:(tile + 1) * TS],
                    [("mixT", g, h, tile) for g in "ABCD" for h in range(4)], [mxk])
                g1 = load_vec("ln1_g")
                b1 = load_vec("ln1_b")
                for c in range(8):
                    sq = SQ2[c % 2]
                    sqk = ("SQ2", c % 2)
                    tt("dve", sq, MX[:, c, :], MX[:, c, :], ALU.mult, [mxk], [sqk])
                    for j in range(4):
                        mm(ps[4 + j][:, 0:4], sq[:, j * 128:(j + 1) * 128], IND[:, c, :], c == 0, c == 7,
                           [sqk, "const"], [("ps", 4 + j)])
                for j in range(4):
                    ts("dve", RS[:, j, :], ps[4 + j][:, 0:4], 1.0 / 256.0, ALU.mult, [("ps", 4 + j)], ["RS"],
                       s2=1e-6, op1=ALU.add)
                rsqrt(RS.rearrange("p a b -> p (a b)"), RS.rearrange("p a b -> p (a b)"), 0.0, ["RS"], ["RS"])
                if S2 <= 2:
                    continue
                for j in range(4):
                    blk = tile * 4 + j
                    dma("sp", XB, src_x[blk * 128:(blk + 1) * 128, :], [("xs", blk)], ["XB"])
                    for half in range(2):
                        ybs = []
                        for g in range(4):
                            b, bk = bank("Y", [0, 1, 2, 3])
                            ybs.append((b, bk))
                            for c2 in range(2):
                                c = 2 * g + c2
                                mm(ps[b], MX[:, c, j * 128:(j + 1) * 128], WOS[:, c, half * 512:(half + 1) * 512],
                                   c2 == 0, c2 == 1, [mxk, "WOS"], [bk])
                        acc = ACCY[:, half * 512:(half + 1) * 512]
                        ts("dve", acc, ps[ybs[0][0]], RS[:, j, 0:1], ALU.mult, [ybs[0][1], "RS"], ["ACCY"])
                        for g in range(1, 4):
                            stt(acc, ps[ybs[g][0]], RS[:, j, g:g + 1], acc, ALU.mult, ALU.add, [ybs[g][1], "RS", "ACCY"], ["ACCY"])
                    stt(U, XB, ALPHA, ACCY, ALU.mult, ALU.add, ["XB", "ACCY"], ["U"])
                    layernorm(U, "U", X1[:, j, :], ("X1", j), g1, b1)
                    transpose_to(X1[:, j, :], ("X1", j), X1T, j * 128, "X1T")
                if S2 <= 3:
                    continue
                g2 = load_vec("ln2_g")
                b2 = load_vec("ln2_b")
                for sc in range(11):
                    wi = sc % 2
                    dma("pool", WG[wi], wg_l[:, :, sc * 256:(sc + 1) * 256], [], [("WG", wi)])
                    dma("pool", WU[wi], wu_l[:, :, sc * 256:(sc + 1) * 256], [], [("WU", wi)])
                    for j2 in range(2):
                        fc = sc * 2 + j2
                        bg, bkg = bank("G", [0, 1])
                        bu, bku = bank("Uu", [2, 3])
                        for k in range(8):
                            mm(ps[bg], WG[wi][:, k, j2 * 128:(j2 + 1) * 128], X1T[:, k, :], k == 0, k == 7,
                               [("WG", wi), "X1T"], [bkg])
                        for k in range(8):
                            mm(ps[bu], WU[wi][:, k, j2 * 128:(j2 + 1) * 128], X1T[:, k, :], k == 0, k == 7,
                               [("WU", wi), "X1T"], [bku])
                        sg = SG[fc % 2]
                        sgk = ("SG", fc % 2)
                        act(sg, ps[bg], AF.Silu, [bkg], [sgk])
                        tt("dve", AT[:, fc, :], sg, ps[bu], ALU.mult, [sgk, bku], [("AT", fc)])
                if S2 <= 4:
                    continue
                for half in range(2):
                    fb = [4, 5, 6, 7]
                    for sc in range(11):
                        wi = (half * 11 + sc) % 2
                        dma("pool", WD[wi], wd_l[:, sc * 2:sc * 2 + 2, half * 512:(half + 1) * 512], [], [("WD", wi)])
                        for j2 in range(2):
                            fc = sc * 2 + j2
                            for j in range(4):
                                mm(ps[fb[j]], AT[:, fc, j * 128:(j + 1) * 128], WD[wi][:, j2, :], fc == 0, fc == NFC - 1,
                                   [("AT", fc), ("WD", wi)], [("ps", fb[j])])
                    for j in range(4):
                        stt(X1[:, j, half * 512:(half + 1) * 512], X1[:, j, half * 512:(half + 1) * 512], ALPHA,
                            ps[fb[j]], ALU.mult, ALU.add, [("ps", fb[j]), ("X1", j)], [("X1", j)])
                for j in range(4):
                    blk = tile * 4 + j
                    layernorm(X1[:, j, :], ("X1", j), U, "U", g2, b2)
                    o = dma("sp", dst_x[blk * 128:(blk + 1) * 128, :], U, ["U"], [("xs", blk)])
                    if last:
                        all_dma_out.append(o)
                    else:
                        transpose_to(U, "U", X1T, j * 128, "X1T")
                if not last:
                    dma("sp", xtm_f[tile].bitcast(BF16).rearrange("(c p) t -> p c t", p=128), X1T, ["X1T"], [("xtm", tile)])
                    P.cc((lambda jj: (lambda e: e.collective_compute("AllGather", ALU.bypass, replica_groups=rgroups,
                                                                     ins=[xtm_f[jj]], outs=[xta_f[jj]])))(tile),
                         [("xtm", tile)], [("xta", tile)])
                    xta_v = xta_f[tile].bitcast(BF16).rearrange("(r k p) t -> p k r t", r=2, k=8, p=128)
                    for r_ in range(2):
                        dma("sp", XT[:, :, (r_ * 4 + tile) * TS:(r_ * 4 + tile + 1) * TS], xta_v[:, :, r_, :],
                            [("xta", tile)], [("XT", r_ * 4 + tile)])

        build_xt0()
        for l in range(n_layers):
            P.barrier()
            phase1(l)
            P.barrier()
            if phase2:
                phase2_(l, l == n_layers - 1)
        if debug and not phase2:
            pass
        P.barrier()
        P.emit()
    return nc


def kernel(**inputs):
    x = np.ascontiguousarray(np.asarray(inputs["x"], dtype=np.float32))
    nc = build_program()
    base = {nm: np.ascontiguousarray(np.asarray(inputs[nm], dtype=np.float32)) for nm, _ in WNAMES}
    cst = [make_consts(0), make_consts(1)]
    in_maps = []
    for core in range(8):
        bq, r = core // 2, core % 2
        xt = x[bq].reshape(8, 512, D)
        m = dict(base)
        m.update(cst[r])
        m["x"] = np.ascontiguousarray(xt[[nat_tile(p) for p in range(8)]]).reshape(T, D)
        m["xq"] = np.ascontiguousarray(xt[[2 * j + r for j in range(4)]]).reshape(TQ, D)
        in_maps.append(m)
    res = run_bass_kernel_spmd(nc, in_maps, core_ids=list(range(8)))
    out = np.zeros((4, 8, 512, D), np.float32)
    for core in range(8):
        bq, r = core // 2, core % 2
        o = np.asarray(res.results[core]["out"], dtype=np.float32).reshape(4, 512, D)
        for j in range(4):
            out[bq, 2 * j + r] = o[j]
    return out.reshape(4, T, D)
```

```python
import numpy as np
import concourse.bass as bass
import concourse.mybir as mybir
from concourse.bass_utils import run_bass_kernel_spmd
from contextlib import ExitStack

F32 = mybir.dt.float32
BF16 = mybir.dt.bfloat16
AF = mybir.ActivationFunctionType
ALU = mybir.AluOpType
AX = mybir.AxisListType

ENG = ("pe", "act", "dve", "pool", "sp")
NDMASEM = 8


class Op:
    __slots__ = ("eng", "fn", "deps", "is_dma", "signal", "cnt", "semi", "val", "waits", "gid", "is_cc")

    def __init__(self, eng, fn, is_dma):
        self.eng = eng
        self.fn = fn
        self.deps = []
        self.is_dma = is_dma
        self.signal = False
        self.cnt = 0
        self.semi = -1
        self.val = 0
        self.waits = []
        self.is_cc = False


class Prog:
    def __init__(self, nc, stack):
        self.nc = nc
        self.stack = stack
        self.all = []
        self.res = {}
        self.ntens = 0
        self._bar_from = 0

    def sb(self, shape, dt, name=None):
        self.ntens += 1
        return self.stack.enter_context(self.nc.sbuf_tensor("s_" + (name or f"sb{self.ntens}"), list(shape), dt))

    def psum(self, shape, dt, name=None):
        self.ntens += 1
        return self.stack.enter_context(self.nc.psum_tensor(name or f"ps{self.ntens}", list(shape), dt))

    def op(self, eng, fn, reads=(), writes=(), dma=False):
        o = Op(eng, fn, dma)
        deps = o.deps
        res = self.res
        for r in reads:
            st = res.get(r)
            if st is None:
                st = res[r] = [None, []]
            if st[0] is not None:
                deps.append(st[0])
        for w in writes:
            st = res.get(w)
            if st is None:
                st = res[w] = [None, []]
            if st[0] is not None:
                deps.append(st[0])
            deps.extend(st[1])
        for r in reads:
            res[r][1].append(o)
        for w in writes:
            st = res[w]
            st[0] = o
            st[1] = []
        self.all.append(o)
        return o

    def dma(self, eng, out, in_, reads=(), writes=(), **kw):
        return self.op(eng, lambda e: e.dma_start(out=out, in_=in_, **kw), reads, writes, dma=True)

    def cc(self, fn, reads=(), writes=()):
        o = self.op("pool", fn, reads, writes, dma=True)
        o.is_cc = True
        return o

    def wait_all(self, eng, ops):
        o = Op(eng, None, False)
        o.deps = list(ops)
        self.all.append(o)
        return o


    def barrier(self):
        deps = []
        last = {}
        for o in self.all[self._bar_from:]:
            if o.fn is None:
                continue
            if o.is_dma:
                deps.append(o)
            else:
                last[o.eng] = o
        deps.extend(last.values())
        self._bar_from = len(self.all)
        for e in ENG:
            w = Op(e, None, False)
            w.deps = list(deps)
            self.all.append(w)

    def finalize(self):
        nc = self.nc
        for o in self.all:
            nd = []
            seen = set()
            for d in o.deps:
                if id(d) in seen:
                    continue
                seen.add(id(d))
                if d is o:
                    continue
                if not d.is_dma and d.eng == "pe" and o.eng == "pe" and not o.is_dma:
                    continue
                nd.append(d)
                if not d.is_dma:
                    d.signal = True
            o.deps = nd
        cnt = {e: 0 for e in ENG}
        for o in self.all:
            if o.is_dma or o.fn is None:
                continue
            if o.signal:
                cnt[o.eng] += 1
            o.cnt = cnt[o.eng]
        known = {e: {} for e in ENG}
        pool_next = {e: 0 for e in ENG}
        pool_last = {e: [0] * NDMASEM for e in ENG}
        self.nwaits = 0
        ncc = 0
        for o in self.all:
            E = o.eng
            kn = known[E]
            need = {}
            if o.is_cc:
                o.semi = ("cc", ncc)
                o.val = 1
                ncc += 1
            elif o.is_dma:
                s = pool_next[E]
                pool_next[E] = (s + 1) % NDMASEM
                prev = pool_last[E][s]
                key = ("d", E, s)
                if prev > 0 and kn.get(key, 0) < prev:
                    need[key] = prev
                o.semi = (E, s)
                o.val = prev + 16
                pool_last[E][s] = o.val
            for d in o.deps:
                if d.is_dma:
                    key = ("d",) + d.semi
                    v = d.val
                else:
                    key = ("c", d.eng)
                    v = d.cnt
                if kn.get(key, 0) < v and need.get(key, 0) < v:
                    need[key] = v
            for key, v in need.items():
                kn[key] = v
            o.waits = list(need.items())
            self.nwaits += len(need)
        st = self.stack
        self.csem = {e: st.enter_context(nc.semaphore(f"c_{e}")) for e in ENG}
        self.dsem = {(e, s): st.enter_context(nc.semaphore(f"d_{e}{s}")) for e in ("sp", "pool", "act") for s in range(NDMASEM)}
        self.ccsem = [st.enter_context(nc.semaphore(f"cc{i}")) for i in range(ncc)]
        self.byeng = {e: [o for o in self.all if o.eng == e] for e in ENG}

    def _sem(self, key):
        if key[0] == "c":
            return self.csem[key[1]]
        if key[1] == "cc":
            return self.ccsem[key[2]]
        return self.dsem[(key[1], key[2])]

    def emit_engine(self, E, e):
        for o in self.byeng[E]:
            for key, v in o.waits:
                e.wait_ge(self._sem(key), v)
            if o.fn is None:
                continue
            ins = o.fn(e)
            if o.is_cc:
                ins.then_inc(self.ccsem[o.semi[1]])
            elif o.is_dma:
                ins.then_inc(self.dsem[o.semi], 16)
            elif o.signal:
                ins.then_inc(self.csem[E], 1)

    def emit(self):
        nc = self.nc
        self.finalize()
        with nc.Block() as block:
            @block.tensor
            def _(e):
                self.emit_engine("pe", e)

            @block.scalar
            def _(e):
                self.emit_engine("act", e)

            @block.vector
            def _(e):
                self.emit_engine("dve", e)

            @block.gpsimd
            def _(e):
                self.emit_engine("pool", e)

            @block.sync
            def _(e):
                self.emit_engine("sp", e)

D = 1024
T = 4096
NB = 32
NT = 8
TS = 512
DFF = 2816
NFC = 22
INW = 2468
NL = 4
ALPHA = float((2.0 * NL) ** 0.25)
GBASE = {"A": 0, "B": 772, "C": 1188, "D": 1956}
GW = {"A": 772, "B": 416, "C": 768, "D": 512}
MASKV = -30000.0
NS = 4
TQ = NS * 512


def nat_tile(p):
    return 2 * (p % 4) + p // 4


def pos_tile(nt):
    return (nt % 2) * 4 + nt // 2


def pb_nat(n):
    return pos_tile(n // 4) * 4 + n % 4
STAGE = 99
KVF = 0
KVT = 8
VENG = 'dve'
S2 = 99
AMODE = 0
WARM = 0
P2T = 8


def make_consts(rank):
    c = {}
    c["identf"] = np.eye(128, dtype=np.float32)
    p = np.arange(128)[:, None]
    xx = np.arange(896)[None, :]
    ms = np.where((xx - 384) < p, MASKV, 0.0)
    mx = np.where((xx - 384) <= p, MASKV, 0.0)
    full = np.full((128, 896), MASKV)
    zero = np.zeros((128, 896))
    mb = np.zeros((2, 2, 128, 896), np.float32)
    for ti, st in enumerate((ms, mx)):
        if rank == 0:
            mb[ti, 0] = st
            mb[ti, 1] = full
        else:
            mb[ti, 0] = zero
            mb[ti, 1] = st
    c["mbase"] = mb
    sel = np.zeros((128, 2), np.float32)
    sel[:, rank] = 1.0
    c["sel"] = sel
    s = np.arange(128)[:, None]
    t = np.arange(128)[None, :]
    tri = np.zeros((4, 128, 128), np.float32)
    tri[0] = (s <= t)
    tri[1] = 1.0
    tri[2] = -(s > t).astype(np.float32)
    tri[3] = -(s <= t).astype(np.float32)
    c["tri"] = tri
    pos = np.arange(T, dtype=np.float32)
    inv = (np.float32(10000.0) ** (-np.arange(0, 32, 2, dtype=np.float32) / np.float32(32))).astype(np.float32)
    ang = (pos[:, None] * inv[None, :]).astype(np.float32)
    cs = np.cos(ang).astype(np.float32).T
    sn = np.sin(ang).astype(np.float32).T
    rope = np.zeros((2, 32, T), np.float32)
    rope[0, :16] = cs
    rope[0, 16:] = cs
    rope[1, :16] = -sn
    rope[1, 16:] = sn
    r8 = rope.reshape(2, 32, 8, 512)
    c["rope"] = np.ascontiguousarray(r8[:, :, [nat_tile(pp_) for pp_ in range(8)], :]).reshape(2, 32, T)
    c["ropeq"] = np.ascontiguousarray(r8[:, :, [2 * j_ + rank for j_ in range(4)], :]).reshape(2, 32, TQ)
    bd = np.zeros((4, 128, 256), np.float32)
    pp = np.arange(128)[:, None].astype(np.float32)
    cc = np.arange(128)[None, :].astype(np.float32)
    for h in range(4):
        m = np.float32(2.0 ** (-8.0 * (h + 1) / 4.0))
        d1 = 128.0 + cc - pp
        bd[h, :, 0:128] = np.where(cc < pp, -m * d1, MASKV)
        d2 = cc - pp
        bd[h, :, 128:256] = np.where(cc >= pp, -m * d2, MASKV)
    c["biasd"] = bd
    ind = np.zeros((128, 8, 4), np.float32)
    for ch in range(8):
        ind[:, ch, ch // 2] = 1.0
    c["ind"] = ind
    return c


WNAMES = [("w_in", [NL, D, INW]), ("fox_b_f", [NL, 4]), ("mla_g_q", [NL, 256]), ("mla_g_kv", [NL, 128]),
          ("mla_w_uq", [NL, 256, 384]), ("mla_w_ukv", [NL, 128, 512]), ("swa_sinks", [NL, 4]),
          ("mix_g", [NL, D]), ("w_o", [NL, D, D]), ("ln1_g", [NL, D]), ("ln1_b", [NL, D]),
          ("w_gate", [NL, D, DFF]), ("w_up", [NL, D, DFF]), ("w_down", [NL, DFF, D]),
          ("ln2_g", [NL, D]), ("ln2_b", [NL, D])]
CNAMES = [("identf", [128, 128]), ("mbase", [2, 2, 128, 896]), ("tri", [4, 128, 128]), ("rope", [2, 32, T]),
          ("ropeq", [2, 32, TQ]), ("sel", [128, 2]), ("biasd", [4, 128, 256]), ("ind", [128, 8, 4])]


class Arena:
    def __init__(self, P, nbytes, name):
        self.t = P.sb([128, nbytes // 2], BF16, name)
        self.n = nbytes
        self.off = 0

    def reset(self):
        self.off = 0

    def get(self, shape, dt):
        esz = 4 if dt == F32 else 2
        ne = int(np.prod(shape[1:]))
        n = (ne * esz + 63) // 64 * 64
        a = self.t[:, self.off // 2:(self.off + n) // 2]
        self.off += n
        assert self.off <= self.n, ("arena overflow", self.off, self.n)
        if dt == F32:
            a = a.bitcast(F32)
        a = a[:, 0:ne]
        if len(shape) == 3:
            a = a.rearrange("p (a b) -> p a b", a=shape[1])
        elif len(shape) == 4:
            a = a.rearrange("p (a b c) -> p a b c", a=shape[1], b=shape[2])
        return a


def build_program(n_layers=NL, groups="ABCD", debug=False, phase2=True, n_cores=8):
    nc = bass.Bass("TRN2", target_bir_lowering=False)
    dd = {}
    dd["x"] = nc.dram_tensor("x", [T, D], F32, kind="ExternalInput").ap()
    for nm, shp in WNAMES + CNAMES:
        dd[nm] = nc.dram_tensor(nm, list(shp), F32, kind="ExternalInput").ap()
    dd["xq"] = nc.dram_tensor("xq", [TQ, D], F32, kind="ExternalInput").ap()
    out_d = nc.dram_tensor("out", [TQ, D], F32, kind="ExternalOutput").ap()
    xs_d = nc.dram_tensor("xs", [TQ, D], F32, kind="Internal").ap()
    mixT_d = nc.dram_tensor("mixT", [D, TQ], BF16, kind="ExternalOutput" if debug else "Internal").ap()
    xtm_f = [nc.dram_tensor(f"xtm{j_}", [D, TS // 2], F32, kind="Internal").ap() for j_ in range(NS)]
    xta_f = [nc.dram_tensor(f"xta{j_}", [2 * D, TS // 2], F32, kind="Internal").ap() for j_ in range(NS)]
    rgroups = [[2 * g_, 2 * g_ + 1] for g_ in range(n_cores // 2)]

    with ExitStack() as st:
        P = Prog(nc, st)

        def mm(out, lhsT, rhs, start, stop, r, w, **kw):
            P.op("pe", lambda e: e.matmul(out, lhsT=lhsT, rhs=rhs, start=start, stop=stop, **kw), r, w)

        def tr(out, in_, r, w):
            P.op("pe", lambda e: e.transpose(out=out, in_=in_, identity=identf), list(r) + ["const"], w)

        def act(out, in_, func, r, w, scale=None, bias=None):
            kw = {}
            if scale is not None:
                kw["scale"] = scale
            if bias is not None:
                kw["bias"] = bias
            P.op("act", lambda e: e.activation(out=out, in_=in_, func=func, **kw), r, w)

        def tt(eng, out, in0, in1, op, r, w):
            P.op(eng, lambda e: e.tensor_tensor(out=out, in0=in0, in1=in1, op=op), r, w)

        def ts(eng, out, in0, s1, op0, r, w, s2=None, op1=None):
            if op1 is None:
                P.op(eng, lambda e: e.tensor_scalar(out=out, in0=in0, scalar1=s1, scalar2=None, op0=op0), r, w)
            else:
                P.op(eng, lambda e: e.tensor_scalar(out=out, in0=in0, scalar1=s1, scalar2=s2, op0=op0, op1=op1), r, w)

        def stt(out, in0, scalar, in1, op0, op1, r, w):
            P.op("dve", lambda e: e.scalar_tensor_tensor(out=out, in0=in0, scalar=scalar, in1=in1, op0=op0, op1=op1), r, w)

        def cp(eng, out, in_, r, w):
            if eng == "act":
                P.op("act", lambda e: e.activation(out=out, in_=in_, func=AF.Copy), r, w)
            else:
                P.op(eng, lambda e: e.tensor_copy(out=out, in_=in_), r, w)

        def rsqrt(out, in_, eps, r, w):
            act(out, in_, AF.Ln, r, w, bias=eps)
            act(out, out, AF.Exp, list(w), w, scale=-0.5)

        def recip(out, in_, r, w):
            P.op("dve", lambda e: e.reciprocal(out=out, in_=in_), r, w)

        def dma(eng, out, in_, r, w, **kw):
            return P.dma(eng, out, in_, r, w, **kw)

        ps = [P.psum([128, 512], F32, f"bank{i}")[:] for i in range(8)]
        rot = {}

        def bank(pool, lst):
            i = rot.get(pool, 0)
            rot[pool] = i + 1
            b = lst[i % len(lst)]
            return b, ("ps", b)

        XT = P.sb([128, 8, T], BF16, "XT")[:]
        AR1 = Arena(P, 49152, "AR1")
        AR2 = Arena(P, 80 * 1024, "AR2")
        identf = P.sb([128, 128], F32, "identf")[:]
        identb = P.sb([128, 128], BF16, "identb")[:]
        onesb = P.sb([128, 128], BF16, "onesb")[:]
        MB = P.sb([128, 4, 896], BF16, "MB")[:]
        TRI = P.sb([128, 4, 128], F32, "TRI")[:]
        TRIB = P.sb([128, 2, 128], BF16, "TRIB")[:]
        BIASD = P.sb([128, 4, 256], F32, "BIASD")[:]
        IND = P.sb([128, 8, 4], BF16, "IND")[:]
        SMALL = P.sb([128, 64], F32, "SMALL")[:]
        BF_ = SMALL[:, 0:4]
        ES = SMALL[:, 4:8]
        GQ = SMALL[:, 8:10]
        GKV = SMALL[:, 10:11]
        MG = SMALL[:, 16:24]
        SEL = SMALL[:, 24:26]
        CARRYX = P.sb([128, NB, 4], F32, "CARRYX")[:]

        dma("sp", identf, dd["identf"], [], ["const"])
        dma("pool", identb, dd["identf"], [], ["const"])
        dma("pool", onesb, dd["tri"][1], [], ["const"])
        dma("pool", MB, dd["mbase"].rearrange("t w p c -> p (t w) c"), [], ["const"])
        dma("sp", SEL, dd["sel"], [], ["const"])
        dma("sp", TRI, dd["tri"].rearrange("i p c -> p i c"), [], ["const"])
        dma("pool", TRIB, dd["tri"][2:4].rearrange("i p c -> p i c"), [], ["const"])
        dma("sp", BIASD, dd["biasd"].rearrange("h p c -> p h c"), [], ["const"])
        dma("pool", IND, dd["ind"], [], ["const"])
        P.op("dve", lambda e: e.memset(CARRYX[:, 0, :], 0.0), [], ["carry0"])

        all_dma_out = []

        def build_xt0():
            AR2.reset()
            XB = [AR2.get([128, D], F32) for _ in range(2)]
            for blk in range(NB):
                xb = XB[blk % 2]
                k = ("XB0", blk % 2)
                dma("sp", xb, dd["x"][blk * 128:(blk + 1) * 128, :], [], [k])
                for half in range(2):
                    b, bk = bank("M", list(range(8)))
                    for j in range(4):
                        c = half * 4 + j
                        tr(ps[b][:, j * 128:(j + 1) * 128], xb[:, c * 128:(c + 1) * 128], [k], [bk])
                    cp("act" if half else "dve", XT[:, half * 4:half * 4 + 4, blk * 128:(blk + 1) * 128],
                       ps[b].rearrange("p (a b) -> p a b", a=4), [bk], [("XT", blk // 4)])

        def phase1(l):
            AR1.reset()
            AR2.reset()
            KT = [AR1.get([128, T], BF16) for _ in range(4)]
            V = AR1.get([128, NB, 256], BF16)
            QT = [[AR2.get([128, TS], BF16) for _ in range(2)] for _ in range(4)]
            WIN = AR2.get([128, 8, 772], BF16)
            PT = [AR2.get([128, TS], BF16) for _ in range(3)]
            CT = [[AR2.get([128, TS], F32) for _ in range(2)] for _ in range(2)]
            RC = AR2.get([128, TS], F32)
            MO = [AR2.get([128, TS], BF16) for _ in range(2)]
            GATE = AR2.get([128, NB, 4], F32)
            TOTs = AR2.get([128, NB, 4], F32)
            CNEG = AR2.get([128, NB, 4], F32)
            BT = [AR2.get([128, NB, 4], F32) for _ in range(2)]
            WUQ = AR2.get([128, 2, 384], BF16)
            WUQS = AR2.get([128, 2, 384], BF16)
            WUKV = AR2.get([128, 512], BF16)
            WKRS = AR2.get([128, 8, 96], BF16)
            SQ = AR2.get([128, 2, TS], BF16)
            RSB = AR2.get([128, TS], F32)
            CQN = AR2.get([128, 2, TS], BF16)
            CKVN = AR2.get([128, TS], BF16)
            ROPE = AR2.get([128, 2, TS], F32)
            RT1 = AR2.get([128, TS], F32)
            RT2 = AR2.get([128, TS], F32)
            SBD = [AR2.get([128, 256], F32) for _ in range(2)]
            PD = [AR2.get([128, 256], BF16) for _ in range(2)]
            XTQ = AR2.get([128, 8, TS], BF16)
            SPB = [AR2.get([128, TS], BF16) for _ in range(2)]
            MO2 = AR2.get([128, TS], BF16)

            def make_xtq(j):
                ts("dve", XTQ, XT[:, :, j * TS:(j + 1) * TS], SEL[:, 0:1], ALU.mult, [("XT", j), "const"], ["XTQ"])
                stt(XTQ, XT[:, :, (4 + j) * TS:(5 + j) * TS], SEL[:, 1:2], XTQ, ALU.mult, ALU.add,
                    [("XT", 4 + j), "const", "XTQ"], ["XTQ"])

            w_in_l = dd["w_in"][l].rearrange("(k p) n -> p k n", p=128)
            ALLB = list(range(8))
            mo_cnt = [0]

            def load_win(g):
                dma("pool", WIN[:, :, 0:GW[g]], w_in_l[:, :, GBASE[g]:GBASE[g] + GW[g]], [], ["WIN"])

            def xt_keys(tile):
                return [("XT", tile)]

            def proj_fm(dst, dk, col0, tile, eng, wkeys, dkeys, pool="M", banks=None, lw=None, own=False):
                b, bk = bank(pool, banks or [7])
                W = WIN if lw is None else lw
                for k in range(8):
                    if own:
                        mm(ps[b][0:dk, :], W[:, k, col0:col0 + dk], XTQ[:, k, :], k == 0, k == 7, ["XTQ"] + wkeys, [bk])
                    else:
                        mm(ps[b][0:dk, :], W[:, k, col0:col0 + dk], XT[:, k, tile * TS:(tile + 1) * TS], k == 0, k == 7,
                           xt_keys(tile) + wkeys, [bk])
                if dst is not None:
                    cp(eng, dst, ps[b][0:dk, :], [bk], dkeys)
                return b, bk

            def store_mix(g, h, qt, src_bank, bk, rc_ap, extra_r):
                i = mo_cnt[0] % 2
                mo_cnt[0] += 1
                mo = MO[i]
                mk = ("MO", i)
                if rc_ap is None:
                    cp("dve", mo[0:64, :], ps[src_bank][0:64, :], [bk], [mk])
                else:
                    tt("dve", mo[0:64, :], ps[src_bank][0:64, :], rc_ap, ALU.mult, [bk] + extra_r, [mk])
                row = "ABCD".index(g) * 256 + h * 64
                o = dma("sp", mixT_d[row:row + 64, qt * TS:(qt + 1) * TS], mo[0:64, :], [mk], [("mixT", g, h, qt)])

            def softmax_unit(g, h, qt, dk, scale, qap, qkeys, bias_fn, bias_keys, u):
                kbl = [4 * p_ + i_ for p_ in list(range(qt + 1)) + list(range(4, 4 + qt + 1)) for i_ in range(4)]
                nkb = len(kbl)
                ob = [3, 4][u % 2]
                db = [5, 6][u % 2]
                obk, dbk = ("ps", ob), ("ps", db)
                LA = 2
                sb_of = {}
                for step in range(nkb + LA):
                    if step < nkb:
                        kb = kbl[step]
                        b, bk = bank("S", [0, 1, 2])
                        sb_of[step] = (b, bk)
                        p_, i = kb // 4, kb % 4
                        wh = 0 if p_ == qt else (1 if p_ == 4 + qt else -1)
                        mm(ps[b], KT[h][0:dk, kb * 128:(kb + 1) * 128], qap, True, wh < 0,
                           [("KT", h, kb // 4)] + qkeys, [bk])
                        if wh >= 0:
                            mm(ps[b], identb, MB[:, wh, 384 - 128 * i:384 - 128 * i + 512], False, True, ["const"], [bk])
                    s2 = step - LA
                    if 0 <= s2 < nkb:
                        kb = kbl[s2]
                        b, bk = sb_of.pop(s2)
                        pi = s2 % 3
                        pt, pk = PT[pi], ("PT", pi)
                        act(pt, ps[b], AF.Exp, [bk] + bias_keys, [pk], scale=scale,
                            bias=(bias_fn(kb) if bias_fn else None))
                        mm(ps[ob][0:64, :], V[:, kb, h * 64:(h + 1) * 64], pt, s2 == 0, s2 == nkb - 1,
                           [pk, ("V", kb // 4)], [obk])
                        mm(ps[db][0:64, :], onesb[:, 0:64], pt, s2 == 0, s2 == nkb - 1, [pk, "const"], [dbk])
                recip(RC[0:64, :], ps[db][0:64, :], [dbk], ["RC"])
                store_mix(g, h, qt, ob, obk, RC[0:64, :], ["RC"])

            def sb_units(hs, qt, qaps, qkeys):
                order = [pos_tile(nt_) * 4 + i_ for nt_ in reversed(range(2 * qt + 2)) for i_ in reversed(range(4))]
                nkb = len(order)
                scale = 0.125
                zb = [[0, 1], [4, 5]]
                accb = [2, 6]
                obb = [3, 7]
                LA = 1
                z_of = {}
                for step in range(nkb + LA):
                    for si, h in enumerate(hs):
                        if step < nkb:
                            kb = order[step]
                            b = zb[si][step % 2]
                            bk = ("ps", b)
                            z_of[(si, step)] = (b, bk)
                            p_, i = kb // 4, kb % 4
                            wh = 0 if p_ == qt else (1 if p_ == 4 + qt else -1)
                            for _w in range(WARM):
                                mm(ps[b], KT[h][0:64, kb * 128:(kb + 1) * 128], qaps[si], True, True,
                                   [("KT", h, kb // 4)] + qkeys[si], [bk])
                            mm(ps[b], KT[h][0:64, kb * 128:(kb + 1) * 128], qaps[si], True, wh < 0,
                               [("KT", h, kb // 4)] + qkeys[si], [bk])
                            if wh >= 0:
                                mm(ps[b], identb, MB[:, 2 + wh, 384 - 128 * i:384 - 128 * i + 512], False, True,
                                   ["const"], [bk])
                    for si, h in enumerate(hs):
                        s2 = step - LA
                        if 0 <= s2 < nkb:
                            kb = order[s2]
                            b, bk = z_of.pop((si, s2))
                            SP, T1 = CT[si]
                            spk, t1k = ("SP", si), ("T1", si)
                            ab, abk = accb[si], ("ps", accb[si])
                            ob, obk = obb[si], ("ps", obb[si])
                            spb, spbk = SPB[si], ("SPB", si)
                            act(SP, ps[b], AF.Exp, [bk], [spk], scale=scale)
                            act(spb, SP, AF.Ln, [spk], [spbk], bias=1.0)
                            mm(ps[ab], TRIB[:, 0, :], spb, s2 == 0, True, [spbk, "const"], [abk], skip_group_check=True)
                            stt(T1, ps[b], scale, spb, ALU.mult, ALU.subtract, [bk, spbk], [t1k])
                            tt("dve", T1, ps[ab], T1, ALU.add, [abk, t1k], [t1k])
                            if s2 < nkb - 1:
                                mm(ps[ab], TRIB[:, 1, :], spb, False, True, [spbk, "const"], [abk], skip_group_check=True)
                            pi = (s2 * 2 + si) % 3
                            pt, pk = PT[pi], ("PT", pi)
                            act(pt, T1, AF.Exp, [t1k], [pk])
                            mm(ps[ob][0:64, :], V[:, kb, h * 64:(h + 1) * 64], pt, s2 == 0, s2 == nkb - 1,
                               [pk, ("V", kb // 4)], [obk])
                for si, h in enumerate(hs):
                    store_mix("C", h, qt, obb[si], ("ps", obb[si]), None, [])

            def kv_pass(g, kcols, vcol0, nv, gates):
                KVB = list(range(7)) if gates else ALLB
                for tile in range(NT):
                    for hp in range(len(kcols) // 2):
                        b, bk = proj_fm(None, 128, kcols[2 * hp], tile, None, ["WIN"], None, banks=KVB)
                        cp("dve", KT[2 * hp][0:64, tile * TS:(tile + 1) * TS], ps[b][0:64, :], [bk], [("KT", 2 * hp, tile)])
                        P.op("act", (lambda o_, i_: (lambda e: e.activation(out=o_, in_=i_, func=AF.Copy)))(
                            KT[2 * hp + 1][0:64, tile * TS:(tile + 1) * TS], ps[b][64:128, :]), [bk], [("KT", 2 * hp + 1, tile)])
                    for j in range(4):
                        blk = tile * 4 + j
                        b, bk = bank("M", KVB)
                        n = nv
                        for k in range(8):
                            mm(ps[b][:, 0:n], XT[:, k, blk * 128:(blk + 1) * 128], WIN[:, k, vcol0:vcol0 + n],
                               k == 0, k == 7, xt_keys(tile) + ["WIN"], [bk])
                        cp("dve", V[:, blk, 0:nv], ps[b][:, 0:nv], [bk], [("V", tile)])
                        if gates:
                            for k in range(8):
                                mm(ps[7][:, blk * 4:(blk + 1) * 4], XT[:, k, blk * 128:(blk + 1) * 128],
                                   WIN[:, k, vcol0 + nv:vcol0 + nv + 4], k == 0, k == 7, xt_keys(tile) + ["WIN"],
                                   [("ps", 7)], skip_group_check=True)
                if gates:
                    cp("dve", GATE.rearrange("p a b -> p (a b)"), ps[7][:, 0:128], [("ps", 7)], ["GATE"])

            if "A" in groups:
                load_win("A")
                dma("sp", BF_, dd["fox_b_f"][l].partition_broadcast(128), [], ["BF"])
                kv_pass("A", [256, 320, 384, 448], 512, 256, AMODE != 1)
                if AMODE in (0, 3):
                    for h in range(4):
                        ts("dve", GATE[:, :, h], GATE[:, :, h], BF_[:, h:h + 1], ALU.add, ["GATE", "BF"], ["GATE"])
                    GF = GATE.rearrange("p a b -> p (a b)")
                    act(GF, GF, AF.Exp, ["GATE"], ["GATE"], scale=-1.0)
                    act(GF, GF, AF.Ln, ["GATE"], ["GATE"], bias=1.0)
                    if False:
                        pass
                    b1, bk1 = bank("M", ALLB)
                    mm(ps[b1][:, 0:128], TRI[:, 1, :], GF, True, True, ["GATE", "const"], [bk1])
                    b2, bk2 = bank("M", ALLB)
                    mm(ps[b2][:, 0:128], TRI[:, 0, :], GF, True, True, ["GATE", "const"], [bk2])
                    cp("dve", TOTs.rearrange("p a b -> p (a b)"), ps[b1][:, 0:128], [bk1], ["TOTs"])
                    if False:
                        pass
                    for n_ in range(NB - 1):
                        tt("dve", CARRYX[:, pb_nat(n_ + 1), :], CARRYX[:, pb_nat(n_), :], TOTs[:, pb_nat(n_), :], ALU.add,
                           ["TOTs", "carry0", "CARRYX"], ["CARRYX"])
                    tt("dve", CNEG, ps[b2][:, 0:128].rearrange("p (a b) -> p a b", a=NB), CARRYX, ALU.add,
                       [bk2, "CARRYX", "carry0"], ["CNEG"])
                    if False:
                        pass
                    u = 0
                    for qt in range(NS):
                        bt = BT[qt % 2]
                        btk = ("BT", qt % 2)
                        for h in range(4):
                            ts("dve", bt[:, :, h], CNEG[:, :, h], CARRYX[:, (4 + qt) * 4, h:h + 1], ALU.subtract,
                               ["CNEG", "CARRYX", "carry0"], [btk])
                        make_xtq(qt)
                        for hp in range(2):
                            b_, bk_ = proj_fm(None, 128, hp * 128, qt, None, ["WIN"], None, own=True)
                            cp("dve", QT[2 * hp][qt % 2][0:64, :], ps[b_][0:64, :], [bk_], [("QT", 2 * hp, qt % 2)])
                            P.op("act", (lambda o_, i_: (lambda e: e.activation(out=o_, in_=i_, func=AF.Copy)))(
                                QT[2 * hp + 1][qt % 2][0:64, :], ps[b_][64:128, :]), [bk_], [("QT", 2 * hp + 1, qt % 2)])
                        for h in range(4):
                            qb = qt % 2
                            qk = ("QT", h, qb)
                            softmax_unit("A", h, qt, 64, 0.125, QT[h][qb][0:64, :], [qk],
                                         (lambda kb, hh=h, bb=bt: bb[:, kb, hh:hh + 1]), [btk], u)
                            u += 1

            if "B" in groups:
                P.barrier()
                load_win("B")
                wb = GBASE["B"]
                uq = dd["mla_w_uq"][l].rearrange("(c p) n -> p c n", p=128)
                dma("pool", WUQ, uq, [], ["WUQ"])
                uq4 = uq.rearrange("p c (h x) -> p c h x", h=4)
                wq4 = WUQS.rearrange("p c (h x) -> p c h x", h=4)
                for c in range(2):
                    dma("pool", wq4[:, c, :, 0:64], uq4[:, c, :, 0:64], [], ["WUQS"])
                    dma("pool", wq4[:, c, :, 64:80], uq4[:, c, :, 80:96], [], ["WUQS"])
                    dma("pool", wq4[:, c, :, 80:96], uq4[:, c, :, 64:80], [], ["WUQS"])
                dma("pool", WUKV, dd["mla_w_ukv"][l], [], ["WUKV"])
                dma("pool", WKRS[:, :, 0:64], w_in_l[:, :, wb + 320:wb + 384], [], ["WKRS"])
                dma("pool", WKRS[:, :, 64:80], w_in_l[:, :, wb + 400:wb + 416], [], ["WKRS"])
                dma("pool", WKRS[:, :, 80:96], w_in_l[:, :, wb + 384:wb + 400], [], ["WKRS"])
                dma("sp", GQ, dd["mla_g_q"][l].rearrange("(c p) -> p c", p=128), [], ["GQ"], allow_slow_non_contiguous=True)
                dma("sp", GKV, dd["mla_g_kv"][l].rearrange("(c p) -> p c", p=128), [], ["GKV"], allow_slow_non_contiguous=True)
                ts("dve", GQ, GQ, 16.0, ALU.mult, ["GQ"], ["GQ"])
                ts("dve", GKV, GKV, float(np.sqrt(128.0)), ALU.mult, ["GKV"], ["GKV"])
                WUKV4 = WUKV.rearrange("p (h x) -> p h x", h=4)

                def load_rope(tile, own=False):
                    src_ = dd["ropeq"] if own else dd["rope"]
                    dma("sp", ROPE[64:96, 0, :], src_[0, :, tile * TS:(tile + 1) * TS], [], ["ROPE"])
                    dma("sp", ROPE[64:96, 1, :], src_[1, :, tile * TS:(tile + 1) * TS], [], ["ROPE"])

                def rope_apply(ba, bka, bb, bkb, dsts, dkeys):
                    tt("dve", RT1[64:96, :], ps[ba][64:96, :], ROPE[64:96, 0, :], ALU.mult, [bka, "ROPE"], ["RT1"])
                    tt("dve", RT2[64:96, :], ps[bb][64:96, :], ROPE[64:96, 1, :], ALU.mult, [bkb, "ROPE"], ["RT2"])
                    for dst, dk_ in zip(dsts, dkeys):
                        tt("dve", dst, RT1[64:96, :], RT2[64:96, :], ALU.add, ["RT1", "RT2"], [dk_])

                for tile in range(NT):
                    ba, bka = proj_fm(None, 128, 256, tile, None, ["WIN"], None, banks=ALLB)
                    act(SQ[:, 0, :], ps[ba], AF.Square, [bka], ["SQ"])
                    bs, bks = bank("M", ALLB)
                    mm(ps[bs], onesb, SQ[:, 0, :], True, True, ["SQ", "const"], [bks])
                    rsqrt(RSB, ps[bs], 128.0 * 1e-6, [bks], ["RSB"])
                    stt(CKVN, ps[ba], GKV[:, 0:1], RSB, ALU.mult, ALU.mult, [bka, "RSB", "GKV"], ["CKVN"])
                    for h in range(4):
                        b, bk = bank("M", ALLB)
                        mm(ps[b][0:64, :], WUKV[:, h * 128:h * 128 + 64], CKVN, True, True, ["CKVN", "WUKV"], [bk])
                        cp("dve", KT[h][0:64, tile * TS:(tile + 1) * TS], ps[b][0:64, :], [bk], [("KT", h, tile)])
                    for j in range(4):
                        blk = tile * 4 + j
                        b, bk = bank("M", ALLB)
                        mm(ps[b][:, 0:256].rearrange("p (h x) -> p h x", h=4), CKVN[:, j * 128:(j + 1) * 128],
                           WUKV4[:, :, 64:128], True, True, ["CKVN", "WUKV"], [bk])
                        cp("dve", V[:, blk, :], ps[b][:, 0:256], [bk], [("V", tile)])
                    bc, bkc = proj_fm(None, 96, 320, tile, None, ["WIN"], None, banks=ALLB)
                    bd_, bkd = proj_fm(None, 96, 0, tile, None, ["WKRS"], None, banks=ALLB, lw=WKRS)
                    load_rope(tile)
                    rope_apply(bc, bkc, bd_, bkd, [KT[h][64:96, tile * TS:(tile + 1) * TS] for h in range(4)],
                               [("KT", h, tile) for h in range(4)])
                u = 0
                sc_b = float(96.0 ** -0.5)
                for qt in range(NS):
                    make_xtq(qt)
                    bq = []
                    for c in range(2):
                        bq.append(proj_fm(None, 128, c * 128, qt, None, ["WIN"], None, banks=[7, 0, 1, 2], own=True))
                        act(SQ[:, c, :], ps[bq[c][0]], AF.Square, [bq[c][1]], ["SQ"])
                    bs, bks = bank("M", [7, 0, 1, 2])
                    for c in range(2):
                        mm(ps[bs], onesb, SQ[:, c, :], c == 0, c == 1, ["SQ", "const"], [bks])
                    rsqrt(RSB, ps[bs], 256.0 * 1e-6, [bks], ["RSB"])
                    for c in range(2):
                        stt(CQN[:, c, :], ps[bq[c][0]], GQ[:, c:c + 1], RSB, ALU.mult, ALU.mult,
                            [bq[c][1], "RSB", "GQ"], ["CQN"])
                    load_rope(qt, own=True)
                    for h in range(4):
                        qb = u % 2
                        qk = ("QT", h, qb)
                        be, bke = bank("M", [7, 0, 1, 2])
                        for c in range(2):
                            mm(ps[be][0:96, :], WUQ[:, c, h * 96:(h + 1) * 96], CQN[:, c, :], c == 0, c == 1,
                               ["CQN", "WUQ"], [bke])
                        bf_, bkf = bank("M", [7, 0, 1, 2])
                        for c in range(2):
                            mm(ps[bf_][0:96, :], WUQS[:, c, h * 96:(h + 1) * 96], CQN[:, c, :], c == 0, c == 1,
                               ["CQN", "WUQS"], [bkf])
                        cp("dve", QT[h][qb][0:64, :], ps[be][0:64, :], [bke], [qk])
                        rope_apply(be, bke, bf_, bkf, [QT[h][qb][64:96, :]], [qk])
                        softmax_unit("B", h, qt, 96, sc_b, QT[h][qb][0:96, :], [qk], None, [], u)
                        u += 1

            if "C" in groups:
                P.barrier()
                load_win("C")
                kv_pass("C", [256, 320, 384, 448], 512, 256, False)
                for qt in range(NS):
                    make_xtq(qt)
                    for pair in range(2):
                        hs = [2 * pair, 2 * pair + 1]
                        qaps, qkeys = [], []
                        qb = qt % 2
                        b_, bk_ = proj_fm(None, 128, pair * 128, qt, None, ["WIN"], None, banks=[0, 1, 4, 5], own=True)
                        cp("dve", QT[hs[0]][qb][0:64, :], ps[b_][0:64, :], [bk_], [("QT", hs[0], qb)])
                        P.op("act", (lambda o_, i_: (lambda e: e.activation(out=o_, in_=i_, func=AF.Copy)))(
                            QT[hs[1]][qb][0:64, :], ps[b_][64:128, :]), [bk_], [("QT", hs[1], qb)])
                        for h in hs:
                            qk = ("QT", h, qb)
                            qaps.append(QT[h][qb][0:64, :])
                            qkeys.append([qk])
                        sb_units(hs, qt, qaps, qkeys)

            if "D" in groups:
                P.barrier()
                load_win("D")
                dma("sp", ES, dd["swa_sinks"][l].partition_broadcast(128), [], ["ES"])
                act(ES, ES, AF.Exp, ["ES"], ["ES"])
                kv_pass("D", [256, 320], 384, 128, False)
                steps = [(js, h, half, j) for js in range(NS) for h in range(4) for half in range(2) for j in range(4)]
                sstate = {}

                def d_unit(js, h, half):
                    return (js * 4 + h) * 2 + half

                def d_stage_s(st_):
                    js, h, half, j = st_
                    u = d_unit(js, h, half)
                    kvh = h // 2
                    nt = 2 * js + half
                    qb = u % 2
                    qk = ("QT", h, qb)
                    if j == 0:
                        proj_fm(QT[h][qb][0:64, :], 64, h * 64, pos_tile(nt), "dve", ["WIN"], [qk])
                    n = 4 * nt + j
                    pn = pb_nat(n)
                    pm = pb_nat(n - 1) if n > 0 else 0
                    b, bk = bank("S", [0, 1, 2])
                    qa = QT[h][qb][0:64, j * 128:(j + 1) * 128]
                    if n > 0:
                        mm(ps[b][:, 0:128], KT[kvh][0:64, pm * 128:(pm + 1) * 128], qa, True, True,
                           [("KT", kvh, pm // 4), qk], [bk])
                    mm(ps[b][:, 128:256], KT[kvh][0:64, pn * 128:(pn + 1) * 128], qa, True, True,
                       [("KT", kvh, pn // 4), qk], [bk])
                    sstate[st_] = (b, bk)

                def d_stage_rest(st_, idx):
                    js, h, half, j = st_
                    u = d_unit(js, h, half)
                    kvh = h // 2
                    nt = 2 * js + half
                    n = 4 * nt + j
                    pn = pb_nat(n)
                    pm = pb_nat(n - 1) if n > 0 else 0
                    lo = 0 if n > 0 else 128
                    b, bk = sstate.pop(st_)
                    ob = [3, 4][u % 2]
                    db = [5, 6][u % 2]
                    obk, dbk = ("ps", ob), ("ps", db)
                    si = idx % 2
                    sbd, pd = SBD[si], PD[si]
                    stt(sbd[:, lo:256], ps[b][:, lo:256], 0.125, BIASD[:, h, lo:256], ALU.mult, ALU.add,
                        [bk, "const"], [("SBD", si)])
                    act(pd[:, lo:256], sbd[:, lo:256], AF.Exp, [("SBD", si)], [("PD", si)])
                    oc = slice(j * 128, (j + 1) * 128)
                    if n > 0:
                        mm(ps[ob][0:64, oc], V[:, pm, kvh * 64:(kvh + 1) * 64], pd[:, 0:128], True, False,
                           [("PD", si), ("V", pm // 4)], [obk])
                        mm(ps[db][0:64, oc], onesb[:, 0:64], pd[:, 0:128], True, False, [("PD", si), "const"], [dbk])
                    mm(ps[ob][0:64, oc], V[:, pn, kvh * 64:(kvh + 1) * 64], pd[:, 128:256], n == 0, True,
                       [("PD", si), ("V", pn // 4)], [obk])
                    mm(ps[db][0:64, oc], onesb[:, 0:64], pd[:, 128:256], n == 0, True, [("PD", si), "const"], [dbk])
                    if j == 3:
                        ts("dve", RC[0:64, :], ps[db][0:64, :], ES[0:64, h:h + 1], ALU.add, [dbk, "ES"], ["RC"])
                        act(RC[0:64, :], RC[0:64, :], AF.Ln, ["RC"], ["RC"])
                        act(RC[0:64, :], RC[0:64, :], AF.Exp, ["RC"], ["RC"], scale=-1.0)
                        tt("dve", MO[half][0:64, :], ps[ob][0:64, :], RC[0:64, :], ALU.mult, [obk, "RC"], [("MO", half)])
                        if half == 1:
                            ts("dve", MO2[0:64, :], MO[0][0:64, :], SEL[0:64, 0:1], ALU.mult, [("MO", 0), "const"], ["MO2"])
                            stt(MO2[0:64, :], MO[1][0:64, :], SEL[0:64, 1:2], MO2[0:64, :], ALU.mult, ALU.add,
                                [("MO", 1), "const", "MO2"], ["MO2"])
                            row = 3 * 256 + h * 64
                            dma("sp", mixT_d[row:row + 64, js * TS:(js + 1) * TS], MO2[0:64, :], ["MO2"],
                                [("mixT", "D", h, js)])

                LA_D = 1
                for idx in range(len(steps) + LA_D):
                    if idx < len(steps):
                        d_stage_s(steps[idx])
                    if idx - LA_D >= 0:
                        d_stage_rest(steps[idx - LA_D], idx - LA_D)

        def phase2_(l, last):
            AR1.reset()
            AR2.reset()
            AT = AR1.get([128, NFC, TS], BF16)
            MX = AR1.get([128, 8, TS], BF16)
            X1 = AR1.get([128, 4, D], F32)
            WOS = AR2.get([128, 8, D], BF16)
            WST = AR2.get([128, D], F32)
            LNV = [AR2.get([128, D], F32) for _ in range(2)]
            WG = [AR2.get([128, 8, 256], BF16) for _ in range(2)]
            WU = [AR2.get([128, 8, 256], BF16) for _ in range(2)]
            WD = [AR2.get([128, 2, TS], BF16) for _ in range(2)]
            X1T = AR2.get([128, 8, TS], BF16)
            SQ2 = [AR2.get([128, TS], BF16) for _ in range(2)]
            ACCY = AR2.get([128, D], F32)
            U = AR2.get([128, D], F32)
            XB = AR2.get([128, D], F32)
            SG = [AR2.get([128, TS], F32) for _ in range(2)]
            RS = AR2.get([128, 4, 4], F32)
            STAT = AR2.get([128, 2, 6], F32)
            MV = AR2.get([128, 4], F32)

            dma("sp", MG, dd["mix_g"][l].rearrange("(c p) -> p c", p=128), [], ["MG"], allow_slow_non_contiguous=True)
            wo_l = dd["w_o"][l].rearrange("(c p) n -> p c n", p=128)
            for c in range(8):
                dma("sp", WST, wo_l[:, c, :], [], ["WST"])
                ts("dve", WOS[:, c, :], WST, MG[:, c:c + 1], ALU.mult, ["WST", "MG"], ["WOS"])
            wg_l = dd["w_gate"][l].rearrange("(k p) n -> p k n", p=128)
            wu_l = dd["w_up"][l].rearrange("(k p) n -> p k n", p=128)
            wd_l = dd["w_down"][l].rearrange("(c p) n -> p c n", p=128)
            mixT_v = mixT_d.rearrange("(c p) t -> p c t", p=128)
            src_x = dd["xq"] if l == 0 else xs_d
            dst_x = out_d if last else xs_d
            lnvi = [0]

            def load_vec(name):
                i = lnvi[0] % 2
                lnvi[0] += 1
                dma("sp", LNV[i], dd[name][l].partition_broadcast(128), [], [("LNV", i)])
                return LNV[i], ("LNV", i)

            def layernorm(src, srck, dst, dstk, gk, bk_):
                g_ap, gkey = gk
                b_ap, bkey = bk_
                for hf in range(2):
                    P.op("dve", (lambda hh: (lambda e: e.bn_stats(out=STAT[:, hh, :], in_=src[:, hh * 512:(hh + 1) * 512])))(hf),
                         [srck], ["STAT"])
                P.op("dve", lambda e: e.bn_aggr(out=MV[:, 0:2], in_=STAT.rearrange("p a b -> p (a b)")), ["STAT"], ["MV"])
                rsqrt(MV[:, 2:3], MV[:, 1:2], 1e-5, ["MV"], ["MV"])
                ts("dve", MV[:, 3:4], MV[:, 0:1], MV[:, 2:3], ALU.mult, ["MV"], ["MV"], s2=-1.0, op1=ALU.mult)
                act(dst, src, AF.Identity, [srck, "MV"], [dstk], scale=MV[:, 2:3], bias=MV[:, 3:4])
                tt("dve", dst, dst, g_ap, ALU.mult, [dstk, gkey], [dstk])
                tt("dve", dst, dst, b_ap, ALU.add, [dstk, bkey], [dstk])

            def transpose_to(src, srck, dstT, col0, dstk):
                for half in range(2):
                    b, bk = bank("M2", [4, 5, 6, 7])
                    for j in range(4):
                        c = half * 4 + j
                        tr(ps[b][:, j * 128:(j + 1) * 128], src[:, c * 128:(c + 1) * 128], [srck], [bk])
                    cp("act", dstT[:, half * 4:half * 4 + 4, col0:col0 + 128],
                       ps[b].rearrange("p (a b) -> p a b", a=4), [bk], [dstk])

            if S2 <= 1:
                return
            for tile in range(NS):
                mxk = "MX"
                dma("sp", MX, mixT_v[:, :, tile * TS:(tile + 1) * TS],
                    [("mixT", g, h, tile) for g in "ABCD" for h in range(4)], [mxk])
                g1 = load_vec("ln1_g")
                b1 = load_vec("ln1_b")
                for c in range(8):
                    sq = SQ2[c % 2]
                    sqk = ("SQ2", c % 2)
                    tt("dve", sq, MX[:, c, :], MX[:, c, :], ALU.mult, [mxk], [sqk])
                    for j in range(4):
                        mm(ps[4 + j][:, 0:4], sq[:, j * 128:(j + 1) * 128], IND[:, c, :], c == 0, c == 7,
                           [sqk, "const"], [("ps", 4 + j)])
                for j in range(4):
                    ts("dve", RS[:, j, :], ps[4 + j][:, 0:4], 1.0 / 256.0, ALU.mult, [("ps", 4 + j)], ["RS"],
                       s2=1e-6, op1=ALU.add)
                rsqrt(RS.rearrange("p a b -> p (a b)"), RS.rearrange("p a b -> p (a b)"), 0.0, ["RS"], ["RS"])
                if S2 <= 2:
                    continue
                for j in range(4):
                    blk = tile * 4 + j
                    dma("sp", XB, src_x[blk * 128:(blk + 1) * 128, :], [("xs", blk)], ["XB"])
                    for half in range(2):
                        ybs = []
                        for g in range(4):
                            b, bk = bank("Y", [0, 1, 2, 3])
                            ybs.append((b, bk))
                            for c2 in range(2):
                                c = 2 * g + c2
                                mm(ps[b], MX[:, c, j * 128:(j + 1) * 128], WOS[:, c, half * 512:(half + 1) * 512],
                                   c2 == 0, c2 == 1, [mxk, "WOS"], [bk])
                        acc = ACCY[:, half * 512:(half + 1) * 512]
                        ts("dve", acc, ps[ybs[0][0]], RS[:, j, 0:1], ALU.mult, [ybs[0][1], "RS"], ["ACCY"])
                        for g in range(1, 4):
                            stt(acc, ps[ybs[g][0]], RS[:, j, g:g + 1], acc, ALU.mult, ALU.add, [ybs[g][1], "RS", "ACCY"], ["ACCY"])
                    stt(U, XB, ALPHA, ACCY, ALU.mult, ALU.add, ["XB", "ACCY"], ["U"])
                    layernorm(U, "U", X1[:, j, :], ("X1", j), g1, b1)
                    transpose_to(X1[:, j, :], ("X1", j), X1T, j * 128, "X1T")
                if S2 <= 3:
                    continue
                g2 = load_vec("ln2_g")
                b2 = load_vec("ln2_b")
                for sc in range(11):
                    wi = sc % 2
                    dma("pool", WG[wi], wg_l[:, :, sc * 256:(sc + 1) * 256], [], [("WG", wi)])
                    dma("pool", WU[wi], wu_l[:, :, sc * 256:(sc + 1) * 256], [], [("WU", wi)])
                    for j2 in range(2):
                        fc = sc * 2 + j2
                        bg, bkg = bank("G", [0, 1])
                        bu, bku = bank("Uu", [2, 3])
                        for k in range(8):
                            mm(ps[bg], WG[wi][:, k, j2 * 128:(j2 + 1) * 128], X1T[:, k, :], k == 0, k == 7,
                               [("WG", wi), "X1T"], [bkg])
                        for k in range(8):
                            mm(ps[bu], WU[wi][:, k, j2 * 128:(j2 + 1) * 128], X1T[:, k, :], k == 0, k == 7,
                               [("WU", wi), "X1T"], [bku])
                        sg = SG[fc % 2]
                        sgk = ("SG", fc % 2)
                        act(sg, ps[bg], AF.Silu, [bkg], [sgk])
                        tt("dve", AT[:, fc, :], sg, ps[bu], ALU.mult, [sgk, bku], [("AT", fc)])
                if S2 <= 4:
                    continue
                for half in range(2):
                    fb = [4, 5, 6, 7]
                    for sc in range(11):
                        wi = (half * 11 + sc) % 2
                        dma("pool", WD[wi], wd_l[:, sc * 2:sc * 2 + 2, half * 512:(half + 1) * 512], [], [("WD", wi)])
                        for j2 in range(2):
                            fc = sc * 2 + j2
                            for j in range(4):
                                mm(ps[fb[j]], AT[:, fc, j * 128:(j + 1) * 128], WD[wi][:, j2, :], fc == 0, fc == NFC - 1,
                                   [("AT", fc), ("WD", wi)], [("ps", fb[j])])
                    for j in range(4):
                        stt(X1[:, j, half * 512:(half + 1) * 512], X1[:, j, half * 512:(half + 1) * 512], ALPHA,
                            ps[fb[j]], ALU.mult, ALU.add, [("ps", fb[j]), ("X1", j)], [("X1", j)])
                for j in range(4):
                    blk = tile * 4 + j
                    layernorm(X1[:, j, :], ("X1", j), U, "U", g2, b2)
                    o = dma("sp", dst_x[blk * 128:(blk + 1) * 128, :], U, ["U"], [("xs", blk)])
                    if last:
                        all_dma_out.append(o)
                    else:
                        transpose_to(U, "U", X1T, j * 128, "X1T")
                if not last:
                    dma("sp", xtm_f[tile].bitcast(BF16).rearrange("(c p) t -> p c t", p=128), X1T, ["X1T"], [("xtm", tile)])
                    P.cc((lambda jj: (lambda e: e.collective_compute("AllGather", ALU.bypass, replica_groups=rgroups,
                                                                     ins=[xtm_f[jj]], outs=[xta_f[jj]])))(tile),
                         [("xtm", tile)], [("xta", tile)])
                    xta_v = xta_f[tile].bitcast(BF16).rearrange("(r k p) t -> p k r t", r=2, k=8, p=128)
                    for r_ in range(2):
                        dma("sp", XT[:, :, (r_ * 4 + tile) * TS:(r_ * 4 + tile + 1) * TS], xta_v[:, :, r_, :],
                            [("xta", tile)], [("XT", r_ * 4 + tile)])

        build_xt0()
        for l in range(n_layers):
            P.barrier()
            phase1(l)
            P.barrier()
            if phase2:
                phase2_(l, l == n_layers - 1)
        if debug and not phase2:
            pass
        P.barrier()
        P.emit()
    return nc


def kernel(**inputs):
    x = np.ascontiguousarray(np.asarray(inputs["x"], dtype=np.float32))
    nc = build_program()
    base = {nm: np.ascontiguousarray(np.asarray(inputs[nm], dtype=np.float32)) for nm, _ in WNAMES}
    cst = [make_consts(0), make_consts(1)]
    in_maps = []
    for core in range(8):
        bq, r = core // 2, core % 2
        xt = x[bq].reshape(8, 512, D)
        m = dict(base)
        m.update(cst[r])
        m["x"] = np.ascontiguousarray(xt[[nat_tile(p) for p in range(8)]]).reshape(T, D)
        m["xq"] = np.ascontiguousarray(xt[[2 * j + r for j in range(4)]]).reshape(TQ, D)
        in_maps.append(m)
    res = run_bass_kernel_spmd(nc, in_maps, core_ids=list(range(8)))
    out = np.zeros((4, 8, 512, D), np.float32)
    for core in range(8):
        bq, r = core // 2, core % 2
        o = np.asarray(res.results[core]["out"], dtype=np.float32).reshape(4, 512, D)
        for j in range(4):
            out[bq, 2 * j + r] = o[j]
    return out.reshape(4, T, D)
```

```python
import numpy as np
import concourse.bass as bass
import concourse.mybir as mybir
from concourse.bass_utils import run_bass_kernel_spmd
from contextlib import ExitStack

F32 = mybir.dt.float32
BF16 = mybir.dt.bfloat16
AF = mybir.ActivationFunctionType
ALU = mybir.AluOpType
AX = mybir.AxisListType

ENG = ("pe", "act", "dve", "pool", "sp")
NDMASEM = 8


class Op:
    __slots__ = ("eng", "fn", "deps", "is_dma", "signal", "cnt", "semi", "val", "waits", "gid", "is_cc")

    def __init__(self, eng, fn, is_dma):
        self.eng = eng
        self.fn = fn
        self.deps = []
        self.is_dma = is_dma
        self.signal = False
        self.cnt = 0
        self.semi = -1
        self.val = 0
        self.waits = []
        self.is_cc = False


class Prog:
    def __init__(self, nc, stack):
        self.nc = nc
        self.stack = stack
        self.all = []
        self.res = {}
        self.ntens = 0
        self._bar_from = 0

    def sb(self, shape, dt, name=None):
        self.ntens += 1
        return self.stack.enter_context(self.nc.sbuf_tensor("s_" + (name or f"sb{self.ntens}"), list(shape), dt))

    def psum(self, shape, dt, name=None):
        self.ntens += 1
        return self.stack.enter_context(self.nc.psum_tensor(name or f"ps{self.ntens}", list(shape), dt))

    def op(self, eng, fn, reads=(), writes=(), dma=False):
        o = Op(eng, fn, dma)
        deps = o.deps
        res = self.res
        for r in reads:
            st = res.get(r)
            if st is None:
                st = res[r] = [None, []]
            if st[0] is not None:
                deps.append(st[0])
        for w in writes:
            st = res.get(w)
            if st is None:
                st = res[w] = [None, []]
            if st[0] is not None:
                deps.append(st[0])
            deps.extend(st[1])
        for r in reads:
            res[r][1].append(o)
        for w in writes:
            st = res[w]
            st[0] = o
            st[1] = []
        self.all.append(o)
        return o

    def dma(self, eng, out, in_, reads=(), writes=(), **kw):
        return self.op(eng, lambda e: e.dma_start(out=out, in_=in_, **kw), reads, writes, dma=True)

    def cc(self, fn, reads=(), writes=()):
        o = self.op("pool", fn, reads, writes, dma=True)
        o.is_cc = True
        return o

    def wait_all(self, eng, ops):
        o = Op(eng, None, False)
        o.deps = list(ops)
        self.all.append(o)
        return o


    def barrier(self):
        deps = []
        last = {}
        for o in self.all[self._bar_from:]:
            if o.fn is None:
                continue
            if o.is_dma:
                deps.append(o)
            else:
                last[o.eng] = o
        deps.extend(last.values())
        self._bar_from = len(self.all)
        for e in ENG:
            w = Op(e, None, False)
            w.deps = list(deps)
            self.all.append(w)

    def finalize(self):
        nc = self.nc
        for o in self.all:
            nd = []
            seen = set()
            for d in o.deps:
                if id(d) in seen:
                    continue
                seen.add(id(d))
                if d is o:
                    continue
                if not d.is_dma and d.eng == "pe" and o.eng == "pe" and not o.is_dma:
                    continue
                nd.append(d)
                if not d.is_dma:
                    d.signal = True
            o.deps = nd
        cnt = {e: 0 for e in ENG}
        for o in self.all:
            if o.is_dma or o.fn is None:
                continue
            if o.signal:
                cnt[o.eng] += 1
            o.cnt = cnt[o.eng]
        known = {e: {} for e in ENG}
        pool_next = {e: 0 for e in ENG}
        pool_last = {e: [0] * NDMASEM for e in ENG}
        self.nwaits = 0
        ncc = 0
        for o in self.all:
            E = o.eng
            kn = known[E]
            need = {}
            if o.is_cc:
                o.semi = ("cc", ncc)
                o.val = 1
                ncc += 1
            elif o.is_dma:
                s = pool_next[E]
                pool_next[E] = (s + 1) % NDMASEM
                prev = pool_last[E][s]
                key = ("d", E, s)
                if prev > 0 and kn.get(key, 0) < prev:
                    need[key] = prev
                o.semi = (E, s)
                o.val = prev + 16
                pool_last[E][s] = o.val
            for d in o.deps:
                if d.is_dma:
                    key = ("d",) + d.semi
                    v = d.val
                else:
                    key = ("c", d.eng)
                    v = d.cnt
                if kn.get(key, 0) < v and need.get(key, 0) < v:
                    need[key] = v
            for key, v in need.items():
                kn[key] = v
            o.waits = list(need.items())
            self.nwaits += len(need)
        st = self.stack
        self.csem = {e: st.enter_context(nc.semaphore(f"c_{e}")) for e in ENG}
        self.dsem = {(e, s): st.enter_context(nc.semaphore(f"d_{e}{s}")) for e in ("sp", "pool", "act") for s in range(NDMASEM)}
        self.ccsem = [st.enter_context(nc.semaphore(f"cc{i}")) for i in range(ncc)]
        self.byeng = {e: [o for o in self.all if o.eng == e] for e in ENG}

    def _sem(self, key):
        if key[0] == "c":
            return self.csem[key[1]]
        if key[1] == "cc":
            return self.ccsem[key[2]]
        return self.dsem[(key[1], key[2])]

    def emit_engine(self, E, e):
        for o in self.byeng[E]:
            for key, v in o.waits:
                e.wait_ge(self._sem(key), v)
            if o.fn is None:
                continue
            ins = o.fn(e)
            if o.is_cc:
                ins.then_inc(self.ccsem[o.semi[1]])
            elif o.is_dma:
                ins.then_inc(self.dsem[o.semi], 16)
            elif o.signal:
                ins.then_inc(self.csem[E], 1)

    def emit(self):
        nc = self.nc
        self.finalize()
        with nc.Block() as block:
            @block.tensor
            def _(e):
                self.emit_engine("pe", e)

            @block.scalar
            def _(e):
                self.emit_engine("act", e)

            @block.vector
            def _(e):
                self.emit_engine("dve", e)

            @block.gpsimd
            def _(e):
                self.emit_engine("pool", e)

            @block.sync
            def _(e):
                self.emit_engine("sp", e)

D = 1024
T = 4096
NB = 32
NT = 8
TS = 512
DFF = 2816
NFC = 22
INW = 2468
NL = 4
ALPHA = float((2.0 * NL) ** 0.25)
GBASE = {"A": 0, "B": 772, "C": 1188, "D": 1956}
GW = {"A": 772, "B": 416, "C": 768, "D": 512}
MASKV = -30000.0
NS = 4
TQ = NS * 512


def nat_tile(p):
    return 2 * (p % 4) + p // 4


def pos_tile(nt):
    return (nt % 2) * 4 + nt // 2


def pb_nat(n):
    return pos_tile(n // 4) * 4 + n % 4
STAGE = 99
KVF = 0
KVT = 8
VENG = 'dve'
S2 = 99
AMODE = 0
WARM = 0
P2T = 8


def make_consts(rank):
    c = {}
    c["identf"] = np.eye(128, dtype=np.float32)
    p = np.arange(128)[:, None]
    xx = np.arange(896)[None, :]
    ms = np.where((xx - 384) < p, MASKV, 0.0)
    mx = np.where((xx - 384) <= p, MASKV, 0.0)
    full = np.full((128, 896), MASKV)
    zero = np.zeros((128, 896))
    mb = np.zeros((2, 2, 128, 896), np.float32)
    for ti, st in enumerate((ms, mx)):
        if rank == 0:
            mb[ti, 0] = st
            mb[ti, 1] = full
        else:
            mb[ti, 0] = zero
            mb[ti, 1] = st
    c["mbase"] = mb
    sel = np.zeros((128, 2), np.float32)
    sel[:, rank] = 1.0
    c["sel"] = sel
    s = np.arange(128)[:, None]
    t = np.arange(128)[None, :]
    tri = np.zeros((4, 128, 128), np.float32)
    tri[0] = (s <= t)
    tri[1] = 1.0
    tri[2] = -(s > t).astype(np.float32)
    tri[3] = -(s <= t).astype(np.float32)
    c["tri"] = tri
    pos = np.arange(T, dtype=np.float32)
    inv = (np.float32(10000.0) ** (-np.arange(0, 32, 2, dtype=np.float32) / np.float32(32))).astype(np.float32)
    ang = (pos[:, None] * inv[None, :]).astype(np.float32)
    cs = np.cos(ang).astype(np.float32).T
    sn = np.sin(ang).astype(np.float32).T
    rope = np.zeros((2, 32, T), np.float32)
    rope[0, :16] = cs
    rope[0, 16:] = cs
    rope[1, :16] = -sn
    rope[1, 16:] = sn
    r8 = rope.reshape(2, 32, 8, 512)
    c["rope"] = np.ascontiguousarray(r8[:, :, [nat_tile(pp_) for pp_ in range(8)], :]).reshape(2, 32, T)
    c["ropeq"] = np.ascontiguousarray(r8[:, :, [2 * j_ + rank for j_ in range(4)], :]).reshape(2, 32, TQ)
    bd = np.zeros((4, 128, 256), np.float32)
    pp = np.arange(128)[:, None].astype(np.float32)
    cc = np.arange(128)[None, :].astype(np.float32)
    for h in range(4):
        m = np.float32(2.0 ** (-8.0 * (h + 1) / 4.0))
        d1 = 128.0 + cc - pp
        bd[h, :, 0:128] = np.where(cc < pp, -m * d1, MASKV)
        d2 = cc - pp
        bd[h, :, 128:256] = np.where(cc >= pp, -m * d2, MASKV)
    c["biasd"] = bd
    ind = np.zeros((128, 8, 4), np.float32)
    for ch in range(8):
        ind[:, ch, ch // 2] = 1.0
    c["ind"] = ind
    return c


WNAMES = [("w_in", [NL, D, INW]), ("fox_b_f", [NL, 4]), ("mla_g_q", [NL, 256]), ("mla_g_kv", [NL, 128]),
          ("mla_w_uq", [NL, 256, 384]), ("mla_w_ukv", [NL, 128, 512]), ("swa_sinks", [NL, 4]),
          ("mix_g", [NL, D]), ("w_o", [NL, D, D]), ("ln1_g", [NL, D]), ("ln1_b", [NL, D]),
          ("w_gate", [NL, D, DFF]), ("w_up", [NL, D, DFF]), ("w_down", [NL, DFF, D]),
          ("ln2_g", [NL, D]), ("ln2_b", [NL, D])]
CNAMES = [("identf", [128, 128]), ("mbase", [2, 2, 128, 896]), ("tri", [4, 128, 128]), ("rope", [2, 32, T]),
          ("ropeq", [2, 32, TQ]), ("sel", [128, 2]), ("biasd", [4, 128, 256]), ("ind", [128, 8, 4])]


class Arena:
    def __init__(self, P, nbytes, name):
        self.t = P.sb([128, nbytes // 2], BF16, name)
        self.n = nbytes
        self.off = 0

    def reset(self):
        self.off = 0

    def get(self, shape, dt):
        esz = 4 if dt == F32 else 2
        ne = int(np.prod(shape[1:]))
        n = (ne * esz + 63) // 64 * 64
        a = self.t[:, self.off // 2:(self.off + n) // 2]
        self.off += n
        assert self.off <= self.n, ("arena overflow", self.off, self.n)
        if dt == F32:
            a = a.bitcast(F32)
        a = a[:, 0:ne]
        if len(shape) == 3:
            a = a.rearrange("p (a b) -> p a b", a=shape[1])
        elif len(shape) == 4:
            a = a.rearrange("p (a b c) -> p a b c", a=shape[1], b=shape[2])
        return a


def build_program(n_layers=NL, groups="ABCD", debug=False, phase2=True, n_cores=8):
    nc = bass.Bass("TRN2", target_bir_lowering=False)
    dd = {}
    dd["x"] = nc.dram_tensor("x", [T, D], F32, kind="ExternalInput").ap()
    for nm, shp in WNAMES + CNAMES:
        dd[nm] = nc.dram_tensor(nm, list(shp), F32, kind="ExternalInput").ap()
    dd["xq"] = nc.dram_tensor("xq", [TQ, D], F32, kind="ExternalInput").ap()
    out_d = nc.dram_tensor("out", [TQ, D], F32, kind="ExternalOutput").ap()
    xs_d = nc.dram_tensor("xs", [TQ, D], F32, kind="Internal").ap()
    mixT_d = nc.dram_tensor("mixT", [D, TQ], BF16, kind="ExternalOutput" if debug else "Internal").ap()
    xtm_f = [nc.dram_tensor(f"xtm{j_}", [D, TS // 2], F32, kind="Internal").ap() for j_ in range(NS)]
    xta_f = [nc.dram_tensor(f"xta{j_}", [2 * D, TS // 2], F32, kind="Internal").ap() for j_ in range(NS)]
    rgroups = [[2 * g_, 2 * g_ + 1] for g_ in range(n_cores // 2)]

    with ExitStack() as st:
        P = Prog(nc, st)

        def mm(out, lhsT, rhs, start, stop, r, w, **kw):
            P.op("pe", lambda e: e.matmul(out, lhsT=lhsT, rhs=rhs, start=start, stop=stop, **kw), r, w)

        def tr(out, in_, r, w):
            P.op("pe", lambda e: e.transpose(out=out, in_=in_, identity=identf), list(r) + ["const"], w)

        def act(out, in_, func, r, w, scale=None, bias=None):
            kw = {}
            if scale is not None:
                kw["scale"] = scale
            if bias is not None:
                kw["bias"] = bias
            P.op("act", lambda e: e.activation(out=out, in_=in_, func=func, **kw), r, w)

        def tt(eng, out, in0, in1, op, r, w):
            P.op(eng, lambda e: e.tensor_tensor(out=out, in0=in0, in1=in1, op=op), r, w)

        def ts(eng, out, in0, s1, op0, r, w, s2=None, op1=None):
            if op1 is None:
                P.op(eng, lambda e: e.tensor_scalar(out=out, in0=in0, scalar1=s1, scalar2=None, op0=op0), r, w)
            else:
                P.op(eng, lambda e: e.tensor_scalar(out=out, in0=in0, scalar1=s1, scalar2=s2, op0=op0, op1=op1), r, w)

        def stt(out, in0, scalar, in1, op0, op1, r, w):
            P.op("dve", lambda e: e.scalar_tensor_tensor(out=out, in0=in0, scalar=scalar, in1=in1, op0=op0, op1=op1), r, w)

        def cp(eng, out, in_, r, w):
            if eng == "act":
                P.op("act", lambda e: e.activation(out=out, in_=in_, func=AF.Copy), r, w)
            else:
                P.op(eng, lambda e: e.tensor_copy(out=out, in_=in_), r, w)

        def rsqrt(out, in_, eps, r, w):
            act(out, in_, AF.Ln, r, w, bias=eps)
            act(out, out, AF.Exp, list(w), w, scale=-0.5)

        def recip(out, in_, r, w):
            P.op("dve", lambda e: e.reciprocal(out=out, in_=in_), r, w)

        def dma(eng, out, in_, r, w, **kw):
            return P.dma(eng, out, in_, r, w, **kw)

        ps = [P.psum([128, 512], F32, f"bank{i}")[:] for i in range(8)]
        rot = {}

        def bank(pool, lst):
            i = rot.get(pool, 0)
            rot[pool] = i + 1
            b = lst[i % len(lst)]
            return b, ("ps", b)

        XT = P.sb([128, 8, T], BF16, "XT")[:]
        AR1 = Arena(P, 49152, "AR1")
        AR2 = Arena(P, 80 * 1024, "AR2")
        identf = P.sb([128, 128], F32, "identf")[:]
        identb = P.sb([128, 128], BF16, "identb")[:]
        onesb = P.sb([128, 128], BF16, "onesb")[:]
        MB = P.sb([128, 4, 896], BF16, "MB")[:]
        TRI = P.sb([128, 4, 128], F32, "TRI")[:]
        TRIB = P.sb([128, 2, 128], BF16, "TRIB")[:]
        BIASD = P.sb([128, 4, 256], F32, "BIASD")[:]
        IND = P.sb([128, 8, 4], BF16, "IND")[:]
        SMALL = P.sb([128, 64], F32, "SMALL")[:]
        BF_ = SMALL[:, 0:4]
        ES = SMALL[:, 4:8]
        GQ = SMALL[:, 8:10]
        GKV = SMALL[:, 10:11]
        MG = SMALL[:, 16:24]
        SEL = SMALL[:, 24:26]
        CARRYX = P.sb([128, NB, 4], F32, "CARRYX")[:]

        dma("sp", identf, dd["identf"], [], ["const"])
        dma("pool", identb, dd["identf"], [], ["const"])
        dma("pool", onesb, dd["tri"][1], [], ["const"])
        dma("pool", MB, dd["mbase"].rearrange("t w p c -> p (t w) c"), [], ["const"])
        dma("sp", SEL, dd["sel"], [], ["const"])
        dma("sp", TRI, dd["tri"].rearrange("i p c -> p i c"), [], ["const"])
        dma("pool", TRIB, dd["tri"][2:4].rearrange("i p c -> p i c"), [], ["const"])
        dma("sp", BIASD, dd["biasd"].rearrange("h p c -> p h c"), [], ["const"])
        dma("pool", IND, dd["ind"], [], ["const"])
        P.op("dve", lambda e: e.memset(CARRYX[:, 0, :], 0.0), [], ["carry0"])

        all_dma_out = []

        def build_xt0():
            AR2.reset()
            XB = [AR2.get([128, D], F32) for _ in range(2)]
            for blk in range(NB):
                xb = XB[blk % 2]
                k = ("XB0", blk % 2)
                dma("sp", xb, dd["x"][blk * 128:(blk + 1) * 128, :], [], [k])
                for half in range(2):
                    b, bk = bank("M", list(range(8)))
                    for j in range(4):
                        c = half * 4 + j
                        tr(ps[b][:, j * 128:(j + 1) * 128], xb[:, c * 128:(c + 1) * 128], [k], [bk])
                    cp("act" if half else "dve", XT[:, half * 4:half * 4 + 4, blk * 128:(blk + 1) * 128],
                       ps[b].rearrange("p (a b) -> p a b", a=4), [bk], [("XT", blk // 4)])

        def phase1(l):
            AR1.reset()
            AR2.reset()
            KT = [AR1.get([128, T], BF16) for _ in range(4)]
            V = AR1.get([128, NB, 256], BF16)
            QT = [[AR2.get([128, TS], BF16) for _ in range(2)] for _ in range(4)]
            WIN = AR2.get([128, 8, 772], BF16)
            PT = [AR2.get([128, TS], BF16) for _ in range(3)]
            CT = [[AR2.get([128, TS], F32) for _ in range(2)] for _ in range(2)]
            RC = AR2.get([128, TS], F32)
            MO = [AR2.get([128, TS], BF16) for _ in range(2)]
            GATE = AR2.get([128, NB, 4], F32)
            TOTs = AR2.get([128, NB, 4], F32)
            CNEG = AR2.get([128, NB, 4], F32)
            BT = [AR2.get([128, NB, 4], F32) for _ in range(2)]
            WUQ = AR2.get([128, 2, 384], BF16)
            WUQS = AR2.get([128, 2, 384], BF16)
            WUKV = AR2.get([128, 512], BF16)
            WKRS = AR2.get([128, 8, 96], BF16)
            SQ = AR2.get([128, 2, TS], BF16)
            RSB = AR2.get([128, TS], F32)
            CQN = AR2.get([128, 2, TS], BF16)
            CKVN = AR2.get([128, TS], BF16)
            ROPE = AR2.get([128, 2, TS], F32)
            RT1 = AR2.get([128, TS], F32)
            RT2 = AR2.get([128, TS], F32)
            SBD = [AR2.get([128, 256], F32) for _ in range(2)]
            PD = [AR2.get([128, 256], BF16) for _ in range(2)]
            XTQ = AR2.get([128, 8, TS], BF16)
            SPB2 = [[AR2.get([128, TS], BF16) for _ in range(2)] for _ in range(2)]
            T1B = [[CT[0][1], AR2.get([128, TS], F32)], [CT[1][1], AR2.get([128, TS], F32)]]
            MO2 = AR2.get([128, TS], BF16)

            def make_xtq(j):
                ts("dve", XTQ, XT[:, :, j * TS:(j + 1) * TS], SEL[:, 0:1], ALU.mult, [("XT", j), "const"], ["XTQ"])
                stt(XTQ, XT[:, :, (4 + j) * TS:(5 + j) * TS], SEL[:, 1:2], XTQ, ALU.mult, ALU.add,
                    [("XT", 4 + j), "const", "XTQ"], ["XTQ"])

            w_in_l = dd["w_in"][l].rearrange("(k p) n -> p k n", p=128)
            ALLB = list(range(8))
            mo_cnt = [0]

            def load_win(g):
                dma("pool", WIN[:, :, 0:GW[g]], w_in_l[:, :, GBASE[g]:GBASE[g] + GW[g]], [], ["WIN"])

            def xt_keys(tile):
                return [("XT", tile)]

            def proj_fm(dst, dk, col0, tile, eng, wkeys, dkeys, pool="M", banks=None, lw=None, own=False):
                b, bk = bank(pool, banks or [7])
                W = WIN if lw is None else lw
                for k in range(8):
                    if own:
                        mm(ps[b][0:dk, :], W[:, k, col0:col0 + dk], XTQ[:, k, :], k == 0, k == 7, ["XTQ"] + wkeys, [bk])
                    else:
                        mm(ps[b][0:dk, :], W[:, k, col0:col0 + dk], XT[:, k, tile * TS:(tile + 1) * TS], k == 0, k == 7,
                           xt_keys(tile) + wkeys, [bk])
                if dst is not None:
                    cp(eng, dst, ps[b][0:dk, :], [bk], dkeys)
                return b, bk

            def store_mix(g, h, qt, src_bank, bk, rc_ap, extra_r):
                i = mo_cnt[0] % 2
                mo_cnt[0] += 1
                mo = MO[i]
                mk = ("MO", i)
                if rc_ap is None:
                    cp("dve", mo[0:64, :], ps[src_bank][0:64, :], [bk], [mk])
                else:
                    tt("dve", mo[0:64, :], ps[src_bank][0:64, :], rc_ap, ALU.mult, [bk] + extra_r, [mk])
                row = "ABCD".index(g) * 256 + h * 64
                o = dma("sp", mixT_d[row:row + 64, qt * TS:(qt + 1) * TS], mo[0:64, :], [mk], [("mixT", g, h, qt)])

            def softmax_unit(g, h, qt, dk, scale, qap, qkeys, bias_fn, bias_keys, u):
                kbl = [4 * p_ + i_ for p_ in list(range(qt + 1)) + list(range(4, 4 + qt + 1)) for i_ in range(4)]
                nkb = len(kbl)
                ob = [3, 4][u % 2]
                db = [5, 6][u % 2]
                obk, dbk = ("ps", ob), ("ps", db)
                LA = 2
                sb_of = {}
                for step in range(nkb + LA):
                    if step < nkb:
                        kb = kbl[step]
                        b, bk = bank("S", [0, 1, 2])
                        sb_of[step] = (b, bk)
                        p_, i = kb // 4, kb % 4
                        wh = 0 if p_ == qt else (1 if p_ == 4 + qt else -1)
                        mm(ps[b], KT[h][0:dk, kb * 128:(kb + 1) * 128], qap, True, wh < 0,
                           [("KT", h, kb // 4)] + qkeys, [bk])
                        if wh >= 0:
                            mm(ps[b], identb, MB[:, wh, 384 - 128 * i:384 - 128 * i + 512], False, True, ["const"], [bk])
                    s2 = step - LA
                    if 0 <= s2 < nkb:
                        kb = kbl[s2]
                        b, bk = sb_of.pop(s2)
                        pi = s2 % 3
                        pt, pk = PT[pi], ("PT", pi)
                        act(pt, ps[b], AF.Exp, [bk] + bias_keys, [pk], scale=scale,
                            bias=(bias_fn(kb) if bias_fn else None))
                        mm(ps[ob][0:64, :], V[:, kb, h * 64:(h + 1) * 64], pt, s2 == 0, s2 == nkb - 1,
                           [pk, ("V", kb // 4)], [obk])
                        mm(ps[db][0:64, :], onesb[:, 0:64], pt, s2 == 0, s2 == nkb - 1, [pk, "const"], [dbk])
                recip(RC[0:64, :], ps[db][0:64, :], [dbk], ["RC"])
                store_mix(g, h, qt, ob, obk, RC[0:64, :], ["RC"])

            def sb_units(hs, qt, qaps, qkeys):
                order = [pos_tile(nt_) * 4 + i_ for nt_ in reversed(range(2 * qt + 2)) for i_ in reversed(range(4))]
                nkb = len(order)
                scale = 0.125
                zb = [[0, 1], [4, 5]]
                accb = [2, 6]
                obb = [3, 7]
                z_of = {}
                for step in range(nkb + 2):
                    if step < nkb:
                        for si, h in enumerate(hs):
                            kb = order[step]
                            b = zb[si][step % 2]
                            bk = ("ps", b)
                            z_of[(si, step)] = (b, bk)
                            p_, i = kb // 4, kb % 4
                            wh = 0 if p_ == qt else (1 if p_ == 4 + qt else -1)
                            mm(ps[b], KT[h][0:64, kb * 128:(kb + 1) * 128], qaps[si], True, wh < 0,
                               [("KT", h, kb // 4)] + qkeys[si], [bk])
                            if wh >= 0:
                                mm(ps[b], identb, MB[:, 2 + wh, 384 - 128 * i:384 - 128 * i + 512], False, True,
                                   ["const"], [bk])
                    s1 = step - 1
                    if 0 <= s1 < nkb:
                        par = s1 % 2
                        for si, h in enumerate(hs):
                            b, bk = z_of[(si, s1)]
                            act(CT[si][0], ps[b], AF.Exp, [bk], [("SP", si)], scale=scale)
                        for si, h in enumerate(hs):
                            act(SPB2[si][par], CT[si][0], AF.Ln, [("SP", si)], [("SPB", si, par)], bias=1.0)
                        for si, h in enumerate(hs):
                            b, bk = z_of.pop((si, s1))
                            stt(T1B[si][par], ps[b], scale, SPB2[si][par], ALU.mult, ALU.subtract,
                                [bk, ("SPB", si, par)], [("T1", si, par)])
                    s2 = step - 2
                    if 0 <= s2 < nkb:
                        par = s2 % 2
                        kb = order[s2]
                        for si, h in enumerate(hs):
                            mm(ps[accb[si]], TRIB[:, 0, :], SPB2[si][par], s2 == 0, True, [("SPB", si, par), "const"],
                               [("ps", accb[si])], skip_group_check=True)
                        for si, h in enumerate(hs):
                            tt("dve", T1B[si][par], ps[accb[si]], T1B[si][par], ALU.add,
                               [("ps", accb[si]), ("T1", si, par)], [("T1", si, par)])
                        if s2 < nkb - 1:
                            for si, h in enumerate(hs):
                                mm(ps[accb[si]], TRIB[:, 1, :], SPB2[si][par], False, True, [("SPB", si, par), "const"],
                                   [("ps", accb[si])], skip_group_check=True)
                        pts = []
                        for si, h in enumerate(hs):
                            pi = (s2 * 2 + si) % 3
                            pt, pk = PT[pi], ("PT", pi)
                            pts.append((pt, pk))
                            act(pt, T1B[si][par], AF.Exp, [("T1", si, par)], [pk])
                        for si, h in enumerate(hs):
                            pt, pk = pts[si]
                            mm(ps[obb[si]][0:64, :], V[:, kb, h * 64:(h + 1) * 64], pt, s2 == 0, s2 == nkb - 1,
                               [pk, ("V", kb // 4)], [("ps", obb[si])])
                for si, h in enumerate(hs):
                    store_mix("C", h, qt, obb[si], ("ps", obb[si]), None, [])

            def kv_pass(g, kcols, vcol0, nv, gates):
                KVB = list(range(7)) if gates else ALLB
                for tile in range(NT):
                    for hp in range(len(kcols) // 2):
                        b, bk = proj_fm(None, 128, kcols[2 * hp], tile, None, ["WIN"], None, banks=KVB)
                        cp("dve", KT[2 * hp][0:64, tile * TS:(tile + 1) * TS], ps[b][0:64, :], [bk], [("KT", 2 * hp, tile)])
                        P.op("act", (lambda o_, i_: (lambda e: e.activation(out=o_, in_=i_, func=AF.Copy)))(
                            KT[2 * hp + 1][0:64, tile * TS:(tile + 1) * TS], ps[b][64:128, :]), [bk], [("KT", 2 * hp + 1, tile)])
                    for j in range(4):
                        blk = tile * 4 + j
                        b, bk = bank("M", KVB)
                        n = nv
                        for k in range(8):
                            mm(ps[b][:, 0:n], XT[:, k, blk * 128:(blk + 1) * 128], WIN[:, k, vcol0:vcol0 + n],
                               k == 0, k == 7, xt_keys(tile) + ["WIN"], [bk])
                        cp("dve", V[:, blk, 0:nv], ps[b][:, 0:nv], [bk], [("V", tile)])
                        if gates:
                            for k in range(8):
                                mm(ps[7][:, blk * 4:(blk + 1) * 4], XT[:, k, blk * 128:(blk + 1) * 128],
                                   WIN[:, k, vcol0 + nv:vcol0 + nv + 4], k == 0, k == 7, xt_keys(tile) + ["WIN"],
                                   [("ps", 7)], skip_group_check=True)
                if gates:
                    cp("dve", GATE.rearrange("p a b -> p (a b)"), ps[7][:, 0:128], [("ps", 7)], ["GATE"])

            if "A" in groups:
                load_win("A")
                dma("sp", BF_, dd["fox_b_f"][l].partition_broadcast(128), [], ["BF"])
                kv_pass("A", [256, 320, 384, 448], 512, 256, AMODE != 1)
                if AMODE in (0, 3):
                    for h in range(4):
                        ts("dve", GATE[:, :, h], GATE[:, :, h], BF_[:, h:h + 1], ALU.add, ["GATE", "BF"], ["GATE"])
                    GF = GATE.rearrange("p a b -> p (a b)")
                    act(GF, GF, AF.Exp, ["GATE"], ["GATE"], scale=-1.0)
                    act(GF, GF, AF.Ln, ["GATE"], ["GATE"], bias=1.0)
                    if False:
                        pass
                    b1, bk1 = bank("M", ALLB)
                    mm(ps[b1][:, 0:128], TRI[:, 1, :], GF, True, True, ["GATE", "const"], [bk1])
                    b2, bk2 = bank("M", ALLB)
                    mm(ps[b2][:, 0:128], TRI[:, 0, :], GF, True, True, ["GATE", "const"], [bk2])
                    cp("dve", TOTs.rearrange("p a b -> p (a b)"), ps[b1][:, 0:128], [bk1], ["TOTs"])
                    if False:
                        pass
                    for n_ in range(NB - 1):
                        tt("dve", CARRYX[:, pb_nat(n_ + 1), :], CARRYX[:, pb_nat(n_), :], TOTs[:, pb_nat(n_), :], ALU.add,
                           ["TOTs", "carry0", "CARRYX"], ["CARRYX"])
                    tt("dve", CNEG, ps[b2][:, 0:128].rearrange("p (a b) -> p a b", a=NB), CARRYX, ALU.add,
                       [bk2, "CARRYX", "carry0"], ["CNEG"])
                    if False:
                        pass
                    u = 0
                    for qt in range(NS):
                        bt = BT[qt % 2]
                        btk = ("BT", qt % 2)
                        for h in range(4):
                            ts("dve", bt[:, :, h], CNEG[:, :, h], CARRYX[:, (4 + qt) * 4, h:h + 1], ALU.subtract,
                               ["CNEG", "CARRYX", "carry0"], [btk])
                        make_xtq(qt)
                        for hp in range(2):
                            b_, bk_ = proj_fm(None, 128, hp * 128, qt, None, ["WIN"], None, own=True)
                            cp("dve", QT[2 * hp][qt % 2][0:64, :], ps[b_][0:64, :], [bk_], [("QT", 2 * hp, qt % 2)])
                            P.op("act", (lambda o_, i_: (lambda e: e.activation(out=o_, in_=i_, func=AF.Copy)))(
                                QT[2 * hp + 1][qt % 2][0:64, :], ps[b_][64:128, :]), [bk_], [("QT", 2 * hp + 1, qt % 2)])
                        for h in range(4):
                            qb = qt % 2
                            qk = ("QT", h, qb)
                            softmax_unit("A", h, qt, 64, 0.125, QT[h][qb][0:64, :], [qk],
                                         (lambda kb, hh=h, bb=bt: bb[:, kb, hh:hh + 1]), [btk], u)
                            u += 1

            if "B" in groups:
                P.barrier()
                load_win("B")
                wb = GBASE["B"]
                uq = dd["mla_w_uq"][l].rearrange("(c p) n -> p c n", p=128)
                dma("pool", WUQ, uq, [], ["WUQ"])
                uq4 = uq.rearrange("p c (h x) -> p c h x", h=4)
                wq4 = WUQS.rearrange("p c (h x) -> p c h x", h=4)
                for c in range(2):
                    dma("pool", wq4[:, c, :, 0:64], uq4[:, c, :, 0:64], [], ["WUQS"])
                    dma("pool", wq4[:, c, :, 64:80], uq4[:, c, :, 80:96], [], ["WUQS"])
                    dma("pool", wq4[:, c, :, 80:96], uq4[:, c, :, 64:80], [], ["WUQS"])
                dma("pool", WUKV, dd["mla_w_ukv"][l], [], ["WUKV"])
                dma("pool", WKRS[:, :, 0:64], w_in_l[:, :, wb + 320:wb + 384], [], ["WKRS"])
                dma("pool", WKRS[:, :, 64:80], w_in_l[:, :, wb + 400:wb + 416], [], ["WKRS"])
                dma("pool", WKRS[:, :, 80:96], w_in_l[:, :, wb + 384:wb + 400], [], ["WKRS"])
                dma("sp", GQ, dd["mla_g_q"][l].rearrange("(c p) -> p c", p=128), [], ["GQ"], allow_slow_non_contiguous=True)
                dma("sp", GKV, dd["mla_g_kv"][l].rearrange("(c p) -> p c", p=128), [], ["GKV"], allow_slow_non_contiguous=True)
                ts("dve", GQ, GQ, 16.0, ALU.mult, ["GQ"], ["GQ"])
                ts("dve", GKV, GKV, float(np.sqrt(128.0)), ALU.mult, ["GKV"], ["GKV"])
                WUKV4 = WUKV.rearrange("p (h x) -> p h x", h=4)

                def load_rope(tile, own=False):
                    src_ = dd["ropeq"] if own else dd["rope"]
                    dma("sp", ROPE[64:96, 0, :], src_[0, :, tile * TS:(tile + 1) * TS], [], ["ROPE"])
                    dma("sp", ROPE[64:96, 1, :], src_[1, :, tile * TS:(tile + 1) * TS], [], ["ROPE"])

                def rope_apply(ba, bka, bb, bkb, dsts, dkeys):
                    tt("dve", RT1[64:96, :], ps[ba][64:96, :], ROPE[64:96, 0, :], ALU.mult, [bka, "ROPE"], ["RT1"])
                    tt("dve", RT2[64:96, :], ps[bb][64:96, :], ROPE[64:96, 1, :], ALU.mult, [bkb, "ROPE"], ["RT2"])
                    for dst, dk_ in zip(dsts, dkeys):
                        tt("dve", dst, RT1[64:96, :], RT2[64:96, :], ALU.add, ["RT1", "RT2"], [dk_])

                for tile in range(NT):
                    ba, bka = proj_fm(None, 128, 256, tile, None, ["WIN"], None, banks=ALLB)
                    act(SQ[:, 0, :], ps[ba], AF.Square, [bka], ["SQ"])
                    bs, bks = bank("M", ALLB)
                    mm(ps[bs], onesb, SQ[:, 0, :], True, True, ["SQ", "const"], [bks])
                    rsqrt(RSB, ps[bs], 128.0 * 1e-6, [bks], ["RSB"])
                    stt(CKVN, ps[ba], GKV[:, 0:1], RSB, ALU.mult, ALU.mult, [bka, "RSB", "GKV"], ["CKVN"])
                    for h in range(4):
                        b, bk = bank("M", ALLB)
                        mm(ps[b][0:64, :], WUKV[:, h * 128:h * 128 + 64], CKVN, True, True, ["CKVN", "WUKV"], [bk])
                        cp("dve", KT[h][0:64, tile * TS:(tile + 1) * TS], ps[b][0:64, :], [bk], [("KT", h, tile)])
                    for j in range(4):
                        blk = tile * 4 + j
                        b, bk = bank("M", ALLB)
                        mm(ps[b][:, 0:256].rearrange("p (h x) -> p h x", h=4), CKVN[:, j * 128:(j + 1) * 128],
                           WUKV4[:, :, 64:128], True, True, ["CKVN", "WUKV"], [bk])
                        cp("dve", V[:, blk, :], ps[b][:, 0:256], [bk], [("V", tile)])
                    bc, bkc = proj_fm(None, 96, 320, tile, None, ["WIN"], None, banks=ALLB)
                    bd_, bkd = proj_fm(None, 96, 0, tile, None, ["WKRS"], None, banks=ALLB, lw=WKRS)
                    load_rope(tile)
                    rope_apply(bc, bkc, bd_, bkd, [KT[h][64:96, tile * TS:(tile + 1) * TS] for h in range(4)],
                               [("KT", h, tile) for h in range(4)])
                u = 0
                sc_b = float(96.0 ** -0.5)
                for qt in range(NS):
                    make_xtq(qt)
                    bq = []
                    for c in range(2):
                        bq.append(proj_fm(None, 128, c * 128, qt, None, ["WIN"], None, banks=[7, 0, 1, 2], own=True))
                        act(SQ[:, c, :], ps[bq[c][0]], AF.Square, [bq[c][1]], ["SQ"])
                    bs, bks = bank("M", [7, 0, 1, 2])
                    for c in range(2):
                        mm(ps[bs], onesb, SQ[:, c, :], c == 0, c == 1, ["SQ", "const"], [bks])
                    rsqrt(RSB, ps[bs], 256.0 * 1e-6, [bks], ["RSB"])
                    for c in range(2):
                        stt(CQN[:, c, :], ps[bq[c][0]], GQ[:, c:c + 1], RSB, ALU.mult, ALU.mult,
                            [bq[c][1], "RSB", "GQ"], ["CQN"])
                    load_rope(qt, own=True)
                    for h in range(4):
                        qb = u % 2
                        qk = ("QT", h, qb)
                        be, bke = bank("M", [7, 0, 1, 2])
                        for c in range(2):
                            mm(ps[be][0:96, :], WUQ[:, c, h * 96:(h + 1) * 96], CQN[:, c, :], c == 0, c == 1,
                               ["CQN", "WUQ"], [bke])
                        bf_, bkf = bank("M", [7, 0, 1, 2])
                        for c in range(2):
                            mm(ps[bf_][0:96, :], WUQS[:, c, h * 96:(h + 1) * 96], CQN[:, c, :], c == 0, c == 1,
                               ["CQN", "WUQS"], [bkf])
                        cp("dve", QT[h][qb][0:64, :], ps[be][0:64, :], [bke], [qk])
                        rope_apply(be, bke, bf_, bkf, [QT[h][qb][64:96, :]], [qk])
                        softmax_unit("B", h, qt, 96, sc_b, QT[h][qb][0:96, :], [qk], None, [], u)
                        u += 1

            if "C" in groups:
                P.barrier()
                load_win("C")
                kv_pass("C", [256, 320, 384, 448], 512, 256, False)
                for qt in range(NS):
                    make_xtq(qt)
                    for pair in range(2):
                        hs = [2 * pair, 2 * pair + 1]
                        qaps, qkeys = [], []
                        qb = qt % 2
                        b_, bk_ = proj_fm(None, 128, pair * 128, qt, None, ["WIN"], None, banks=[0, 1, 4, 5], own=True)
                        cp("dve", QT[hs[0]][qb][0:64, :], ps[b_][0:64, :], [bk_], [("QT", hs[0], qb)])
                        P.op("act", (lambda o_, i_: (lambda e: e.activation(out=o_, in_=i_, func=AF.Copy)))(
                            QT[hs[1]][qb][0:64, :], ps[b_][64:128, :]), [bk_], [("QT", hs[1], qb)])
                        for h in hs:
                            qk = ("QT", h, qb)
                            qaps.append(QT[h][qb][0:64, :])
                            qkeys.append([qk])
                        sb_units(hs, qt, qaps, qkeys)

            if "D" in groups:
                P.barrier()
                load_win("D")
                dma("sp", ES, dd["swa_sinks"][l].partition_broadcast(128), [], ["ES"])
                act(ES, ES, AF.Exp, ["ES"], ["ES"])
                kv_pass("D", [256, 320], 384, 128, False)
                steps = [(js, h, half, j) for js in range(NS) for h in range(4) for half in range(2) for j in range(4)]
                sstate = {}

                def d_unit(js, h, half):
                    return (js * 4 + h) * 2 + half

                def d_stage_s(st_):
                    js, h, half, j = st_
                    u = d_unit(js, h, half)
                    kvh = h // 2
                    nt = 2 * js + half
                    qb = u % 2
                    qk = ("QT", h, qb)
                    if j == 0:
                        proj_fm(QT[h][qb][0:64, :], 64, h * 64, pos_tile(nt), "dve", ["WIN"], [qk])
                    n = 4 * nt + j
                    pn = pb_nat(n)
                    pm = pb_nat(n - 1) if n > 0 else 0
                    b, bk = bank("S", [0, 1, 2])
                    qa = QT[h][qb][0:64, j * 128:(j + 1) * 128]
                    if n > 0:
                        mm(ps[b][:, 0:128], KT[kvh][0:64, pm * 128:(pm + 1) * 128], qa, True, True,
                           [("KT", kvh, pm // 4), qk], [bk])
                    mm(ps[b][:, 128:256], KT[kvh][0:64, pn * 128:(pn + 1) * 128], qa, True, True,
                       [("KT", kvh, pn // 4), qk], [bk])
                    sstate[st_] = (b, bk)

                def d_stage_rest(st_, idx):
                    js, h, half, j = st_
                    u = d_unit(js, h, half)
                    kvh = h // 2
                    nt = 2 * js + half
                    n = 4 * nt + j
                    pn = pb_nat(n)
                    pm = pb_nat(n - 1) if n > 0 else 0
                    lo = 0 if n > 0 else 128
                    b, bk = sstate.pop(st_)
                    ob = [3, 4][u % 2]
                    db = [5, 6][u % 2]
                    obk, dbk = ("ps", ob), ("ps", db)
                    si = idx % 2
                    sbd, pd = SBD[si], PD[si]
                    stt(sbd[:, lo:256], ps[b][:, lo:256], 0.125, BIASD[:, h, lo:256], ALU.mult, ALU.add,
                        [bk, "const"], [("SBD", si)])
                    act(pd[:, lo:256], sbd[:, lo:256], AF.Exp, [("SBD", si)], [("PD", si)])
                    oc = slice(j * 128, (j + 1) * 128)
                    if n > 0:
                        mm(ps[ob][0:64, oc], V[:, pm, kvh * 64:(kvh + 1) * 64], pd[:, 0:128], True, False,
                           [("PD", si), ("V", pm // 4)], [obk])
                        mm(ps[db][0:64, oc], onesb[:, 0:64], pd[:, 0:128], True, False, [("PD", si), "const"], [dbk])
                    mm(ps[ob][0:64, oc], V[:, pn, kvh * 64:(kvh + 1) * 64], pd[:, 128:256], n == 0, True,
                       [("PD", si), ("V", pn // 4)], [obk])
                    mm(ps[db][0:64, oc], onesb[:, 0:64], pd[:, 128:256], n == 0, True, [("PD", si), "const"], [dbk])
                    if j == 3:
                        ts("dve", RC[0:64, :], ps[db][0:64, :], ES[0:64, h:h + 1], ALU.add, [dbk, "ES"], ["RC"])
                        act(RC[0:64, :], RC[0:64, :], AF.Ln, ["RC"], ["RC"])
                        act(RC[0:64, :], RC[0:64, :], AF.Exp, ["RC"], ["RC"], scale=-1.0)
                        tt("dve", MO[half][0:64, :], ps[ob][0:64, :], RC[0:64, :], ALU.mult, [obk, "RC"], [("MO", half)])
                        if half == 1:
                            ts("dve", MO2[0:64, :], MO[0][0:64, :], SEL[0:64, 0:1], ALU.mult, [("MO", 0), "const"], ["MO2"])
                            stt(MO2[0:64, :], MO[1][0:64, :], SEL[0:64, 1:2], MO2[0:64, :], ALU.mult, ALU.add,
                                [("MO", 1), "const", "MO2"], ["MO2"])
                            row = 3 * 256 + h * 64
                            dma("sp", mixT_d[row:row + 64, js * TS:(js + 1) * TS], MO2[0:64, :], ["MO2"],
                                [("mixT", "D", h, js)])

                LA_D = 1
                for idx in range(len(steps) + LA_D):
                    if idx < len(steps):
                        d_stage_s(steps[idx])
                    if idx - LA_D >= 0:
                        d_stage_rest(steps[idx - LA_D], idx - LA_D)

        def phase2_(l, last):
            AR1.reset()
            AR2.reset()
            AT = AR1.get([128, NFC, TS], BF16)
            MX = AR1.get([128, 8, TS], BF16)
            X1 = AR1.get([128, 4, D], F32)
            WOS = AR2.get([128, 8, D], BF16)
            WST = AR2.get([128, D], F32)
            LNV = [AR2.get([128, D], F32) for _ in range(2)]
            WG = [AR2.get([128, 8, 256], BF16) for _ in range(2)]
            WU = [AR2.get([128, 8, 256], BF16) for _ in range(2)]
            WD = [AR2.get([128, 2, TS], BF16) for _ in range(2)]
            X1T = AR2.get([128, 8, TS], BF16)
            SQ2 = [AR2.get([128, TS], BF16) for _ in range(2)]
            ACCY = AR2.get([128, D], F32)
            U = AR2.get([128, D], F32)
            XB = AR2.get([128, D], F32)
            SG = [AR2.get([128, TS], F32) for _ in range(2)]
            RS = AR2.get([128, 4, 4], F32)
            STAT = AR2.get([128, 2, 6], F32)
            MV = AR2.get([128, 4], F32)

            dma("sp", MG, dd["mix_g"][l].rearrange("(c p) -> p c", p=128), [], ["MG"], allow_slow_non_contiguous=True)
            wo_l = dd["w_o"][l].rearrange("(c p) n -> p c n", p=128)
            for c in range(8):
                dma("sp", WST, wo_l[:, c, :], [], ["WST"])
                ts("dve", WOS[:, c, :], WST, MG[:, c:c + 1], ALU.mult, ["WST", "MG"], ["WOS"])
            wg_l = dd["w_gate"][l].rearrange("(k p) n -> p k n", p=128)
            wu_l = dd["w_up"][l].rearrange("(k p) n -> p k n", p=128)
            wd_l = dd["w_down"][l].rearrange("(c p) n -> p c n", p=128)
            mixT_v = mixT_d.rearrange("(c p) t -> p c t", p=128)
            src_x = dd["xq"] if l == 0 else xs_d
            dst_x = out_d if last else xs_d
            lnvi = [0]

            def load_vec(name):
                i = lnvi[0] % 2
                lnvi[0] += 1
                dma("sp", LNV[i], dd[name][l].partition_broadcast(128), [], [("LNV", i)])
                return LNV[i], ("LNV", i)

            def layernorm(src, srck, dst, dstk, gk, bk_):
                g_ap, gkey = gk
                b_ap, bkey = bk_
                for hf in range(2):
                    P.op("dve", (lambda hh: (lambda e: e.bn_stats(out=STAT[:, hh, :], in_=src[:, hh * 512:(hh + 1) * 512])))(hf),
                         [srck], ["STAT"])
                P.op("dve", lambda e: e.bn_aggr(out=MV[:, 0:2], in_=STAT.rearrange("p a b -> p (a b)")), ["STAT"], ["MV"])
                rsqrt(MV[:, 2:3], MV[:, 1:2], 1e-5, ["MV"], ["MV"])
                ts("dve", MV[:, 3:4], MV[:, 0:1], MV[:, 2:3], ALU.mult, ["MV"], ["MV"], s2=-1.0, op1=ALU.mult)
                act(dst, src, AF.Identity, [srck, "MV"], [dstk], scale=MV[:, 2:3], bias=MV[:, 3:4])
                tt("dve", dst, dst, g_ap, ALU.mult, [dstk, gkey], [dstk])
                tt("dve", dst, dst, b_ap, ALU.add, [dstk, bkey], [dstk])

            def transpose_to(src, srck, dstT, col0, dstk):
                for half in range(2):
                    b, bk = bank("M2", [4, 5, 6, 7])
                    for j in range(4):
                        c = half * 4 + j
                        tr(ps[b][:, j * 128:(j + 1) * 128], src[:, c * 128:(c + 1) * 128], [srck], [bk])
                    cp("act", dstT[:, half * 4:half * 4 + 4, col0:col0 + 128],
                       ps[b].rearrange("p (a b) -> p a b", a=4), [bk], [dstk])

            if S2 <= 1:
                return
            for tile in range(NS):
                mxk = "MX"
                dma("sp", MX, mixT_v[:, :, tile * TS:(tile + 1) * TS],
                    [("mixT", g, h, tile) for g in "ABCD" for h in range(4)], [mxk])
                g1 = load_vec("ln1_g")
                b1 = load_vec("ln1_b")
                for c in range(8):
                    sq = SQ2[c % 2]
                    sqk = ("SQ2", c % 2)
                    tt("dve", sq, MX[:, c, :], MX[:, c, :], ALU.mult, [mxk], [sqk])
                    for j in range(4):
                        mm(ps[4 + j][:, 0:4], sq[:, j * 128:(j + 1) * 128], IND[:, c, :], c == 0, c == 7,
                           [sqk, "const"], [("ps", 4 + j)])
                for j in range(4):
                    ts("dve", RS[:, j, :], ps[4 + j][:, 0:4], 1.0 / 256.0, ALU.mult, [("ps", 4 + j)], ["RS"],
                       s2=1e-6, op1=ALU.add)
                rsqrt(RS.rearrange("p a b -> p (a b)"), RS.rearrange("p a b -> p (a b)"), 0.0, ["RS"], ["RS"])
                if S2 <= 2:
                    continue
                for j in range(4):
                    blk = tile * 4 + j
                    dma("sp", XB, src_x[blk * 128:(blk + 1) * 128, :], [("xs", blk)], ["XB"])
                    for half in range(2):
                        ybs = []
                        for g in range(4):
                            b, bk = bank("Y", [0, 1, 2, 3])
                            ybs.append((b, bk))
                            for c2 in range(2):
                                c = 2 * g + c2
                                mm(ps[b], MX[:, c, j * 128:(j + 1) * 128], WOS[:, c, half * 512:(half + 1) * 512],
                                   c2 == 0, c2 == 1, [mxk, "WOS"], [bk])
                        acc = ACCY[:, half * 512:(half + 1) * 512]
                        ts("dve", acc, ps[ybs[0][0]], RS[:, j, 0:1], ALU.mult, [ybs[0][1], "RS"], ["ACCY"])
                        for g in range(1, 4):
                            stt(acc, ps[ybs[g][0]], RS[:, j, g:g + 1], acc, ALU.mult, ALU.add, [ybs[g][1], "RS", "ACCY"], ["ACCY"])
                    stt(U, XB, ALPHA, ACCY, ALU.mult, ALU.add, ["XB", "ACCY"], ["U"])
                    layernorm(U, "U", X1[:, j, :], ("X1", j), g1, b1)
                    transpose_to(X1[:, j, :], ("X1", j), X1T, j * 128, "X1T")
                if S2 <= 3:
                    continue
                g2 = load_vec("ln2_g")
                b2 = load_vec("ln2_b")
                for sc in range(11):
                    wi = sc % 2
                    dma("pool", WG[wi], wg_l[:, :, sc * 256:(sc + 1) * 256], [], [("WG", wi)])
                    dma("pool", WU[wi], wu_l[:, :, sc * 256:(sc + 1) * 256], [], [("WU", wi)])
                    for j2 in range(2):
                        fc = sc * 2 + j2
                        bg, bkg = bank("G", [0, 1])
                        bu, bku = bank("Uu", [2, 3])
                        for k in range(8):
                            mm(ps[bg], WG[wi][:, k, j2 * 128:(j2 + 1) * 128], X1T[:, k, :], k == 0, k == 7,
                               [("WG", wi), "X1T"], [bkg])
                        for k in range(8):
                            mm(ps[bu], WU[wi][:, k, j2 * 128:(j2 + 1) * 128], X1T[:, k, :], k == 0, k == 7,
                               [("WU", wi), "X1T"], [bku])
                        sg = SG[fc % 2]
                        sgk = ("SG", fc % 2)
                        act(sg, ps[bg], AF.Silu, [bkg], [sgk])
                        tt("dve", AT[:, fc, :], sg, ps[bu], ALU.mult, [sgk, bku], [("AT", fc)])
                if S2 <= 4:
                    continue
                for half in range(2):
                    fb = [4, 5, 6, 7]
                    for sc in range(11):
                        wi = (half * 11 + sc) % 2
                        dma("pool", WD[wi], wd_l[:, sc * 2:sc * 2 + 2, half * 512:(half + 1) * 512], [], [("WD", wi)])
                        for j2 in range(2):
                            fc = sc * 2 + j2
                            for j in range(4):
                                mm(ps[fb[j]], AT[:, fc, j * 128:(j + 1) * 128], WD[wi][:, j2, :], fc == 0, fc == NFC - 1,
                                   [("AT", fc), ("WD", wi)], [("ps", fb[j])])
                    for j in range(4):
                        stt(X1[:, j, half * 512:(half + 1) * 512], X1[:, j, half * 512:(half + 1) * 512], ALPHA,
                            ps[fb[j]], ALU.mult, ALU.add, [("ps", fb[j]), ("X1", j)], [("X1", j)])
                for j in range(4):
                    blk = tile * 4 + j
                    layernorm(X1[:, j, :], ("X1", j), U, "U", g2, b2)
                    o = dma("sp", dst_x[blk * 128:(blk + 1) * 128, :], U, ["U"], [("xs", blk)])
                    if last:
                        all_dma_out.append(o)
                    else:
                        transpose_to(U, "U", X1T, j * 128, "X1T")
                if not last:
                    dma("sp", xtm_f[tile].bitcast(BF16).rearrange("(c p) t -> p c t", p=128), X1T, ["X1T"], [("xtm", tile)])
                    P.cc((lambda jj: (lambda e: e.collective_compute("AllGather", ALU.bypass, replica_groups=rgroups,
                                                                     ins=[xtm_f[jj]], outs=[xta_f[jj]])))(tile),
                         [("xtm", tile)], [("xta", tile)])
                    xta_v = xta_f[tile].bitcast(BF16).rearrange("(r k p) t -> p k r t", r=2, k=8, p=128)
                    for r_ in range(2):
                        dma("sp", XT[:, :, (r_ * 4 + tile) * TS:(r_ * 4 + tile + 1) * TS], xta_v[:, :, r_, :],
                            [("xta", tile)], [("XT", r_ * 4 + tile)])

        build_xt0()
        for l in range(n_layers):
            P.barrier()
            phase1(l)
            P.barrier()
            if phase2:
                phase2_(l, l == n_layers - 1)
        if debug and not phase2:
            pass
        P.barrier()
        P.emit()
    return nc


def kernel(**inputs):
    x = np.ascontiguousarray(np.asarray(inputs["x"], dtype=np.float32))
    nc = build_program()
    base = {nm: np.ascontiguousarray(np.asarray(inputs[nm], dtype=np.float32)) for nm, _ in WNAMES}
    cst = [make_consts(0), make_consts(1)]
    in_maps = []
    for core in range(8):
        bq, r = core // 2, core % 2
        xt = x[bq].reshape(8, 512, D)
        m = dict(base)
        m.update(cst[r])
        m["x"] = np.ascontiguousarray(xt[[nat_tile(p) for p in range(8)]]).reshape(T, D)
        m["xq"] = np.ascontiguousarray(xt[[2 * j + r for j in range(4)]]).reshape(TQ, D)
        in_maps.append(m)
    res = run_bass_kernel_spmd(nc, in_maps, core_ids=list(range(8)))
    out = np.zeros((4, 8, 512, D), np.float32)
    for core in range(8):
        bq, r = core // 2, core % 2
        o = np.asarray(res.results[core]["out"], dtype=np.float32).reshape(4, 512, D)
        for j in range(4):
            out[bq, 2 * j + r] = o[j]
    return out.reshape(4, T, D)
```

```python
import numpy as np
import concourse.bass as bass
import concourse.mybir as mybir
from concourse.bass_utils import run_bass_kernel_spmd
from contextlib import ExitStack

F32 = mybir.dt.float32
BF16 = mybir.dt.bfloat16
AF = mybir.ActivationFunctionType
ALU = mybir.AluOpType
AX = mybir.AxisListType

ENG = ("pe", "act", "dve", "pool", "sp")
NDMASEM = 8


class Op:
    __slots__ = ("eng", "fn", "deps", "is_dma", "signal", "cnt", "semi", "val", "waits", "gid", "is_cc")

    def __init__(self, eng, fn, is_dma):
        self.eng = eng
        self.fn = fn
        self.deps = []
        self.is_dma = is_dma
        self.signal = False
        self.cnt = 0
        self.semi = -1
        self.val = 0
        self.waits = []
        self.is_cc = False


class Prog:
    def __init__(self, nc, stack):
        self.nc = nc
        self.stack = stack
        self.all = []
        self.res = {}
        self.ntens = 0
        self._bar_from = 0

    def sb(self, shape, dt, name=None):
        self.ntens += 1
        return self.stack.enter_context(self.nc.sbuf_tensor("s_" + (name or f"sb{self.ntens}"), list(shape), dt))

    def psum(self, shape, dt, name=None):
        self.ntens += 1
        return self.stack.enter_context(self.nc.psum_tensor(name or f"ps{self.ntens}", list(shape), dt))

    def op(self, eng, fn, reads=(), writes=(), dma=False):
        o = Op(eng, fn, dma)
        deps = o.deps
        res = self.res
        for r in reads:
            st = res.get(r)
            if st is None:
                st = res[r] = [None, []]
            if st[0] is not None:
                deps.append(st[0])
        for w in writes:
            st = res.get(w)
            if st is None:
                st = res[w] = [None, []]
            if st[0] is not None:
                deps.append(st[0])
            deps.extend(st[1])
        for r in reads:
            res[r][1].append(o)
        for w in writes:
            st = res[w]
            st[0] = o
            st[1] = []
        self.all.append(o)
        return o

    def dma(self, eng, out, in_, reads=(), writes=(), **kw):
        return self.op(eng, lambda e: e.dma_start(out=out, in_=in_, **kw), reads, writes, dma=True)

    def cc(self, fn, reads=(), writes=()):
        o = self.op("pool", fn, reads, writes, dma=True)
        o.is_cc = True
        return o

    def wait_all(self, eng, ops):
        o = Op(eng, None, False)
        o.deps = list(ops)
        self.all.append(o)
        return o


    def barrier(self):
        deps = []
        last = {}
        for o in self.all[self._bar_from:]:
            if o.fn is None:
                continue
            if o.is_dma:
                deps.append(o)
            else:
                last[o.eng] = o
        deps.extend(last.values())
        self._bar_from = len(self.all)
        for e in ENG:
            w = Op(e, None, False)
            w.deps = list(deps)
            self.all.append(w)

    def finalize(self):
        nc = self.nc
        for o in self.all:
            nd = []
            seen = set()
            for d in o.deps:
                if id(d) in seen:
                    continue
                seen.add(id(d))
                if d is o:
                    continue
                if not d.is_dma and d.eng == "pe" and o.eng == "pe" and not o.is_dma:
                    continue
                nd.append(d)
                if not d.is_dma:
                    d.signal = True
            o.deps = nd
        cnt = {e: 0 for e in ENG}
        for o in self.all:
            if o.is_dma or o.fn is None:
                continue
            if o.signal:
                cnt[o.eng] += 1
            o.cnt = cnt[o.eng]
        known = {e: {} for e in ENG}
        pool_next = {e: 0 for e in ENG}
        pool_last = {e: [0] * NDMASEM for e in ENG}
        self.nwaits = 0
        ncc = 0
        for o in self.all:
            E = o.eng
            kn = known[E]
            need = {}
            if o.is_cc:
                o.semi = ("cc", ncc)
                o.val = 1
                ncc += 1
            elif o.is_dma:
                s = pool_next[E]
                pool_next[E] = (s + 1) % NDMASEM
                prev = pool_last[E][s]
                key = ("d", E, s)
                if prev > 0 and kn.get(key, 0) < prev:
                    need[key] = prev
                o.semi = (E, s)
                o.val = prev + 16
                pool_last[E][s] = o.val
            for d in o.deps:
                if d.is_dma:
                    key = ("d",) + d.semi
                    v = d.val
                else:
                    key = ("c", d.eng)
                    v = d.cnt
                if kn.get(key, 0) < v and need.get(key, 0) < v:
                    need[key] = v
            for key, v in need.items():
                kn[key] = v
            o.waits = list(need.items())
            self.nwaits += len(need)
        st = self.stack
        self.csem = {e: st.enter_context(nc.semaphore(f"c_{e}")) for e in ENG}
        self.dsem = {(e, s): st.enter_context(nc.semaphore(f"d_{e}{s}")) for e in ("sp", "pool", "act") for s in range(NDMASEM)}
        self.ccsem = [st.enter_context(nc.semaphore(f"cc{i}")) for i in range(ncc)]
        self.byeng = {e: [o for o in self.all if o.eng == e] for e in ENG}

    def _sem(self, key):
        if key[0] == "c":
            return self.csem[key[1]]
        if key[1] == "cc":
            return self.ccsem[key[2]]
        return self.dsem[(key[1], key[2])]

    def emit_engine(self, E, e):
        for o in self.byeng[E]:
            for key, v in o.waits:
                e.wait_ge(self._sem(key), v)
            if o.fn is None:
                continue
            ins = o.fn(e)
            if o.is_cc:
                ins.then_inc(self.ccsem[o.semi[1]])
            elif o.is_dma:
                ins.then_inc(self.dsem[o.semi], 16)
            elif o.signal:
                ins.then_inc(self.csem[E], 1)

    def emit(self):
        nc = self.nc
        self.finalize()
        with nc.Block() as block:
            @block.tensor
            def _(e):
                self.emit_engine("pe", e)

            @block.scalar
            def _(e):
                self.emit_engine("act", e)

            @block.vector
            def _(e):
                self.emit_engine("dve", e)

            @block.gpsimd
            def _(e):
                self.emit_engine("pool", e)

            @block.sync
            def _(e):
                self.emit_engine("sp", e)

D = 1024
T = 4096
NB = 32
NT = 8
TS = 512
DFF = 2816
NFC = 22
INW = 2468
NL = 4
ALPHA = float((2.0 * NL) ** 0.25)
GBASE = {"A": 0, "B": 772, "C": 1188, "D": 1956}
GW = {"A": 772, "B": 416, "C": 768, "D": 512}
MASKV = -30000.0
NS = 4
TQ = NS * 512


def nat_tile(p):
    return 2 * (p % 4) + p // 4


def pos_tile(nt):
    return (nt % 2) * 4 + nt // 2


def pb_nat(n):
    return pos_tile(n // 4) * 4 + n % 4
STAGE = 99
KVF = 0
KVT = 8
VENG = 'dve'
S2 = 99
AMODE = 0
WARM = 0
P2T = 8


def make_consts(rank):
    c = {}
    c["identf"] = np.eye(128, dtype=np.float32)
    p = np.arange(128)[:, None]
    xx = np.arange(896)[None, :]
    ms = np.where((xx - 384) < p, MASKV, 0.0)
    mx = np.where((xx - 384) <= p, MASKV, 0.0)
    full = np.full((128, 896), MASKV)
    zero = np.zeros((128, 896))
    mb = np.zeros((2, 2, 128, 896), np.float32)
    for ti, st in enumerate((ms, mx)):
        if rank == 0:
            mb[ti, 0] = st
            mb[ti, 1] = full
        else:
            mb[ti, 0] = zero
            mb[ti, 1] = st
    c["mbase"] = mb
    sel = np.zeros((128, 2), np.float32)
    sel[:, rank] = 1.0
    c["sel"] = sel
    s = np.arange(128)[:, None]
    t = np.arange(128)[None, :]
    tri = np.zeros((4, 128, 128), np.float32)
    tri[0] = (s <= t)
    tri[1] = 1.0
    tri[2] = -(s > t).astype(np.float32)
    tri[3] = -(s <= t).astype(np.float32)
    c["tri"] = tri
    pos = np.arange(T, dtype=np.float32)
    inv = (np.float32(10000.0) ** (-np.arange(0, 32, 2, dtype=np.float32) / np.float32(32))).astype(np.float32)
    ang = (pos[:, None] * inv[None, :]).astype(np.float32)
    cs = np.cos(ang).astype(np.float32).T
    sn = np.sin(ang).astype(np.float32).T
    rope = np.zeros((2, 32, T), np.float32)
    rope[0, :16] = cs
    rope[0, 16:] = cs
    rope[1, :16] = -sn
    rope[1, 16:] = sn
    r8 = rope.reshape(2, 32, 8, 512)
    c["rope"] = np.ascontiguousarray(r8[:, :, [nat_tile(pp_) for pp_ in range(8)], :]).reshape(2, 32, T)
    c["ropeq"] = np.ascontiguousarray(r8[:, :, [2 * j_ + rank for j_ in range(4)], :]).reshape(2, 32, TQ)
    bd = np.zeros((4, 128, 256), np.float32)
    pp = np.arange(128)[:, None].astype(np.float32)
    cc = np.arange(128)[None, :].astype(np.float32)
    for h in range(4):
        m = np.float32(2.0 ** (-8.0 * (h + 1) / 4.0))
        d1 = 128.0 + cc - pp
        bd[h, :, 0:128] = np.where(cc < pp, -m * d1, MASKV)
        d2 = cc - pp
        bd[h, :, 128:256] = np.where(cc >= pp, -m * d2, MASKV)
    c["biasd"] = bd
    ind = np.zeros((128, 8, 4), np.float32)
    for ch in range(8):
        ind[:, ch, ch // 2] = 1.0
    c["ind"] = ind
    return c


WNAMES = [("w_in", [NL, D, INW]), ("fox_b_f", [NL, 4]), ("mla_g_q", [NL, 256]), ("mla_g_kv", [NL, 128]),
          ("mla_w_uq", [NL, 256, 384]), ("mla_w_ukv", [NL, 128, 512]), ("swa_sinks", [NL, 4]),
          ("mix_g", [NL, D]), ("w_o", [NL, D, D]), ("ln1_g", [NL, D]), ("ln1_b", [NL, D]),
          ("w_gate", [NL, D, DFF]), ("w_up", [NL, D, DFF]), ("w_down", [NL, DFF, D]),
          ("ln2_g", [NL, D]), ("ln2_b", [NL, D])]
CNAMES = [("identf", [128, 128]), ("mbase", [2, 2, 128, 896]), ("tri", [4, 128, 128]), ("rope", [2, 32, T]),
          ("ropeq", [2, 32, TQ]), ("sel", [128, 2]), ("biasd", [4, 128, 256]), ("ind", [128, 8, 4])]


class Arena:
    def __init__(self, P, nbytes, name):
        self.t = P.sb([128, nbytes // 2], BF16, name)
        self.n = nbytes
        self.off = 0

    def reset(self):
        self.off = 0

    def get(self, shape, dt):
        esz = 4 if dt == F32 else 2
        ne = int(np.prod(shape[1:]))
        n = (ne * esz + 63) // 64 * 64
        a = self.t[:, self.off // 2:(self.off + n) // 2]
        self.off += n
        assert self.off <= self.n, ("arena overflow", self.off, self.n)
        if dt == F32:
            a = a.bitcast(F32)
        a = a[:, 0:ne]
        if len(shape) == 3:
            a = a.rearrange("p (a b) -> p a b", a=shape[1])
        elif len(shape) == 4:
            a = a.rearrange("p (a b c) -> p a b c", a=shape[1], b=shape[2])
        return a


def build_program(n_layers=NL, groups="ABCD", debug=False, phase2=True, n_cores=8):
    nc = bass.Bass("TRN2", target_bir_lowering=False)
    dd = {}
    dd["x"] = nc.dram_tensor("x", [T, D], F32, kind="ExternalInput").ap()
    for nm, shp in WNAMES + CNAMES:
        dd[nm] = nc.dram_tensor(nm, list(shp), F32, kind="ExternalInput").ap()
    dd["xq"] = nc.dram_tensor("xq", [TQ, D], F32, kind="ExternalInput").ap()
    out_d = nc.dram_tensor("out", [TQ, D], F32, kind="ExternalOutput").ap()
    xs_d = nc.dram_tensor("xs", [TQ, D], F32, kind="Internal").ap()
    mixT_d = nc.dram_tensor("mixT", [D, TQ], BF16, kind="ExternalOutput" if debug else "Internal").ap()
    xtm_f = [nc.dram_tensor(f"xtm{j_}", [D, TS // 2], F32, kind="Internal").ap() for j_ in range(NS)]
    xta_f = [nc.dram_tensor(f"xta{j_}", [2 * D, TS // 2], F32, kind="Internal").ap() for j_ in range(NS)]
    rgroups = [[2 * g_, 2 * g_ + 1] for g_ in range(n_cores // 2)]

    with ExitStack() as st:
        P = Prog(nc, st)

        def mm(out, lhsT, rhs, start, stop, r, w, **kw):
            P.op("pe", lambda e: e.matmul(out, lhsT=lhsT, rhs=rhs, start=start, stop=stop, **kw), r, w)

        def tr(out, in_, r, w):
            P.op("pe", lambda e: e.transpose(out=out, in_=in_, identity=identf), list(r) + ["const"], w)

        def act(out, in_, func, r, w, scale=None, bias=None):
            kw = {}
            if scale is not None:
                kw["scale"] = scale
            if bias is not None:
                kw["bias"] = bias
            P.op("act", lambda e: e.activation(out=out, in_=in_, func=func, **kw), r, w)

        def tt(eng, out, in0, in1, op, r, w):
            P.op(eng, lambda e: e.tensor_tensor(out=out, in0=in0, in1=in1, op=op), r, w)

        def ts(eng, out, in0, s1, op0, r, w, s2=None, op1=None):
            if op1 is None:
                P.op(eng, lambda e: e.tensor_scalar(out=out, in0=in0, scalar1=s1, scalar2=None, op0=op0), r, w)
            else:
                P.op(eng, lambda e: e.tensor_scalar(out=out, in0=in0, scalar1=s1, scalar2=s2, op0=op0, op1=op1), r, w)

        def stt(out, in0, scalar, in1, op0, op1, r, w):
            P.op("dve", lambda e: e.scalar_tensor_tensor(out=out, in0=in0, scalar=scalar, in1=in1, op0=op0, op1=op1), r, w)

        def cp(eng, out, in_, r, w):
            if eng == "act":
                P.op("act", lambda e: e.activation(out=out, in_=in_, func=AF.Copy), r, w)
            else:
                P.op(eng, lambda e: e.tensor_copy(out=out, in_=in_), r, w)

        def rsqrt(out, in_, eps, r, w):
            act(out, in_, AF.Ln, r, w, bias=eps)
            act(out, out, AF.Exp, list(w), w, scale=-0.5)

        def recip(out, in_, r, w):
            P.op("dve", lambda e: e.reciprocal(out=out, in_=in_), r, w)

        def dma(eng, out, in_, r, w, **kw):
            return P.dma(eng, out, in_, r, w, **kw)

        ps = [P.psum([128, 512], F32, f"bank{i}")[:] for i in range(8)]
        rot = {}

        def bank(pool, lst):
            i = rot.get(pool, 0)
            rot[pool] = i + 1
            b = lst[i % len(lst)]
            return b, ("ps", b)

        XT = P.sb([128, 8, T], BF16, "XT")[:]
        AR1 = Arena(P, 49152, "AR1")
        AR2 = Arena(P, 80 * 1024, "AR2")
        identf = P.sb([128, 128], F32, "identf")[:]
        identb = P.sb([128, 128], BF16, "identb")[:]
        onesb = P.sb([128, 128], BF16, "onesb")[:]
        MB = P.sb([128, 4, 896], BF16, "MB")[:]
        TRI = P.sb([128, 4, 128], F32, "TRI")[:]
        TRIB = P.sb([128, 2, 128], BF16, "TRIB")[:]
        BIASD = P.sb([128, 4, 256], F32, "BIASD")[:]
        IND = P.sb([128, 8, 4], BF16, "IND")[:]
        SMALL = P.sb([128, 64], F32, "SMALL")[:]
        BF_ = SMALL[:, 0:4]
        ES = SMALL[:, 4:8]
        GQ = SMALL[:, 8:10]
        GKV = SMALL[:, 10:11]
        MG = SMALL[:, 16:24]
        SEL = SMALL[:, 24:26]
        CARRYX = P.sb([128, NB, 4], F32, "CARRYX")[:]

        dma("sp", identf, dd["identf"], [], ["const"])
        dma("pool", identb, dd["identf"], [], ["const"])
        dma("pool", onesb, dd["tri"][1], [], ["const"])
        dma("pool", MB, dd["mbase"].rearrange("t w p c -> p (t w) c"), [], ["const"])
        dma("sp", SEL, dd["sel"], [], ["const"])
        dma("sp", TRI, dd["tri"].rearrange("i p c -> p i c"), [], ["const"])
        dma("pool", TRIB, dd["tri"][2:4].rearrange("i p c -> p i c"), [], ["const"])
        dma("sp", BIASD, dd["biasd"].rearrange("h p c -> p h c"), [], ["const"])
        dma("pool", IND, dd["ind"], [], ["const"])
        P.op("dve", lambda e: e.memset(CARRYX[:, 0, :], 0.0), [], ["carry0"])

        all_dma_out = []

        def build_xt0():
            AR2.reset()
            XB = [AR2.get([128, D], F32) for _ in range(2)]
            for blk in range(NB):
                xb = XB[blk % 2]
                k = ("XB0", blk % 2)
                dma("sp", xb, dd["x"][blk * 128:(blk + 1) * 128, :], [], [k])
                for half in range(2):
                    b, bk = bank("M", list(range(8)))
                    for j in range(4):
                        c = half * 4 + j
                        tr(ps[b][:, j * 128:(j + 1) * 128], xb[:, c * 128:(c + 1) * 128], [k], [bk])
                    cp("act" if half else "dve", XT[:, half * 4:half * 4 + 4, blk * 128:(blk + 1) * 128],
                       ps[b].rearrange("p (a b) -> p a b", a=4), [bk], [("XT", blk // 4)])

        def phase1(l):
            AR1.reset()
            AR2.reset()
            KT = [AR1.get([128, T], BF16) for _ in range(4)]
            V = AR1.get([128, NB, 256], BF16)
            QT = [[AR2.get([128, TS], BF16) for _ in range(2)] for _ in range(4)]
            WIN = AR2.get([128, 8, 772], BF16)
            PT = [AR2.get([128, TS], BF16) for _ in range(3)]
            CT = [[AR2.get([128, TS], F32) for _ in range(2)] for _ in range(2)]
            RC = AR2.get([128, TS], F32)
            MO = [AR2.get([128, TS], BF16) for _ in range(2)]
            GATE = AR2.get([128, NB, 4], F32)
            TOTs = AR2.get([128, NB, 4], F32)
            CNEG = AR2.get([128, NB, 4], F32)
            BT = [AR2.get([128, NB, 4], F32) for _ in range(2)]
            WUQ = AR2.get([128, 2, 384], BF16)
            WUQS = AR2.get([128, 2, 384], BF16)
            WUKV = AR2.get([128, 512], BF16)
            WKRS = AR2.get([128, 8, 96], BF16)
            SQ = AR2.get([128, 2, TS], BF16)
            RSB = AR2.get([128, TS], F32)
            CQN = AR2.get([128, 2, TS], BF16)
            CKVN = AR2.get([128, TS], BF16)
            ROPE = AR2.get([128, 2, TS], F32)
            RT1 = AR2.get([128, TS], F32)
            RT2 = AR2.get([128, TS], F32)
            SBD = [AR2.get([128, 256], F32) for _ in range(2)]
            PD = [AR2.get([128, 256], BF16) for _ in range(2)]
            XTQ = AR2.get([128, 8, TS], BF16)
            SPB2 = [[AR2.get([128, TS], BF16) for _ in range(2)] for _ in range(2)]
            T1B = [[CT[0][1], AR2.get([128, TS], F32)], [CT[1][1], AR2.get([128, TS], F32)]]
            MO2 = AR2.get([128, TS], BF16)

            def make_xtq(j):
                ts("dve", XTQ, XT[:, :, j * TS:(j + 1) * TS], SEL[:, 0:1], ALU.mult, [("XT", j), "const"], ["XTQ"])
                stt(XTQ, XT[:, :, (4 + j) * TS:(5 + j) * TS], SEL[:, 1:2], XTQ, ALU.mult, ALU.add,
                    [("XT", 4 + j), "const", "XTQ"], ["XTQ"])

            w_in_l = dd["w_in"][l].rearrange("(k p) n -> p k n", p=128)
            ALLB = list(range(8))
            mo_cnt = [0]

            def load_win(g):
                dma("pool", WIN[:, :, 0:GW[g]], w_in_l[:, :, GBASE[g]:GBASE[g] + GW[g]], [], ["WIN"])

            def xt_keys(tile):
                return [("XT", tile)]

            def proj_fm(dst, dk, col0, tile, eng, wkeys, dkeys, pool="M", banks=None, lw=None, own=False):
                b, bk = bank(pool, banks or [7])
                W = WIN if lw is None else lw
                for k in range(8):
                    if own:
                        mm(ps[b][0:dk, :], W[:, k, col0:col0 + dk], XTQ[:, k, :], k == 0, k == 7, ["XTQ"] + wkeys, [bk])
                    else:
                        mm(ps[b][0:dk, :], W[:, k, col0:col0 + dk], XT[:, k, tile * TS:(tile + 1) * TS], k == 0, k == 7,
                           xt_keys(tile) + wkeys, [bk])
                if dst is not None:
                    cp(eng, dst, ps[b][0:dk, :], [bk], dkeys)
                return b, bk

            def store_mix(g, h, qt, src_bank, bk, rc_ap, extra_r):
                i = mo_cnt[0] % 2
                mo_cnt[0] += 1
                mo = MO[i]
                mk = ("MO", i)
                if rc_ap is None:
                    cp("dve", mo[0:64, :], ps[src_bank][0:64, :], [bk], [mk])
                else:
                    tt("dve", mo[0:64, :], ps[src_bank][0:64, :], rc_ap, ALU.mult, [bk] + extra_r, [mk])
                row = "ABCD".index(g) * 256 + h * 64
                o = dma("sp", mixT_d[row:row + 64, qt * TS:(qt + 1) * TS], mo[0:64, :], [mk], [("mixT", g, h, qt)])

            def softmax_unit(g, h, qt, dk, scale, qap, qkeys, bias_fn, bias_keys, u):
                kbl = [4 * p_ + i_ for p_ in list(range(qt + 1)) + list(range(4, 4 + qt + 1)) for i_ in range(4)]
                nkb = len(kbl)
                ob = [3, 4][u % 2]
                db = [5, 6][u % 2]
                obk, dbk = ("ps", ob), ("ps", db)
                LA = 2
                sb_of = {}
                for step in range(nkb + LA):
                    if step < nkb:
                        kb = kbl[step]
                        b, bk = bank("S", [0, 1, 2])
                        sb_of[step] = (b, bk)
                        p_, i = kb // 4, kb % 4
                        wh = 0 if p_ == qt else (1 if p_ == 4 + qt else -1)
                        mm(ps[b], KT[h][0:dk, kb * 128:(kb + 1) * 128], qap, True, wh < 0,
                           [("KT", h, kb // 4)] + qkeys, [bk])
                        if wh >= 0:
                            mm(ps[b], identb, MB[:, wh, 384 - 128 * i:384 - 128 * i + 512], False, True, ["const"], [bk])
                    s2 = step - LA
                    if 0 <= s2 < nkb:
                        kb = kbl[s2]
                        b, bk = sb_of.pop(s2)
                        pi = s2 % 3
                        pt, pk = PT[pi], ("PT", pi)
                        act(pt, ps[b], AF.Exp, [bk] + bias_keys, [pk], scale=scale,
                            bias=(bias_fn(kb) if bias_fn else None))
                        mm(ps[ob][0:64, :], V[:, kb, h * 64:(h + 1) * 64], pt, s2 == 0, s2 == nkb - 1,
                           [pk, ("V", kb // 4)], [obk])
                        mm(ps[db][0:64, :], onesb[:, 0:64], pt, s2 == 0, s2 == nkb - 1, [pk, "const"], [dbk])
                recip(RC[0:64, :], ps[db][0:64, :], [dbk], ["RC"])
                store_mix(g, h, qt, ob, obk, RC[0:64, :], ["RC"])

            def sb_units(hs, qt, qaps, qkeys):
                order = [pos_tile(nt_) * 4 + i_ for nt_ in reversed(range(2 * qt + 2)) for i_ in reversed(range(4))]
                nkb = len(order)
                scale = 0.125
                zb = [[0, 1], [4, 5]]
                accb = [2, 6]
                obb = [3, 7]
                z_of = {}
                for step in range(nkb + 2):
                    if step < nkb:
                        for si, h in enumerate(hs):
                            kb = order[step]
                            b = zb[si][step % 2]
                            bk = ("ps", b)
                            z_of[(si, step)] = (b, bk)
                            p_, i = kb // 4, kb % 4
                            wh = 0 if p_ == qt else (1 if p_ == 4 + qt else -1)
                            mm(ps[b], KT[h][0:64, kb * 128:(kb + 1) * 128], qaps[si], True, wh < 0,
                               [("KT", h, kb // 4)] + qkeys[si], [bk])
                            if wh >= 0:
                                mm(ps[b], identb, MB[:, 2 + wh, 384 - 128 * i:384 - 128 * i + 512], False, True,
                                   ["const"], [bk])
                    s1 = step - 1
                    if 0 <= s1 < nkb:
                        par = s1 % 2
                        for si, h in enumerate(hs):
                            b, bk = z_of[(si, s1)]
                            act(CT[si][0], ps[b], AF.Exp, [bk], [("SP", si)], scale=scale)
                        for si, h in enumerate(hs):
                            act(SPB2[si][par], CT[si][0], AF.Ln, [("SP", si)], [("SPB", si, par)], bias=1.0)
                        for si, h in enumerate(hs):
                            b, bk = z_of.pop((si, s1))
                            stt(T1B[si][par], ps[b], scale, SPB2[si][par], ALU.mult, ALU.subtract,
                                [bk, ("SPB", si, par)], [("T1", si, par)])
                    s2 = step - 2
                    if 0 <= s2 < nkb:
                        par = s2 % 2
                        kb = order[s2]
                        for si, h in enumerate(hs):
                            mm(ps[accb[si]], TRIB[:, 0, :], SPB2[si][par], s2 == 0, True, [("SPB", si, par), "const"],
                               [("ps", accb[si])], skip_group_check=True)
                        for si, h in enumerate(hs):
                            tt("dve", T1B[si][par], ps[accb[si]], T1B[si][par], ALU.add,
                               [("ps", accb[si]), ("T1", si, par)], [("T1", si, par)])
                        if s2 < nkb - 1:
                            for si, h in enumerate(hs):
                                mm(ps[accb[si]], TRIB[:, 1, :], SPB2[si][par], False, True, [("SPB", si, par), "const"],
                                   [("ps", accb[si])], skip_group_check=True)
                        pts = []
                        for si, h in enumerate(hs):
                            pi = (s2 * 2 + si) % 3
                            pt, pk = PT[pi], ("PT", pi)
                            pts.append((pt, pk))
                            act(pt, T1B[si][par], AF.Exp, [("T1", si, par)], [pk])
                        for si, h in enumerate(hs):
                            pt, pk = pts[si]
                            mm(ps[obb[si]][0:64, :], V[:, kb, h * 64:(h + 1) * 64], pt, s2 == 0, s2 == nkb - 1,
                               [pk, ("V", kb // 4)], [("ps", obb[si])])
                for si, h in enumerate(hs):
                    store_mix("C", h, qt, obb[si], ("ps", obb[si]), None, [])

            def kv_pass(g, kcols, vcol0, nv, gates):
                KVB = list(range(7)) if gates else ALLB
                for tile in range(NT):
                    for hp in range(len(kcols) // 2):
                        b, bk = proj_fm(None, 128, kcols[2 * hp], tile, None, ["WIN"], None, banks=KVB)
                        cp("dve", KT[2 * hp][0:64, tile * TS:(tile + 1) * TS], ps[b][0:64, :], [bk], [("KT", 2 * hp, tile)])
                        P.op("act", (lambda o_, i_: (lambda e: e.activation(out=o_, in_=i_, func=AF.Copy)))(
                            KT[2 * hp + 1][0:64, tile * TS:(tile + 1) * TS], ps[b][64:128, :]), [bk], [("KT", 2 * hp + 1, tile)])
                    for j in range(4):
                        blk = tile * 4 + j
                        b, bk = bank("M", KVB)
                        n = nv
                        for k in range(8):
                            mm(ps[b][:, 0:n], XT[:, k, blk * 128:(blk + 1) * 128], WIN[:, k, vcol0:vcol0 + n],
                               k == 0, k == 7, xt_keys(tile) + ["WIN"], [bk])
                        cp("dve", V[:, blk, 0:nv], ps[b][:, 0:nv], [bk], [("V", tile)])
                        if gates:
                            for k in range(8):
                                mm(ps[7][:, blk * 4:(blk + 1) * 4], XT[:, k, blk * 128:(blk + 1) * 128],
                                   WIN[:, k, vcol0 + nv:vcol0 + nv + 4], k == 0, k == 7, xt_keys(tile) + ["WIN"],
                                   [("ps", 7)], skip_group_check=True)
                if gates:
                    cp("dve", GATE.rearrange("p a b -> p (a b)"), ps[7][:, 0:128], [("ps", 7)], ["GATE"])

            if "A" in groups:
                load_win("A")
                dma("sp", BF_, dd["fox_b_f"][l].partition_broadcast(128), [], ["BF"])
                kv_pass("A", [256, 320, 384, 448], 512, 256, AMODE != 1)
                if AMODE in (0, 3):
                    for h in range(4):
                        ts("dve", GATE[:, :, h], GATE[:, :, h], BF_[:, h:h + 1], ALU.add, ["GATE", "BF"], ["GATE"])
                    GF = GATE.rearrange("p a b -> p (a b)")
                    act(GF, GF, AF.Exp, ["GATE"], ["GATE"], scale=-1.0)
                    act(GF, GF, AF.Ln, ["GATE"], ["GATE"], bias=1.0)
                    if False:
                        pass
                    b1, bk1 = bank("M", ALLB)
                    mm(ps[b1][:, 0:128], TRI[:, 1, :], GF, True, True, ["GATE", "const"], [bk1])
                    b2, bk2 = bank("M", ALLB)
                    mm(ps[b2][:, 0:128], TRI[:, 0, :], GF, True, True, ["GATE", "const"], [bk2])
                    cp("dve", TOTs.rearrange("p a b -> p (a b)"), ps[b1][:, 0:128], [bk1], ["TOTs"])
                    if False:
                        pass
                    for n_ in range(NB - 1):
                        tt("dve", CARRYX[:, pb_nat(n_ + 1), :], CARRYX[:, pb_nat(n_), :], TOTs[:, pb_nat(n_), :], ALU.add,
                           ["TOTs", "carry0", "CARRYX"], ["CARRYX"])
                    tt("dve", CNEG, ps[b2][:, 0:128].rearrange("p (a b) -> p a b", a=NB), CARRYX, ALU.add,
                       [bk2, "CARRYX", "carry0"], ["CNEG"])
                    if False:
                        pass
                    u = 0
                    for qt in range(NS):
                        bt = BT[qt % 2]
                        btk = ("BT", qt % 2)
                        for h in range(4):
                            ts("dve", bt[:, :, h], CNEG[:, :, h], CARRYX[:, (4 + qt) * 4, h:h + 1], ALU.subtract,
                               ["CNEG", "CARRYX", "carry0"], [btk])
                        make_xtq(qt)
                        for hp in range(2):
                            b_, bk_ = proj_fm(None, 128, hp * 128, qt, None, ["WIN"], None, own=True)
                            cp("dve", QT[2 * hp][qt % 2][0:64, :], ps[b_][0:64, :], [bk_], [("QT", 2 * hp, qt % 2)])
                            P.op("act", (lambda o_, i_: (lambda e: e.activation(out=o_, in_=i_, func=AF.Copy)))(
                                QT[2 * hp + 1][qt % 2][0:64, :], ps[b_][64:128, :]), [bk_], [("QT", 2 * hp + 1, qt % 2)])
                        for h in range(4):
                            qb = qt % 2
                            qk = ("QT", h, qb)
                            softmax_unit("A", h, qt, 64, 0.125, QT[h][qb][0:64, :], [qk],
                                         (lambda kb, hh=h, bb=bt: bb[:, kb, hh:hh + 1]), [btk], u)
                            u += 1

            if "B" in groups:
                P.barrier()
                load_win("B")
                wb = GBASE["B"]
                uq = dd["mla_w_uq"][l].rearrange("(c p) n -> p c n", p=128)
                dma("pool", WUQ, uq, [], ["WUQ"])
                uq4 = uq.rearrange("p c (h x) -> p c h x", h=4)
                wq4 = WUQS.rearrange("p c (h x) -> p c h x", h=4)
                for c in range(2):
                    dma("pool", wq4[:, c, :, 0:64], uq4[:, c, :, 0:64], [], ["WUQS"])
                    dma("pool", wq4[:, c, :, 64:80], uq4[:, c, :, 80:96], [], ["WUQS"])
                    dma("pool", wq4[:, c, :, 80:96], uq4[:, c, :, 64:80], [], ["WUQS"])
                dma("pool", WUKV, dd["mla_w_ukv"][l], [], ["WUKV"])
                dma("pool", WKRS[:, :, 0:64], w_in_l[:, :, wb + 320:wb + 384], [], ["WKRS"])
                dma("pool", WKRS[:, :, 64:80], w_in_l[:, :, wb + 400:wb + 416], [], ["WKRS"])
                dma("pool", WKRS[:, :, 80:96], w_in_l[:, :, wb + 384:wb + 400], [], ["WKRS"])
                dma("sp", GQ, dd["mla_g_q"][l].rearrange("(c p) -> p c", p=128), [], ["GQ"], allow_slow_non_contiguous=True)
                dma("sp", GKV, dd["mla_g_kv"][l].rearrange("(c p) -> p c", p=128), [], ["GKV"], allow_slow_non_contiguous=True)
                ts("dve", GQ, GQ, 16.0, ALU.mult, ["GQ"], ["GQ"])
                ts("dve", GKV, GKV, float(np.sqrt(128.0)), ALU.mult, ["GKV"], ["GKV"])
                WUKV4 = WUKV.rearrange("p (h x) -> p h x", h=4)

                def load_rope(tile, own=False):
                    src_ = dd["ropeq"] if own else dd["rope"]
                    dma("sp", ROPE[64:96, 0, :], src_[0, :, tile * TS:(tile + 1) * TS], [], ["ROPE"])
                    dma("sp", ROPE[64:96, 1, :], src_[1, :, tile * TS:(tile + 1) * TS], [], ["ROPE"])

                def rope_apply(ba, bka, bb, bkb, dsts, dkeys):
                    tt("dve", RT1[64:96, :], ps[ba][64:96, :], ROPE[64:96, 0, :], ALU.mult, [bka, "ROPE"], ["RT1"])
                    tt("dve", RT2[64:96, :], ps[bb][64:96, :], ROPE[64:96, 1, :], ALU.mult, [bkb, "ROPE"], ["RT2"])
                    for dst, dk_ in zip(dsts, dkeys):
                        tt("dve", dst, RT1[64:96, :], RT2[64:96, :], ALU.add, ["RT1", "RT2"], [dk_])

                for tile in range(NT):
                    ba, bka = proj_fm(None, 128, 256, tile, None, ["WIN"], None, banks=ALLB)
                    act(SQ[:, 0, :], ps[ba], AF.Square, [bka], ["SQ"])
                    bs, bks = bank("M", ALLB)
                    mm(ps[bs], onesb, SQ[:, 0, :], True, True, ["SQ", "const"], [bks])
                    rsqrt(RSB, ps[bs], 128.0 * 1e-6, [bks], ["RSB"])
                    stt(CKVN, ps[ba], GKV[:, 0:1], RSB, ALU.mult, ALU.mult, [bka, "RSB", "GKV"], ["CKVN"])
                    for h in range(4):
                        b, bk = bank("M", ALLB)
                        mm(ps[b][0:64, :], WUKV[:, h * 128:h * 128 + 64], CKVN, True, True, ["CKVN", "WUKV"], [bk])
                        cp("dve", KT[h][0:64, tile * TS:(tile + 1) * TS], ps[b][0:64, :], [bk], [("KT", h, tile)])
                    for j in range(4):
                        blk = tile * 4 + j
                        b, bk = bank("M", ALLB)
                        mm(ps[b][:, 0:256].rearrange("p (h x) -> p h x", h=4), CKVN[:, j * 128:(j + 1) * 128],
                           WUKV4[:, :, 64:128], True, True, ["CKVN", "WUKV"], [bk])
                        cp("dve", V[:, blk, :], ps[b][:, 0:256], [bk], [("V", tile)])
                    bc, bkc = proj_fm(None, 96, 320, tile, None, ["WIN"], None, banks=ALLB)
                    bd_, bkd = proj_fm(None, 96, 0, tile, None, ["WKRS"], None, banks=ALLB, lw=WKRS)
                    load_rope(tile)
                    rope_apply(bc, bkc, bd_, bkd, [KT[h][64:96, tile * TS:(tile + 1) * TS] for h in range(4)],
                               [("KT", h, tile) for h in range(4)])
                u = 0
                sc_b = float(96.0 ** -0.5)
                for qt in range(NS):
                    make_xtq(qt)
                    bq = []
                    for c in range(2):
                        bq.append(proj_fm(None, 128, c * 128, qt, None, ["WIN"], None, banks=[7, 0, 1, 2], own=True))
                        act(SQ[:, c, :], ps[bq[c][0]], AF.Square, [bq[c][1]], ["SQ"])
                    bs, bks = bank("M", [7, 0, 1, 2])
                    for c in range(2):
                        mm(ps[bs], onesb, SQ[:, c, :], c == 0, c == 1, ["SQ", "const"], [bks])
                    rsqrt(RSB, ps[bs], 256.0 * 1e-6, [bks], ["RSB"])
                    for c in range(2):
                        stt(CQN[:, c, :], ps[bq[c][0]], GQ[:, c:c + 1], RSB, ALU.mult, ALU.mult,
                            [bq[c][1], "RSB", "GQ"], ["CQN"])
                    load_rope(qt, own=True)
                    for h in range(4):
                        qb = u % 2
                        qk = ("QT", h, qb)
                        be, bke = bank("M", [7, 0, 1, 2])
                        for c in range(2):
                            mm(ps[be][0:96, :], WUQ[:, c, h * 96:(h + 1) * 96], CQN[:, c, :], c == 0, c == 1,
                               ["CQN", "WUQ"], [bke])
                        bf_, bkf = bank("M", [7, 0, 1, 2])
                        for c in range(2):
                            mm(ps[bf_][0:96, :], WUQS[:, c, h * 96:(h + 1) * 96], CQN[:, c, :], c == 0, c == 1,
                               ["CQN", "WUQS"], [bkf])
                        cp("dve", QT[h][qb][0:64, :], ps[be][0:64, :], [bke], [qk])
                        rope_apply(be, bke, bf_, bkf, [QT[h][qb][64:96, :]], [qk])
                        softmax_unit("B", h, qt, 96, sc_b, QT[h][qb][0:96, :], [qk], None, [], u)
                        u += 1

            if "C" in groups:
                P.barrier()
                load_win("C")
                kv_pass("C", [256, 320, 384, 448], 512, 256, False)
                for qt in range(NS):
                    make_xtq(qt)
                    for pair in range(2):
                        hs = [2 * pair, 2 * pair + 1]
                        qaps, qkeys = [], []
                        qb = qt % 2
                        b_, bk_ = proj_fm(None, 128, pair * 128, qt, None, ["WIN"], None, banks=[0, 1, 4, 5], own=True)
                        cp("dve", QT[hs[0]][qb][0:64, :], ps[b_][0:64, :], [bk_], [("QT", hs[0], qb)])
                        P.op("act", (lambda o_, i_: (lambda e: e.activation(out=o_, in_=i_, func=AF.Copy)))(
                            QT[hs[1]][qb][0:64, :], ps[b_][64:128, :]), [bk_], [("QT", hs[1], qb)])
                        for h in hs:
                            qk = ("QT", h, qb)
                            qaps.append(QT[h][qb][0:64, :])
                            qkeys.append([qk])
                        sb_units(hs, qt, qaps, qkeys)

            if "D" in groups:
                P.barrier()
                load_win("D")
                dma("sp", ES, dd["swa_sinks"][l].partition_broadcast(128), [], ["ES"])
                act(ES, ES, AF.Exp, ["ES"], ["ES"])
                kv_pass("D", [256, 320], 384, 128, False)
                steps = [(js, h, half, j) for js in range(NS) for h in range(4) for half in range(2) for j in range(4)]
                sstate = {}

                def d_unit(js, h, half):
                    return (js * 4 + h) * 2 + half

                def d_stage_s(st_):
                    js, h, half, j = st_
                    u = d_unit(js, h, half)
                    kvh = h // 2
                    nt = 2 * js + half
                    qb = u % 2
                    qk = ("QT", h, qb)
                    if j == 0:
                        proj_fm(QT[h][qb][0:64, :], 64, h * 64, pos_tile(nt), "dve", ["WIN"], [qk])
                    n = 4 * nt + j
                    pn = pb_nat(n)
                    pm = pb_nat(n - 1) if n > 0 else 0
                    b, bk = bank("S", [0, 1, 2])
                    qa = QT[h][qb][0:64, j * 128:(j + 1) * 128]
                    if n > 0:
                        mm(ps[b][:, 0:128], KT[kvh][0:64, pm * 128:(pm + 1) * 128], qa, True, True,
                           [("KT", kvh, pm // 4), qk], [bk])
                    mm(ps[b][:, 128:256], KT[kvh][0:64, pn * 128:(pn + 1) * 128], qa, True, True,
                       [("KT", kvh, pn // 4), qk], [bk])
                    sstate[st_] = (b, bk)

                def d_stage_rest(st_, idx, phase):
                    js, h, half, j = st_
                    u = d_unit(js, h, half)
                    kvh = h // 2
                    nt = 2 * js + half
                    n = 4 * nt + j
                    pn = pb_nat(n)
                    pm = pb_nat(n - 1) if n > 0 else 0
                    lo = 0 if n > 0 else 128
                    ob = [3, 4][u % 2]
                    db = [5, 6][u % 2]
                    obk, dbk = ("ps", ob), ("ps", db)
                    si = idx % 2
                    sbd, pd = SBD[si], PD[si]
                    if phase == 2:
                        b, bk = sstate.pop(st_)
                        stt(sbd[:, lo:256], ps[b][:, lo:256], 0.125, BIASD[:, h, lo:256], ALU.mult, ALU.add,
                            [bk, "const"], [("SBD", si)])
                        act(pd[:, lo:256], sbd[:, lo:256], AF.Exp, [("SBD", si)], [("PD", si)])
                        return
                    oc = slice(j * 128, (j + 1) * 128)
                    if n > 0:
                        mm(ps[ob][0:64, oc], V[:, pm, kvh * 64:(kvh + 1) * 64], pd[:, 0:128], True, False,
                           [("PD", si), ("V", pm // 4)], [obk])
                        mm(ps[db][0:64, oc], onesb[:, 0:64], pd[:, 0:128], True, False, [("PD", si), "const"], [dbk])
                    mm(ps[ob][0:64, oc], V[:, pn, kvh * 64:(kvh + 1) * 64], pd[:, 128:256], n == 0, True,
                       [("PD", si), ("V", pn // 4)], [obk])
                    mm(ps[db][0:64, oc], onesb[:, 0:64], pd[:, 128:256], n == 0, True, [("PD", si), "const"], [dbk])
                    if j == 3:
                        ts("dve", RC[0:64, :], ps[db][0:64, :], ES[0:64, h:h + 1], ALU.add, [dbk, "ES"], ["RC"])
                        act(RC[0:64, :], RC[0:64, :], AF.Ln, ["RC"], ["RC"])
                        act(RC[0:64, :], RC[0:64, :], AF.Exp, ["RC"], ["RC"], scale=-1.0)
                        tt("dve", MO[half][0:64, :], ps[ob][0:64, :], RC[0:64, :], ALU.mult, [obk, "RC"], [("MO", half)])
                        if half == 1:
                            ts("dve", MO2[0:64, :], MO[0][0:64, :], SEL[0:64, 0:1], ALU.mult, [("MO", 0), "const"], ["MO2"])
                            stt(MO2[0:64, :], MO[1][0:64, :], SEL[0:64, 1:2], MO2[0:64, :], ALU.mult, ALU.add,
                                [("MO", 1), "const", "MO2"], ["MO2"])
                            row = 3 * 256 + h * 64
                            dma("sp", mixT_d[row:row + 64, js * TS:(js + 1) * TS], MO2[0:64, :], ["MO2"],
                                [("mixT", "D", h, js)])

                nst = len(steps)
                for idx in range(nst + 2):
                    if idx < nst:
                        d_stage_s(steps[idx])
                    if 0 <= idx - 1 < nst:
                        d_stage_rest(steps[idx - 1], idx - 1, 2)
                    if 0 <= idx - 2 < nst:
                        d_stage_rest(steps[idx - 2], idx - 2, 3)

        def phase2_(l, last):
            AR1.reset()
            AR2.reset()
            AT = AR1.get([128, NFC, TS], BF16)
            MX = AR1.get([128, 8, TS], BF16)
            X1 = AR1.get([128, 4, D], F32)
            WOS = AR2.get([128, 8, D], BF16)
            WST = [AR2.get([128, D], F32) for _ in range(2)]
            LNV = [AR2.get([128, D], F32) for _ in range(2)]
            WG = [AR2.get([128, 8, 256], BF16) for _ in range(2)]
            WU = [AR2.get([128, 8, 256], BF16) for _ in range(2)]
            WD = [AR2.get([128, 2, TS], BF16) for _ in range(2)]
            X1T = AR2.get([128, 8, TS], BF16)
            SQ2 = [AR2.get([128, TS], BF16) for _ in range(2)]
            ACCY = AR2.get([128, D], F32)
            U = AR2.get([128, D], F32)
            XB = AR2.get([128, D], F32)
            SG = [AR2.get([128, TS], F32) for _ in range(2)]
            RS = AR2.get([128, 4, 4], F32)
            STAT = AR2.get([128, 2, 6], F32)
            MV = AR2.get([128, 4], F32)

            dma("sp", MG, dd["mix_g"][l].rearrange("(c p) -> p c", p=128), [], ["MG"], allow_slow_non_contiguous=True)
            wo_l = dd["w_o"][l].rearrange("(c p) n -> p c n", p=128)
            for c in range(8):
                dma("sp", WST[c % 2], wo_l[:, c, :], [], [("WST", c % 2)])
                ts("dve", WOS[:, c, :], WST[c % 2], MG[:, c:c + 1], ALU.mult, [("WST", c % 2), "MG"], ["WOS"])
            wg_l = dd["w_gate"][l].rearrange("(k p) n -> p k n", p=128)
            wu_l = dd["w_up"][l].rearrange("(k p) n -> p k n", p=128)
            wd_l = dd["w_down"][l].rearrange("(c p) n -> p c n", p=128)
            mixT_v = mixT_d.rearrange("(c p) t -> p c t", p=128)
            src_x = dd["xq"] if l == 0 else xs_d
            dst_x = out_d if last else xs_d
            lnvi = [0]

            def load_vec(name):
                i = lnvi[0] % 2
                lnvi[0] += 1
                dma("sp", LNV[i], dd[name][l].partition_broadcast(128), [], [("LNV", i)])
                return LNV[i], ("LNV", i)

            def layernorm(src, srck, dst, dstk, gk, bk_):
                g_ap, gkey = gk
                b_ap, bkey = bk_
                for hf in range(2):
                    P.op("dve", (lambda hh: (lambda e: e.bn_stats(out=STAT[:, hh, :], in_=src[:, hh * 512:(hh + 1) * 512])))(hf),
                         [srck], ["STAT"])
                P.op("dve", lambda e: e.bn_aggr(out=MV[:, 0:2], in_=STAT.rearrange("p a b -> p (a b)")), ["STAT"], ["MV"])
                rsqrt(MV[:, 2:3], MV[:, 1:2], 1e-5, ["MV"], ["MV"])
                ts("dve", MV[:, 3:4], MV[:, 0:1], MV[:, 2:3], ALU.mult, ["MV"], ["MV"], s2=-1.0, op1=ALU.mult)
                act(dst, src, AF.Identity, [srck, "MV"], [dstk], scale=MV[:, 2:3], bias=MV[:, 3:4])
                tt("dve", dst, dst, g_ap, ALU.mult, [dstk, gkey], [dstk])
                tt("dve", dst, dst, b_ap, ALU.add, [dstk, bkey], [dstk])

            def transpose_to(src, srck, dstT, col0, dstk):
                for half in range(2):
                    b, bk = bank("M2", [4, 5, 6, 7])
                    for j in range(4):
                        c = half * 4 + j
                        tr(ps[b][:, j * 128:(j + 1) * 128], src[:, c * 128:(c + 1) * 128], [srck], [bk])
                    cp("act", dstT[:, half * 4:half * 4 + 4, col0:col0 + 128],
                       ps[b].rearrange("p (a b) -> p a b", a=4), [bk], [dstk])

            if S2 <= 1:
                return
            for tile in range(NS):
                mxk = "MX"
                dma("sp", MX, mixT_v[:, :, tile * TS:(tile + 1) * TS],
                    [("mixT", g, h, tile) for g in "ABCD" for h in range(4)], [mxk])
                g1 = load_vec("ln1_g")
                b1 = load_vec("ln1_b")
                for c in range(8):
                    sq = SQ2[c % 2]
                    sqk = ("SQ2", c % 2)
                    tt("dve", sq, MX[:, c, :], MX[:, c, :], ALU.mult, [mxk], [sqk])
                    for j in range(4):
                        mm(ps[4 + j][:, 0:4], sq[:, j * 128:(j + 1) * 128], IND[:, c, :], c == 0, c == 7,
                           [sqk, "const"], [("ps", 4 + j)])
                for j in range(4):
                    ts("dve", RS[:, j, :], ps[4 + j][:, 0:4], 1.0 / 256.0, ALU.mult, [("ps", 4 + j)], ["RS"],
                       s2=1e-6, op1=ALU.add)
                rsqrt(RS.rearrange("p a b -> p (a b)"), RS.rearrange("p a b -> p (a b)"), 0.0, ["RS"], ["RS"])
                if S2 <= 2:
                    continue
                for j in range(4):
                    blk = tile * 4 + j
                    dma("sp", XB, src_x[blk * 128:(blk + 1) * 128, :], [("xs", blk)], ["XB"])
                    for half in range(2):
                        ybs = []
                        for g in range(4):
                            b, bk = bank("Y", [0, 1, 2, 3])
                            ybs.append((b, bk))
                            for c2 in range(2):
                                c = 2 * g + c2
                                mm(ps[b], MX[:, c, j * 128:(j + 1) * 128], WOS[:, c, half * 512:(half + 1) * 512],
                                   c2 == 0, c2 == 1, [mxk, "WOS"], [bk])
                        acc = ACCY[:, half * 512:(half + 1) * 512]
                        ts("dve", acc, ps[ybs[0][0]], RS[:, j, 0:1], ALU.mult, [ybs[0][1], "RS"], ["ACCY"])
                        for g in range(1, 4):
                            stt(acc, ps[ybs[g][0]], RS[:, j, g:g + 1], acc, ALU.mult, ALU.add, [ybs[g][1], "RS", "ACCY"], ["ACCY"])
                    stt(U, XB, ALPHA, ACCY, ALU.mult, ALU.add, ["XB", "ACCY"], ["U"])
                    layernorm(U, "U", X1[:, j, :], ("X1", j), g1, b1)
                    transpose_to(X1[:, j, :], ("X1", j), X1T, j * 128, "X1T")
                if S2 <= 3:
                    continue
                g2 = load_vec("ln2_g")
                b2 = load_vec("ln2_b")
                for sc in range(11):
                    wi = sc % 2
                    dma("pool", WG[wi], wg_l[:, :, sc * 256:(sc + 1) * 256], [], [("WG", wi)])
                    dma("pool", WU[wi], wu_l[:, :, sc * 256:(sc + 1) * 256], [], [("WU", wi)])
                    for j2 in range(2):
                        fc = sc * 2 + j2
                        bg, bkg = bank("G", [0, 1])
                        bu, bku = bank("Uu", [2, 3])
                        for k in range(8):
                            mm(ps[bg], WG[wi][:, k, j2 * 128:(j2 + 1) * 128], X1T[:, k, :], k == 0, k == 7,
                               [("WG", wi), "X1T"], [bkg])
                        for k in range(8):
                            mm(ps[bu], WU[wi][:, k, j2 * 128:(j2 + 1) * 128], X1T[:, k, :], k == 0, k == 7,
                               [("WU", wi), "X1T"], [bku])
                        sg = SG[fc % 2]
                        sgk = ("SG", fc % 2)
                        act(sg, ps[bg], AF.Silu, [bkg], [sgk])
                        tt("dve", AT[:, fc, :], sg, ps[bu], ALU.mult, [sgk, bku], [("AT", fc)])
                if S2 <= 4:
                    continue
                for half in range(2):
                    fb = [4, 5, 6, 7]
                    for sc in range(11):
                        wi = (half * 11 + sc) % 2
                        dma("pool", WD[wi], wd_l[:, sc * 2:sc * 2 + 2, half * 512:(half + 1) * 512], [], [("WD", wi)])
                        for j2 in range(2):
                            fc = sc * 2 + j2
                            for j in range(4):
                                mm(ps[fb[j]], AT[:, fc, j * 128:(j + 1) * 128], WD[wi][:, j2, :], fc == 0, fc == NFC - 1,
                                   [("AT", fc), ("WD", wi)], [("ps", fb[j])])
                    for j in range(4):
                        stt(X1[:, j, half * 512:(half + 1) * 512], X1[:, j, half * 512:(half + 1) * 512], ALPHA,
                            ps[fb[j]], ALU.mult, ALU.add, [("ps", fb[j]), ("X1", j)], [("X1", j)])
                for j in range(4):
                    blk = tile * 4 + j
                    layernorm(X1[:, j, :], ("X1", j), U, "U", g2, b2)
                    o = dma("sp", dst_x[blk * 128:(blk + 1) * 128, :], U, ["U"], [("xs", blk)])
                    if last:
                        all_dma_out.append(o)
                    else:
                        transpose_to(U, "U", X1T, j * 128, "X1T")
                if not last:
                    dma("sp", xtm_f[tile].bitcast(BF16).rearrange("(c p) t -> p c t", p=128), X1T, ["X1T"], [("xtm", tile)])
                    P.cc((lambda jj: (lambda e: e.collective_compute("AllGather", ALU.bypass, replica_groups=rgroups,
                                                                     ins=[xtm_f[jj]], outs=[xta_f[jj]])))(tile),
                         [("xtm", tile)], [("xta", tile)])
                    xta_v = xta_f[tile].bitcast(BF16).rearrange("(r k p) t -> p k r t", r=2, k=8, p=128)
                    for r_ in range(2):
                        dma("sp", XT[:, :, (r_ * 4 + tile) * TS:(r_ * 4 + tile + 1) * TS], xta_v[:, :, r_, :],
                            [("xta", tile)], [("XT", r_ * 4 + tile)])

        build_xt0()
        for l in range(n_layers):
            P.barrier()
            phase1(l)
            P.barrier()
            if phase2:
                phase2_(l, l == n_layers - 1)
        if debug and not phase2:
            pass
        P.barrier()
        P.emit()
    return nc


def kernel(**inputs):
    x = np.ascontiguousarray(np.asarray(inputs["x"], dtype=np.float32))
    nc = build_program()
    base = {nm: np.ascontiguousarray(np.asarray(inputs[nm], dtype=np.float32)) for nm, _ in WNAMES}
    cst = [make_consts(0), make_consts(1)]
    in_maps = []
    for core in range(8):
        bq, r = core // 2, core % 2
        xt = x[bq].reshape(8, 512, D)
        m = dict(base)
        m.update(cst[r])
        m["x"] = np.ascontiguousarray(xt[[nat_tile(p) for p in range(8)]]).reshape(T, D)
        m["xq"] = np.ascontiguousarray(xt[[2 * j + r for j in range(4)]]).reshape(TQ, D)
        in_maps.append(m)
    res = run_bass_kernel_spmd(nc, in_maps, core_ids=list(range(8)))
    out = np.zeros((4, 8, 512, D), np.float32)
    for core in range(8):
        bq, r = core // 2, core % 2
        o = np.asarray(res.results[core]["out"], dtype=np.float32).reshape(4, 512, D)
        for j in range(4):
            out[bq, 2 * j + r] = o[j]
    return out.reshape(4, T, D)
```

```python
import numpy as np
import concourse.bass as bass
import concourse.mybir as mybir
from concourse.bass_utils import run_bass_kernel_spmd
from contextlib import ExitStack

F32 = mybir.dt.float32
BF16 = mybir.dt.bfloat16
AF = mybir.ActivationFunctionType
ALU = mybir.AluOpType
AX = mybir.AxisListType

ENG = ("pe", "act", "dve", "pool", "sp")
NDMASEM = 8


class Op:
    __slots__ = ("eng", "fn", "deps", "is_dma", "signal", "cnt", "semi", "val", "waits", "gid", "is_cc")

    def __init__(self, eng, fn, is_dma):
        self.eng = eng
        self.fn = fn
        self.deps = []
        self.is_dma = is_dma
        self.signal = False
        self.cnt = 0
        self.semi = -1
        self.val = 0
        self.waits = []
        self.is_cc = False


class Prog:
    def __init__(self, nc, stack):
        self.nc = nc
        self.stack = stack
        self.all = []
        self.res = {}
        self.ntens = 0
        self._bar_from = 0

    def sb(self, shape, dt, name=None):
        self.ntens += 1
        return self.stack.enter_context(self.nc.sbuf_tensor("s_" + (name or f"sb{self.ntens}"), list(shape), dt))

    def psum(self, shape, dt, name=None):
        self.ntens += 1
        return self.stack.enter_context(self.nc.psum_tensor(name or f"ps{self.ntens}", list(shape), dt))

    def op(self, eng, fn, reads=(), writes=(), dma=False):
        o = Op(eng, fn, dma)
        deps = o.deps
        res = self.res
        for r in reads:
            st = res.get(r)
            if st is None:
                st = res[r] = [None, []]
            if st[0] is not None:
                deps.append(st[0])
        for w in writes:
            st = res.get(w)
            if st is None:
                st = res[w] = [None, []]
            if st[0] is not None:
                deps.append(st[0])
            deps.extend(st[1])
        for r in reads:
            res[r][1].append(o)
        for w in writes:
            st = res[w]
            st[0] = o
            st[1] = []
        self.all.append(o)
        return o

    def dma(self, eng, out, in_, reads=(), writes=(), **kw):
        return self.op(eng, lambda e: e.dma_start(out=out, in_=in_, **kw), reads, writes, dma=True)

    def cc(self, fn, reads=(), writes=()):
        o = self.op("pool", fn, reads, writes, dma=True)
        o.is_cc = True
        return o

    def wait_all(self, eng, ops):
        o = Op(eng, None, False)
        o.deps = list(ops)
        self.all.append(o)
        return o


    def barrier(self):
        deps = []
        last = {}
        for o in self.all[self._bar_from:]:
            if o.fn is None:
                continue
            if o.is_dma:
                deps.append(o)
            else:
                last[o.eng] = o
        deps.extend(last.values())
        self._bar_from = len(self.all)
        for e in ENG:
            w = Op(e, None, False)
            w.deps = list(deps)
            self.all.append(w)

    def finalize(self):
        nc = self.nc
        for o in self.all:
            nd = []
            seen = set()
            for d in o.deps:
                if id(d) in seen:
                    continue
                seen.add(id(d))
                if d is o:
                    continue
                if not d.is_dma and d.eng == "pe" and o.eng == "pe" and not o.is_dma:
                    continue
                nd.append(d)
                if not d.is_dma:
                    d.signal = True
            o.deps = nd
        cnt = {e: 0 for e in ENG}
        for o in self.all:
            if o.is_dma or o.fn is None:
                continue
            if o.signal:
                cnt[o.eng] += 1
            o.cnt = cnt[o.eng]
        known = {e: {} for e in ENG}
        pool_next = {e: 0 for e in ENG}
        pool_last = {e: [0] * NDMASEM for e in ENG}
        self.nwaits = 0
        ncc = 0
        for o in self.all:
            E = o.eng
            kn = known[E]
            need = {}
            if o.is_cc:
                o.semi = ("cc", ncc)
                o.val = 1
                ncc += 1
            elif o.is_dma:
                s = pool_next[E]
                pool_next[E] = (s + 1) % NDMASEM
                prev = pool_last[E][s]
                key = ("d", E, s)
                if prev > 0 and kn.get(key, 0) < prev:
                    need[key] = prev
                o.semi = (E, s)
                o.val = prev + 16
                pool_last[E][s] = o.val
            for d in o.deps:
                if d.is_dma:
                    key = ("d",) + d.semi
                    v = d.val
                else:
                    key = ("c", d.eng)
                    v = d.cnt
                if kn.get(key, 0) < v and need.get(key, 0) < v:
                    need[key] = v
            for key, v in need.items():
                kn[key] = v
            o.waits = list(need.items())
            self.nwaits += len(need)
        st = self.stack
        self.csem = {e: st.enter_context(nc.semaphore(f"c_{e}")) for e in ENG}
        self.dsem = {(e, s): st.enter_context(nc.semaphore(f"d_{e}{s}")) for e in ("sp", "pool", "act") for s in range(NDMASEM)}
        self.ccsem = [st.enter_context(nc.semaphore(f"cc{i}")) for i in range(ncc)]
        self.byeng = {e: [o for o in self.all if o.eng == e] for e in ENG}

    def _sem(self, key):
        if key[0] == "c":
            return self.csem[key[1]]
        if key[1] == "cc":
            return self.ccsem[key[2]]
        return self.dsem[(key[1], key[2])]

    def emit_engine(self, E, e):
        for o in self.byeng[E]:
            for key, v in o.waits:
                e.wait_ge(self._sem(key), v)
            if o.fn is None:
                continue
            ins = o.fn(e)
            if o.is_cc:
                ins.then_inc(self.ccsem[o.semi[1]])
            elif o.is_dma:
                ins.then_inc(self.dsem[o.semi], 16)
            elif o.signal:
                ins.then_inc(self.csem[E], 1)

    def emit(self):
        nc = self.nc
        self.finalize()
        with nc.Block() as block:
            @block.tensor
            def _(e):
                self.emit_engine("pe", e)

            @block.scalar
            def _(e):
                self.emit_engine("act", e)

            @block.vector
            def _(e):
                self.emit_engine("dve", e)

            @block.gpsimd
            def _(e):
                self.emit_engine("pool", e)

            @block.sync
            def _(e):
                self.emit_engine("sp", e)

D = 1024
T = 4096
NB = 32
NT = 8
TS = 512
DFF = 2816
NFC = 22
INW = 2468
NL = 4
ALPHA = float((2.0 * NL) ** 0.25)
GBASE = {"A": 0, "B": 772, "C": 1188, "D": 1956}
GW = {"A": 772, "B": 416, "C": 768, "D": 512}
MASKV = -30000.0
NS = 4
TQ = NS * 512


def nat_tile(p):
    return 2 * (p % 4) + p // 4


def pos_tile(nt):
    return (nt % 2) * 4 + nt // 2


def pb_nat(n):
    return pos_tile(n // 4) * 4 + n % 4
STAGE = 99
KVF = 0
KVT = 8
VENG = 'dve'
S2 = 99
AMODE = 0
WARM = 0
P2T = 8


def make_consts(rank):
    c = {}
    c["identf"] = np.eye(128, dtype=np.float32)
    p = np.arange(128)[:, None]
    xx = np.arange(896)[None, :]
    ms = np.where((xx - 384) < p, MASKV, 0.0)
    mx = np.where((xx - 384) <= p, MASKV, 0.0)
    full = np.full((128, 896), MASKV)
    zero = np.zeros((128, 896))
    mb = np.zeros((2, 2, 128, 896), np.float32)
    for ti, st in enumerate((ms, mx)):
        if rank == 0:
            mb[ti, 0] = st
            mb[ti, 1] = full
        else:
            mb[ti, 0] = zero
            mb[ti, 1] = st
    c["mbase"] = mb
    sel = np.zeros((128, 2), np.float32)
    sel[:, rank] = 1.0
    c["sel"] = sel
    s = np.arange(128)[:, None]
    t = np.arange(128)[None, :]
    tri = np.zeros((4, 128, 128), np.float32)
    tri[0] = (s <= t)
    tri[1] = 1.0
    tri[2] = -(s > t).astype(np.float32)
    tri[3] = -(s <= t).astype(np.float32)
    c["tri"] = tri
    pos = np.arange(T, dtype=np.float32)
    inv = (np.float32(10000.0) ** (-np.arange(0, 32, 2, dtype=np.float32) / np.float32(32))).astype(np.float32)
    ang = (pos[:, None] * inv[None, :]).astype(np.float32)
    cs = np.cos(ang).astype(np.float32).T
    sn = np.sin(ang).astype(np.float32).T
    rope = np.zeros((2, 32, T), np.float32)
    rope[0, :16] = cs
    rope[0, 16:] = cs
    rope[1, :16] = -sn
    rope[1, 16:] = sn
    r8 = rope.reshape(2, 32, 8, 512)
    c["rope"] = np.ascontiguousarray(r8[:, :, [nat_tile(pp_) for pp_ in range(8)], :]).reshape(2, 32, T)
    c["ropeq"] = np.ascontiguousarray(r8[:, :, [2 * j_ + rank for j_ in range(4)], :]).reshape(2, 32, TQ)
    bd = np.zeros((4, 128, 256), np.float32)
    pp = np.arange(128)[:, None].astype(np.float32)
    cc = np.arange(128)[None, :].astype(np.float32)
    for h in range(4):
        m = np.float32(2.0 ** (-8.0 * (h + 1) / 4.0))
        d1 = 128.0 + cc - pp
        bd[h, :, 0:128] = np.where(cc < pp, -m * d1, MASKV)
        d2 = cc - pp
        bd[h, :, 128:256] = np.where(cc >= pp, -m * d2, MASKV)
    c["biasd"] = bd
    ind = np.zeros((128, 8, 4), np.float32)
    for ch in range(8):
        ind[:, ch, ch // 2] = 1.0
    c["ind"] = ind
    return c


WNAMES = [("w_in", [NL, D, INW]), ("fox_b_f", [NL, 4]), ("mla_g_q", [NL, 256]), ("mla_g_kv", [NL, 128]),
          ("mla_w_uq", [NL, 256, 384]), ("mla_w_ukv", [NL, 128, 512]), ("swa_sinks", [NL, 4]),
          ("mix_g", [NL, D]), ("w_o", [NL, D, D]), ("ln1_g", [NL, D]), ("ln1_b", [NL, D]),
          ("w_gate", [NL, D, DFF]), ("w_up", [NL, D, DFF]), ("w_down", [NL, DFF, D]),
          ("ln2_g", [NL, D]), ("ln2_b", [NL, D])]
CNAMES = [("identf", [128, 128]), ("mbase", [2, 2, 128, 896]), ("tri", [4, 128, 128]), ("rope", [2, 32, T]),
          ("ropeq", [2, 32, TQ]), ("sel", [128, 2]), ("biasd", [4, 128, 256]), ("ind", [128, 8, 4])]


class Arena:
    def __init__(self, P, nbytes, name):
        self.t = P.sb([128, nbytes // 2], BF16, name)
        self.n = nbytes
        self.off = 0

    def reset(self):
        self.off = 0

    def get(self, shape, dt):
        esz = 4 if dt == F32 else 2
        ne = int(np.prod(shape[1:]))
        n = (ne * esz + 63) // 64 * 64
        a = self.t[:, self.off // 2:(self.off + n) // 2]
        self.off += n
        assert self.off <= self.n, ("arena overflow", self.off, self.n)
        if dt == F32:
            a = a.bitcast(F32)
        a = a[:, 0:ne]
        if len(shape) == 3:
            a = a.rearrange("p (a b) -> p a b", a=shape[1])
        elif len(shape) == 4:
            a = a.rearrange("p (a b c) -> p a b c", a=shape[1], b=shape[2])
        return a


def build_program(n_layers=NL, groups="ABCD", debug=False, phase2=True, n_cores=8):
    nc = bass.Bass("TRN2", target_bir_lowering=False)
    dd = {}
    dd["x"] = nc.dram_tensor("x", [T, D], F32, kind="ExternalInput").ap()
    for nm, shp in WNAMES + CNAMES:
        dd[nm] = nc.dram_tensor(nm, list(shp), F32, kind="ExternalInput").ap()
    dd["xq"] = nc.dram_tensor("xq", [TQ, D], F32, kind="ExternalInput").ap()
    out_d = nc.dram_tensor("out", [TQ, D], F32, kind="ExternalOutput").ap()
    xs_d = nc.dram_tensor("xs", [TQ, D], F32, kind="Internal").ap()
    mixT_d = nc.dram_tensor("mixT", [D, TQ], BF16, kind="ExternalOutput" if debug else "Internal").ap()
    xtm_f = [nc.dram_tensor(f"xtm{j_}", [D, TS // 2], F32, kind="Internal").ap() for j_ in range(NS)]
    xta_f = [nc.dram_tensor(f"xta{j_}", [2 * D, TS // 2], F32, kind="Internal").ap() for j_ in range(NS)]
    rgroups = [[2 * g_, 2 * g_ + 1] for g_ in range(n_cores // 2)]

    with ExitStack() as st:
        P = Prog(nc, st)

        def mm(out, lhsT, rhs, start, stop, r, w, **kw):
            P.op("pe", lambda e: e.matmul(out, lhsT=lhsT, rhs=rhs, start=start, stop=stop, **kw), r, w)

        def tr(out, in_, r, w):
            P.op("pe", lambda e: e.transpose(out=out, in_=in_, identity=identf), list(r) + ["const"], w)

        def act(out, in_, func, r, w, scale=None, bias=None):
            kw = {}
            if scale is not None:
                kw["scale"] = scale
            if bias is not None:
                kw["bias"] = bias
            P.op("act", lambda e: e.activation(out=out, in_=in_, func=func, **kw), r, w)

        def tt(eng, out, in0, in1, op, r, w):
            P.op(eng, lambda e: e.tensor_tensor(out=out, in0=in0, in1=in1, op=op), r, w)

        def ts(eng, out, in0, s1, op0, r, w, s2=None, op1=None):
            if op1 is None:
                P.op(eng, lambda e: e.tensor_scalar(out=out, in0=in0, scalar1=s1, scalar2=None, op0=op0), r, w)
            else:
                P.op(eng, lambda e: e.tensor_scalar(out=out, in0=in0, scalar1=s1, scalar2=s2, op0=op0, op1=op1), r, w)

        def stt(out, in0, scalar, in1, op0, op1, r, w):
            P.op("dve", lambda e: e.scalar_tensor_tensor(out=out, in0=in0, scalar=scalar, in1=in1, op0=op0, op1=op1), r, w)

        def cp(eng, out, in_, r, w):
            if eng == "act":
                P.op("act", lambda e: e.activation(out=out, in_=in_, func=AF.Copy), r, w)
            else:
                P.op(eng, lambda e: e.tensor_copy(out=out, in_=in_), r, w)

        def rsqrt(out, in_, eps, r, w):
            act(out, in_, AF.Ln, r, w, bias=eps)
            act(out, out, AF.Exp, list(w), w, scale=-0.5)

        def recip(out, in_, r, w):
            P.op("dve", lambda e: e.reciprocal(out=out, in_=in_), r, w)

        def dma(eng, out, in_, r, w, **kw):
            return P.dma(eng, out, in_, r, w, **kw)

        ps = [P.psum([128, 512], F32, f"bank{i}")[:] for i in range(8)]
        rot = {}

        def bank(pool, lst):
            i = rot.get(pool, 0)
            rot[pool] = i + 1
            b = lst[i % len(lst)]
            return b, ("ps", b)

        XT = P.sb([128, 8, T], BF16, "XT")[:]
        AR1 = Arena(P, 49152, "AR1")
        AR2 = Arena(P, 80 * 1024, "AR2")
        identf = P.sb([128, 128], F32, "identf")[:]
        identb = P.sb([128, 128], BF16, "identb")[:]
        onesb = P.sb([128, 128], BF16, "onesb")[:]
        MB = P.sb([128, 4, 896], BF16, "MB")[:]
        TRI = P.sb([128, 4, 128], F32, "TRI")[:]
        TRIB = P.sb([128, 2, 128], BF16, "TRIB")[:]
        BIASD = P.sb([128, 4, 256], F32, "BIASD")[:]
        IND = P.sb([128, 8, 4], BF16, "IND")[:]
        SMALL = P.sb([128, 64], F32, "SMALL")[:]
        BF_ = SMALL[:, 0:4]
        ES = SMALL[:, 4:8]
        GQ = SMALL[:, 8:10]
        GKV = SMALL[:, 10:11]
        MG = SMALL[:, 16:24]
        SEL = SMALL[:, 24:26]
        CARRYX = P.sb([128, NB, 4], F32, "CARRYX")[:]

        dma("sp", identf, dd["identf"], [], ["const"])
        dma("pool", identb, dd["identf"], [], ["const"])
        dma("pool", onesb, dd["tri"][1], [], ["const"])
        dma("pool", MB, dd["mbase"].rearrange("t w p c -> p (t w) c"), [], ["const"])
        dma("sp", SEL, dd["sel"], [], ["const"])
        dma("sp", TRI, dd["tri"].rearrange("i p c -> p i c"), [], ["const"])
        dma("pool", TRIB, dd["tri"][2:4].rearrange("i p c -> p i c"), [], ["const"])
        dma("sp", BIASD, dd["biasd"].rearrange("h p c -> p h c"), [], ["const"])
        dma("pool", IND, dd["ind"], [], ["const"])
        P.op("dve", lambda e: e.memset(CARRYX[:, 0, :], 0.0), [], ["carry0"])

        all_dma_out = []

        def build_xt0():
            AR2.reset()
            XB = [AR2.get([128, D], F32) for _ in range(2)]
            for blk in range(NB):
                xb = XB[blk % 2]
                k = ("XB0", blk % 2)
                dma("sp", xb, dd["x"][blk * 128:(blk + 1) * 128, :], [], [k])
                for half in range(2):
                    b, bk = bank("M", list(range(8)))
                    for j in range(4):
                        c = half * 4 + j
                        tr(ps[b][:, j * 128:(j + 1) * 128], xb[:, c * 128:(c + 1) * 128], [k], [bk])
                    cp("act" if half else "dve", XT[:, half * 4:half * 4 + 4, blk * 128:(blk + 1) * 128],
                       ps[b].rearrange("p (a b) -> p a b", a=4), [bk], [("XT", blk // 4)])

        def phase1(l):
            AR1.reset()
            AR2.reset()
            KT = [AR1.get([128, T], BF16) for _ in range(4)]
            V = AR1.get([128, NB, 256], BF16)
            QT = [[AR2.get([128, TS], BF16) for _ in range(2)] for _ in range(4)]
            WIN = AR2.get([128, 8, 772], BF16)
            PT = [AR2.get([128, TS], BF16) for _ in range(3)]
            CT = [[AR2.get([128, TS], F32) for _ in range(2)] for _ in range(2)]
            RC = AR2.get([128, TS], F32)
            MO = [AR2.get([128, TS], BF16) for _ in range(2)]
            GATE = AR2.get([128, NB, 4], F32)
            TOTs = AR2.get([128, NB, 4], F32)
            CNEG = AR2.get([128, NB, 4], F32)
            BT = [AR2.get([128, NB, 4], F32) for _ in range(2)]
            WUQ = AR2.get([128, 2, 384], BF16)
            WUQS = AR2.get([128, 2, 384], BF16)
            WUKV = AR2.get([128, 512], BF16)
            WKRS = AR2.get([128, 8, 96], BF16)
            SQ = AR2.get([128, 2, TS], BF16)
            RSB = AR2.get([128, TS], F32)
            CQN = AR2.get([128, 2, TS], BF16)
            CKVN = AR2.get([128, TS], BF16)
            ROPE = AR2.get([128, 2, TS], F32)
            RT1 = AR2.get([128, TS], F32)
            RT2 = AR2.get([128, TS], F32)
            SBD = [AR2.get([128, 256], F32) for _ in range(2)]
            PD = [AR2.get([128, 256], BF16) for _ in range(2)]
            XTQ = AR2.get([128, 8, TS], BF16)
            SPB2 = [[AR2.get([128, TS], BF16) for _ in range(2)] for _ in range(2)]
            T1B = [[CT[0][1], AR2.get([128, TS], F32)], [CT[1][1], AR2.get([128, TS], F32)]]
            MO2 = AR2.get([128, TS], BF16)

            def make_xtq(j):
                ts("dve", XTQ, XT[:, :, j * TS:(j + 1) * TS], SEL[:, 0:1], ALU.mult, [("XT", j), "const"], ["XTQ"])
                stt(XTQ, XT[:, :, (4 + j) * TS:(5 + j) * TS], SEL[:, 1:2], XTQ, ALU.mult, ALU.add,
                    [("XT", 4 + j), "const", "XTQ"], ["XTQ"])

            w_in_l = dd["w_in"][l].rearrange("(k p) n -> p k n", p=128)
            ALLB = list(range(8))
            mo_cnt = [0]

            def load_win(g):
                dma("pool", WIN[:, :, 0:GW[g]], w_in_l[:, :, GBASE[g]:GBASE[g] + GW[g]], [], ["WIN"])

            def xt_keys(tile):
                return [("XT", tile)]

            def proj_fm(dst, dk, col0, tile, eng, wkeys, dkeys, pool="M", banks=None, lw=None, own=False):
                b, bk = bank(pool, banks or [7])
                W = WIN if lw is None else lw
                for k in range(8):
                    if own:
                        mm(ps[b][0:dk, :], W[:, k, col0:col0 + dk], XTQ[:, k, :], k == 0, k == 7, ["XTQ"] + wkeys, [bk])
                    else:
                        mm(ps[b][0:dk, :], W[:, k, col0:col0 + dk], XT[:, k, tile * TS:(tile + 1) * TS], k == 0, k == 7,
                           xt_keys(tile) + wkeys, [bk])
                if dst is not None:
                    cp(eng, dst, ps[b][0:dk, :], [bk], dkeys)
                return b, bk

            def store_mix(g, h, qt, src_bank, bk, rc_ap, extra_r):
                i = mo_cnt[0] % 2
                mo_cnt[0] += 1
                mo = MO[i]
                mk = ("MO", i)
                if rc_ap is None:
                    cp("dve", mo[0:64, :], ps[src_bank][0:64, :], [bk], [mk])
                else:
                    tt("dve", mo[0:64, :], ps[src_bank][0:64, :], rc_ap, ALU.mult, [bk] + extra_r, [mk])
                row = "ABCD".index(g) * 256 + h * 64
                o = dma("sp", mixT_d[row:row + 64, qt * TS:(qt + 1) * TS], mo[0:64, :], [mk], [("mixT", g, h, qt)])

            def softmax_unit(g, h, qt, dk, scale, qap, qkeys, bias_fn, bias_keys, u):
                kbl = [4 * p_ + i_ for p_ in list(range(qt + 1)) + list(range(4, 4 + qt + 1)) for i_ in range(4)]
                nkb = len(kbl)
                ob = [3, 4][u % 2]
                db = [5, 6][u % 2]
                obk, dbk = ("ps", ob), ("ps", db)
                LA = 2
                sb_of = {}
                for step in range(nkb + LA):
                    if step < nkb:
                        kb = kbl[step]
                        b, bk = bank("S", [0, 1, 2])
                        sb_of[step] = (b, bk)
                        p_, i = kb // 4, kb % 4
                        wh = 0 if p_ == qt else (1 if p_ == 4 + qt else -1)
                        mm(ps[b], KT[h][0:dk, kb * 128:(kb + 1) * 128], qap, True, wh < 0,
                           [("KT", h, kb // 4)] + qkeys, [bk])
                        if wh >= 0:
                            mm(ps[b], identb, MB[:, wh, 384 - 128 * i:384 - 128 * i + 512], False, True, ["const"], [bk])
                    s2 = step - LA
                    if 0 <= s2 < nkb:
                        kb = kbl[s2]
                        b, bk = sb_of.pop(s2)
                        pi = s2 % 3
                        pt, pk = PT[pi], ("PT", pi)
                        act(pt, ps[b], AF.Exp, [bk] + bias_keys, [pk], scale=scale,
                            bias=(bias_fn(kb) if bias_fn else None))
                        mm(ps[ob][0:64, :], V[:, kb, h * 64:(h + 1) * 64], pt, s2 == 0, s2 == nkb - 1,
                           [pk, ("V", kb // 4)], [obk])
                        mm(ps[db][0:64, :], onesb[:, 0:64], pt, s2 == 0, s2 == nkb - 1, [pk, "const"], [dbk])
                recip(RC[0:64, :], ps[db][0:64, :], [dbk], ["RC"])
                store_mix(g, h, qt, ob, obk, RC[0:64, :], ["RC"])

            def sb_units(hs, qt, qaps, qkeys):
                order = [pos_tile(nt_) * 4 + i_ for nt_ in reversed(range(2 * qt + 2)) for i_ in reversed(range(4))]
                nkb = len(order)
                scale = 0.125
                zb = [[0, 1], [4, 5]]
                accb = [2, 6]
                obb = [3, 7]
                z_of = {}
                for step in range(nkb + 2):
                    if step < nkb:
                        for si, h in enumerate(hs):
                            kb = order[step]
                            b = zb[si][step % 2]
                            bk = ("ps", b)
                            z_of[(si, step)] = (b, bk)
                            p_, i = kb // 4, kb % 4
                            wh = 0 if p_ == qt else (1 if p_ == 4 + qt else -1)
                            mm(ps[b], KT[h][0:64, kb * 128:(kb + 1) * 128], qaps[si], True, wh < 0,
                               [("KT", h, kb // 4)] + qkeys[si], [bk])
                            if wh >= 0:
                                mm(ps[b], identb, MB[:, 2 + wh, 384 - 128 * i:384 - 128 * i + 512], False, True,
                                   ["const"], [bk])
                    s1 = step - 1
                    if 0 <= s1 < nkb:
                        par = s1 % 2
                        for si, h in enumerate(hs):
                            b, bk = z_of[(si, s1)]
                            act(CT[si][0], ps[b], AF.Exp, [bk], [("SP", si)], scale=scale)
                        for si, h in enumerate(hs):
                            act(SPB2[si][par], CT[si][0], AF.Ln, [("SP", si)], [("SPB", si, par)], bias=1.0)
                        for si, h in enumerate(hs):
                            b, bk = z_of.pop((si, s1))
                            stt(T1B[si][par], ps[b], scale, SPB2[si][par], ALU.mult, ALU.subtract,
                                [bk, ("SPB", si, par)], [("T1", si, par)])
                    s2 = step - 2
                    if 0 <= s2 < nkb:
                        par = s2 % 2
                        kb = order[s2]
                        for si, h in enumerate(hs):
                            mm(ps[accb[si]], TRIB[:, 0, :], SPB2[si][par], s2 == 0, True, [("SPB", si, par), "const"],
                               [("ps", accb[si])], skip_group_check=True)
                        for si, h in enumerate(hs):
                            tt("dve", T1B[si][par], ps[accb[si]], T1B[si][par], ALU.add,
                               [("ps", accb[si]), ("T1", si, par)], [("T1", si, par)])
                        if s2 < nkb - 1:
                            for si, h in enumerate(hs):
                                mm(ps[accb[si]], TRIB[:, 1, :], SPB2[si][par], False, True, [("SPB", si, par), "const"],
                                   [("ps", accb[si])], skip_group_check=True)
                        pts = []
                        for si, h in enumerate(hs):
                            pi = (s2 * 2 + si) % 3
                            pt, pk = PT[pi], ("PT", pi)
                            pts.append((pt, pk))
                            act(pt, T1B[si][par], AF.Exp, [("T1", si, par)], [pk])
                        for si, h in enumerate(hs):
                            pt, pk = pts[si]
                            mm(ps[obb[si]][0:64, :], V[:, kb, h * 64:(h + 1) * 64], pt, s2 == 0, s2 == nkb - 1,
                               [pk, ("V", kb // 4)], [("ps", obb[si])])
                for si, h in enumerate(hs):
                    store_mix("C", h, qt, obb[si], ("ps", obb[si]), None, [])

            def kv_pass(g, kcols, vcol0, nv, gates):
                KVB = list(range(7)) if gates else ALLB
                for tile in range(NT):
                    for hp in range(len(kcols) // 2):
                        b, bk = proj_fm(None, 128, kcols[2 * hp], tile, None, ["WIN"], None, banks=KVB)
                        cp("dve", KT[2 * hp][0:64, tile * TS:(tile + 1) * TS], ps[b][0:64, :], [bk], [("KT", 2 * hp, tile)])
                        P.op("act", (lambda o_, i_: (lambda e: e.activation(out=o_, in_=i_, func=AF.Copy)))(
                            KT[2 * hp + 1][0:64, tile * TS:(tile + 1) * TS], ps[b][64:128, :]), [bk], [("KT", 2 * hp + 1, tile)])
                    for j in range(4):
                        blk = tile * 4 + j
                        b, bk = bank("M", KVB)
                        n = nv
                        for k in range(8):
                            mm(ps[b][:, 0:n], XT[:, k, blk * 128:(blk + 1) * 128], WIN[:, k, vcol0:vcol0 + n],
                               k == 0, k == 7, xt_keys(tile) + ["WIN"], [bk])
                        cp("dve", V[:, blk, 0:nv], ps[b][:, 0:nv], [bk], [("V", tile)])
                        if gates:
                            for k in range(8):
                                mm(ps[7][:, blk * 4:(blk + 1) * 4], XT[:, k, blk * 128:(blk + 1) * 128],
                                   WIN[:, k, vcol0 + nv:vcol0 + nv + 4], k == 0, k == 7, xt_keys(tile) + ["WIN"],
                                   [("ps", 7)], skip_group_check=True)
                if gates:
                    cp("dve", GATE.rearrange("p a b -> p (a b)"), ps[7][:, 0:128], [("ps", 7)], ["GATE"])

            if "A" in groups:
                load_win("A")
                dma("sp", BF_, dd["fox_b_f"][l].partition_broadcast(128), [], ["BF"])
                kv_pass("A", [256, 320, 384, 448], 512, 256, AMODE != 1)
                if AMODE in (0, 3):
                    for h in range(4):
                        ts("dve", GATE[:, :, h], GATE[:, :, h], BF_[:, h:h + 1], ALU.add, ["GATE", "BF"], ["GATE"])
                    GF = GATE.rearrange("p a b -> p (a b)")
                    act(GF, GF, AF.Exp, ["GATE"], ["GATE"], scale=-1.0)
                    act(GF, GF, AF.Ln, ["GATE"], ["GATE"], bias=1.0)
                    if False:
                        pass
                    b1, bk1 = bank("M", ALLB)
                    mm(ps[b1][:, 0:128], TRI[:, 1, :], GF, True, True, ["GATE", "const"], [bk1])
                    b2, bk2 = bank("M", ALLB)
                    mm(ps[b2][:, 0:128], TRI[:, 0, :], GF, True, True, ["GATE", "const"], [bk2])
                    cp("dve", TOTs.rearrange("p a b -> p (a b)"), ps[b1][:, 0:128], [bk1], ["TOTs"])
                    if False:
                        pass
                    for n_ in range(NB - 1):
                        tt("dve", CARRYX[:, pb_nat(n_ + 1), :], CARRYX[:, pb_nat(n_), :], TOTs[:, pb_nat(n_), :], ALU.add,
                           ["TOTs", "carry0", "CARRYX"], ["CARRYX"])
                    tt("dve", CNEG, ps[b2][:, 0:128].rearrange("p (a b) -> p a b", a=NB), CARRYX, ALU.add,
                       [bk2, "CARRYX", "carry0"], ["CNEG"])
                    if False:
                        pass
                    u = 0
                    for qt in range(NS):
                        bt = BT[qt % 2]
                        btk = ("BT", qt % 2)
                        for h in range(4):
                            ts("dve", bt[:, :, h], CNEG[:, :, h], CARRYX[:, (4 + qt) * 4, h:h + 1], ALU.subtract,
                               ["CNEG", "CARRYX", "carry0"], [btk])
                        make_xtq(qt)
                        for hp in range(2):
                            b_, bk_ = proj_fm(None, 128, hp * 128, qt, None, ["WIN"], None, own=True)
                            cp("dve", QT[2 * hp][qt % 2][0:64, :], ps[b_][0:64, :], [bk_], [("QT", 2 * hp, qt % 2)])
                            P.op("act", (lambda o_, i_: (lambda e: e.activation(out=o_, in_=i_, func=AF.Copy)))(
                                QT[2 * hp + 1][qt % 2][0:64, :], ps[b_][64:128, :]), [bk_], [("QT", 2 * hp + 1, qt % 2)])
                        for h in range(4):
                            qb = qt % 2
                            qk = ("QT", h, qb)
                            softmax_unit("A", h, qt, 64, 0.125, QT[h][qb][0:64, :], [qk],
                                         (lambda kb, hh=h, bb=bt: bb[:, kb, hh:hh + 1]), [btk], u)
                            u += 1

            if "B" in groups:
                load_win("B")
                wb = GBASE["B"]
                uq = dd["mla_w_uq"][l].rearrange("(c p) n -> p c n", p=128)
                dma("pool", WUQ, uq, [], ["WUQ"])
                uq4 = uq.rearrange("p c (h x) -> p c h x", h=4)
                wq4 = WUQS.rearrange("p c (h x) -> p c h x", h=4)
                for c in range(2):
                    dma("pool", wq4[:, c, :, 0:64], uq4[:, c, :, 0:64], [], ["WUQS"])
                    dma("pool", wq4[:, c, :, 64:80], uq4[:, c, :, 80:96], [], ["WUQS"])
                    dma("pool", wq4[:, c, :, 80:96], uq4[:, c, :, 64:80], [], ["WUQS"])
                dma("pool", WUKV, dd["mla_w_ukv"][l], [], ["WUKV"])
                dma("pool", WKRS[:, :, 0:64], w_in_l[:, :, wb + 320:wb + 384], [], ["WKRS"])
                dma("pool", WKRS[:, :, 64:80], w_in_l[:, :, wb + 400:wb + 416], [], ["WKRS"])
                dma("pool", WKRS[:, :, 80:96], w_in_l[:, :, wb + 384:wb + 400], [], ["WKRS"])
                dma("sp", GQ, dd["mla_g_q"][l].rearrange("(c p) -> p c", p=128), [], ["GQ"], allow_slow_non_contiguous=True)
                dma("sp", GKV, dd["mla_g_kv"][l].rearrange("(c p) -> p c", p=128), [], ["GKV"], allow_slow_non_contiguous=True)
                ts("dve", GQ, GQ, 16.0, ALU.mult, ["GQ"], ["GQ"])
                ts("dve", GKV, GKV, float(np.sqrt(128.0)), ALU.mult, ["GKV"], ["GKV"])
                WUKV4 = WUKV.rearrange("p (h x) -> p h x", h=4)

                def load_rope(tile, own=False):
                    src_ = dd["ropeq"] if own else dd["rope"]
                    dma("sp", ROPE[64:96, 0, :], src_[0, :, tile * TS:(tile + 1) * TS], [], ["ROPE"])
                    dma("sp", ROPE[64:96, 1, :], src_[1, :, tile * TS:(tile + 1) * TS], [], ["ROPE"])

                def rope_apply(ba, bka, bb, bkb, dsts, dkeys):
                    tt("dve", RT1[64:96, :], ps[ba][64:96, :], ROPE[64:96, 0, :], ALU.mult, [bka, "ROPE"], ["RT1"])
                    tt("dve", RT2[64:96, :], ps[bb][64:96, :], ROPE[64:96, 1, :], ALU.mult, [bkb, "ROPE"], ["RT2"])
                    for dst, dk_ in zip(dsts, dkeys):
                        tt("dve", dst, RT1[64:96, :], RT2[64:96, :], ALU.add, ["RT1", "RT2"], [dk_])

                for tile in range(NT):
                    ba, bka = proj_fm(None, 128, 256, tile, None, ["WIN"], None, banks=ALLB)
                    act(SQ[:, 0, :], ps[ba], AF.Square, [bka], ["SQ"])
                    bs, bks = bank("M", ALLB)
                    mm(ps[bs], onesb, SQ[:, 0, :], True, True, ["SQ", "const"], [bks])
                    rsqrt(RSB, ps[bs], 128.0 * 1e-6, [bks], ["RSB"])
                    stt(CKVN, ps[ba], GKV[:, 0:1], RSB, ALU.mult, ALU.mult, [bka, "RSB", "GKV"], ["CKVN"])
                    for h in range(4):
                        b, bk = bank("M", ALLB)
                        mm(ps[b][0:64, :], WUKV[:, h * 128:h * 128 + 64], CKVN, True, True, ["CKVN", "WUKV"], [bk])
                        cp("dve", KT[h][0:64, tile * TS:(tile + 1) * TS], ps[b][0:64, :], [bk], [("KT", h, tile)])
                    for j in range(4):
                        blk = tile * 4 + j
                        b, bk = bank("M", ALLB)
                        mm(ps[b][:, 0:256].rearrange("p (h x) -> p h x", h=4), CKVN[:, j * 128:(j + 1) * 128],
                           WUKV4[:, :, 64:128], True, True, ["CKVN", "WUKV"], [bk])
                        cp("dve", V[:, blk, :], ps[b][:, 0:256], [bk], [("V", tile)])
                    bc, bkc = proj_fm(None, 96, 320, tile, None, ["WIN"], None, banks=ALLB)
                    bd_, bkd = proj_fm(None, 96, 0, tile, None, ["WKRS"], None, banks=ALLB, lw=WKRS)
                    load_rope(tile)
                    rope_apply(bc, bkc, bd_, bkd, [KT[h][64:96, tile * TS:(tile + 1) * TS] for h in range(4)],
                               [("KT", h, tile) for h in range(4)])
                u = 0
                sc_b = float(96.0 ** -0.5)
                for qt in range(NS):
                    make_xtq(qt)
                    bq = []
                    for c in range(2):
                        bq.append(proj_fm(None, 128, c * 128, qt, None, ["WIN"], None, banks=[7, 0, 1, 2], own=True))
                        act(SQ[:, c, :], ps[bq[c][0]], AF.Square, [bq[c][1]], ["SQ"])
                    bs, bks = bank("M", [7, 0, 1, 2])
                    for c in range(2):
                        mm(ps[bs], onesb, SQ[:, c, :], c == 0, c == 1, ["SQ", "const"], [bks])
                    rsqrt(RSB, ps[bs], 256.0 * 1e-6, [bks], ["RSB"])
                    for c in range(2):
                        stt(CQN[:, c, :], ps[bq[c][0]], GQ[:, c:c + 1], RSB, ALU.mult, ALU.mult,
                            [bq[c][1], "RSB", "GQ"], ["CQN"])
                    load_rope(qt, own=True)
                    for h in range(4):
                        qb = u % 2
                        qk = ("QT", h, qb)
                        be, bke = bank("M", [7, 0, 1, 2])
                        for c in range(2):
                            mm(ps[be][0:96, :], WUQ[:, c, h * 96:(h + 1) * 96], CQN[:, c, :], c == 0, c == 1,
                               ["CQN", "WUQ"], [bke])
                        bf_, bkf = bank("M", [7, 0, 1, 2])
                        for c in range(2):
                            mm(ps[bf_][0:96, :], WUQS[:, c, h * 96:(h + 1) * 96], CQN[:, c, :], c == 0, c == 1,
                               ["CQN", "WUQS"], [bkf])
                        cp("dve", QT[h][qb][0:64, :], ps[be][0:64, :], [bke], [qk])
                        rope_apply(be, bke, bf_, bkf, [QT[h][qb][64:96, :]], [qk])
                        softmax_unit("B", h, qt, 96, sc_b, QT[h][qb][0:96, :], [qk], None, [], u)
                        u += 1

            if "C" in groups:
                load_win("C")
                kv_pass("C", [256, 320, 384, 448], 512, 256, False)
                for qt in range(NS):
                    make_xtq(qt)
                    for pair in range(2):
                        hs = [2 * pair, 2 * pair + 1]
                        qaps, qkeys = [], []
                        qb = qt % 2
                        b_, bk_ = proj_fm(None, 128, pair * 128, qt, None, ["WIN"], None, banks=[0, 1, 4, 5], own=True)
                        cp("dve", QT[hs[0]][qb][0:64, :], ps[b_][0:64, :], [bk_], [("QT", hs[0], qb)])
                        P.op("act", (lambda o_, i_: (lambda e: e.activation(out=o_, in_=i_, func=AF.Copy)))(
                            QT[hs[1]][qb][0:64, :], ps[b_][64:128, :]), [bk_], [("QT", hs[1], qb)])
                        for h in hs:
                            qk = ("QT", h, qb)
                            qaps.append(QT[h][qb][0:64, :])
                            qkeys.append([qk])
                        sb_units(hs, qt, qaps, qkeys)

            if "D" in groups:
                load_win("D")
                dma("sp", ES, dd["swa_sinks"][l].partition_broadcast(128), [], ["ES"])
                act(ES, ES, AF.Exp, ["ES"], ["ES"])
                kv_pass("D", [256, 320], 384, 128, False)
                steps = [(js, h, half, j) for js in range(NS) for h in range(4) for half in range(2) for j in range(4)]
                sstate = {}

                def d_unit(js, h, half):
                    return (js * 4 + h) * 2 + half

                def d_stage_s(st_):
                    js, h, half, j = st_
                    u = d_unit(js, h, half)
                    kvh = h // 2
                    nt = 2 * js + half
                    qb = u % 2
                    qk = ("QT", h, qb)
                    if j == 0:
                        proj_fm(QT[h][qb][0:64, :], 64, h * 64, pos_tile(nt), "dve", ["WIN"], [qk])
                    n = 4 * nt + j
                    pn = pb_nat(n)
                    pm = pb_nat(n - 1) if n > 0 else 0
                    b, bk = bank("S", [0, 1, 2])
                    qa = QT[h][qb][0:64, j * 128:(j + 1) * 128]
                    if n > 0:
                        mm(ps[b][:, 0:128], KT[kvh][0:64, pm * 128:(pm + 1) * 128], qa, True, True,
                           [("KT", kvh, pm // 4), qk], [bk])
                    mm(ps[b][:, 128:256], KT[kvh][0:64, pn * 128:(pn + 1) * 128], qa, True, True,
                       [("KT", kvh, pn // 4), qk], [bk])
                    sstate[st_] = (b, bk)

                def d_stage_rest(st_, idx, phase):
                    js, h, half, j = st_
                    u = d_unit(js, h, half)
                    kvh = h // 2
                    nt = 2 * js + half
                    n = 4 * nt + j
                    pn = pb_nat(n)
                    pm = pb_nat(n - 1) if n > 0 else 0
                    lo = 0 if n > 0 else 128
                    ob = [3, 4][u % 2]
                    db = [5, 6][u % 2]
                    obk, dbk = ("ps", ob), ("ps", db)
                    si = idx % 2
                    sbd, pd = SBD[si], PD[si]
                    if phase == 2:
                        b, bk = sstate.pop(st_)
                        stt(sbd[:, lo:256], ps[b][:, lo:256], 0.125, BIASD[:, h, lo:256], ALU.mult, ALU.add,
                            [bk, "const"], [("SBD", si)])
                        act(pd[:, lo:256], sbd[:, lo:256], AF.Exp, [("SBD", si)], [("PD", si)])
                        return
                    oc = slice(j * 128, (j + 1) * 128)
                    if n > 0:
                        mm(ps[ob][0:64, oc], V[:, pm, kvh * 64:(kvh + 1) * 64], pd[:, 0:128], True, False,
                           [("PD", si), ("V", pm // 4)], [obk])
                        mm(ps[db][0:64, oc], onesb[:, 0:64], pd[:, 0:128], True, False, [("PD", si), "const"], [dbk])
                    mm(ps[ob][0:64, oc], V[:, pn, kvh * 64:(kvh + 1) * 64], pd[:, 128:256], n == 0, True,
                       [("PD", si), ("V", pn // 4)], [obk])
                    mm(ps[db][0:64, oc], onesb[:, 0:64], pd[:, 128:256], n == 0, True, [("PD", si), "const"], [dbk])
                    if j == 3:
                        ts("dve", RC[0:64, :], ps[db][0:64, :], ES[0:64, h:h + 1], ALU.add, [dbk, "ES"], ["RC"])
                        act(RC[0:64, :], RC[0:64, :], AF.Ln, ["RC"], ["RC"])
                        act(RC[0:64, :], RC[0:64, :], AF.Exp, ["RC"], ["RC"], scale=-1.0)
                        tt("dve", MO[half][0:64, :], ps[ob][0:64, :], RC[0:64, :], ALU.mult, [obk, "RC"], [("MO", half)])
                        if half == 1:
                            ts("dve", MO2[0:64, :], MO[0][0:64, :], SEL[0:64, 0:1], ALU.mult, [("MO", 0), "const"], ["MO2"])
                            stt(MO2[0:64, :], MO[1][0:64, :], SEL[0:64, 1:2], MO2[0:64, :], ALU.mult, ALU.add,
                                [("MO", 1), "const", "MO2"], ["MO2"])
                            row = 3 * 256 + h * 64
                            dma("sp", mixT_d[row:row + 64, js * TS:(js + 1) * TS], MO2[0:64, :], ["MO2"],
                                [("mixT", "D", h, js)])

                nst = len(steps)
                for idx in range(nst + 2):
                    if idx < nst:
                        d_stage_s(steps[idx])
                    if 0 <= idx - 1 < nst:
                        d_stage_rest(steps[idx - 1], idx - 1, 2)
                    if 0 <= idx - 2 < nst:
                        d_stage_rest(steps[idx - 2], idx - 2, 3)

        def phase2_(l, last):
            AR1.reset()
            AR2.reset()
            AT = AR1.get([128, NFC, TS], BF16)
            MX = AR1.get([128, 8, TS], BF16)
            X1 = AR1.get([128, 4, D], F32)
            WOS = AR2.get([128, 8, D], BF16)
            WST = [AR2.get([128, D], F32) for _ in range(2)]
            LNV = [AR2.get([128, D], F32) for _ in range(2)]
            WG = [AR2.get([128, 8, 256], BF16) for _ in range(2)]
            WU = [AR2.get([128, 8, 256], BF16) for _ in range(2)]
            WD = [AR2.get([128, 2, TS], BF16) for _ in range(2)]
            X1T = AR2.get([128, 8, TS], BF16)
            SQ2 = [AR2.get([128, TS], BF16) for _ in range(2)]
            ACCY = AR2.get([128, D], F32)
            U = AR2.get([128, D], F32)
            XB = AR2.get([128, D], F32)
            SG = [AR2.get([128, TS], F32) for _ in range(2)]
            RS = AR2.get([128, 4, 4], F32)
            STAT = AR2.get([128, 2, 6], F32)
            MV = AR2.get([128, 4], F32)

            dma("sp", MG, dd["mix_g"][l].rearrange("(c p) -> p c", p=128), [], ["MG"], allow_slow_non_contiguous=True)
            wo_l = dd["w_o"][l].rearrange("(c p) n -> p c n", p=128)
            for c in range(8):
                dma("sp", WST[c % 2], wo_l[:, c, :], [], [("WST", c % 2)])
                ts("dve", WOS[:, c, :], WST[c % 2], MG[:, c:c + 1], ALU.mult, [("WST", c % 2), "MG"], ["WOS"])
            wg_l = dd["w_gate"][l].rearrange("(k p) n -> p k n", p=128)
            wu_l = dd["w_up"][l].rearrange("(k p) n -> p k n", p=128)
            wd_l = dd["w_down"][l].rearrange("(c p) n -> p c n", p=128)
            mixT_v = mixT_d.rearrange("(c p) t -> p c t", p=128)
            src_x = dd["xq"] if l == 0 else xs_d
            dst_x = out_d if last else xs_d
            lnvi = [0]

            def load_vec(name):
                i = lnvi[0] % 2
                lnvi[0] += 1
                dma("sp", LNV[i], dd[name][l].partition_broadcast(128), [], [("LNV", i)])
                return LNV[i], ("LNV", i)

            def layernorm(src, srck, dst, dstk, gk, bk_):
                g_ap, gkey = gk
                b_ap, bkey = bk_
                for hf in range(2):
                    P.op("dve", (lambda hh: (lambda e: e.bn_stats(out=STAT[:, hh, :], in_=src[:, hh * 512:(hh + 1) * 512])))(hf),
                         [srck], ["STAT"])
                P.op("dve", lambda e: e.bn_aggr(out=MV[:, 0:2], in_=STAT.rearrange("p a b -> p (a b)")), ["STAT"], ["MV"])
                rsqrt(MV[:, 2:3], MV[:, 1:2], 1e-5, ["MV"], ["MV"])
                ts("dve", MV[:, 3:4], MV[:, 0:1], MV[:, 2:3], ALU.mult, ["MV"], ["MV"], s2=-1.0, op1=ALU.mult)
                act(dst, src, AF.Identity, [srck, "MV"], [dstk], scale=MV[:, 2:3], bias=MV[:, 3:4])
                tt("dve", dst, dst, g_ap, ALU.mult, [dstk, gkey], [dstk])
                tt("dve", dst, dst, b_ap, ALU.add, [dstk, bkey], [dstk])

            def transpose_to(src, srck, dstT, col0, dstk):
                for half in range(2):
                    b, bk = bank("M2", [4, 5, 6, 7])
                    for j in range(4):
                        c = half * 4 + j
                        tr(ps[b][:, j * 128:(j + 1) * 128], src[:, c * 128:(c + 1) * 128], [srck], [bk])
                    cp("act", dstT[:, half * 4:half * 4 + 4, col0:col0 + 128],
                       ps[b].rearrange("p (a b) -> p a b", a=4), [bk], [dstk])

            if S2 <= 1:
                return
            for tile in range(NS):
                mxk = "MX"
                dma("sp", MX, mixT_v[:, :, tile * TS:(tile + 1) * TS],
                    [("mixT", g, h, tile) for g in "ABCD" for h in range(4)], [mxk])
                g1 = load_vec("ln1_g")
                b1 = load_vec("ln1_b")
                for c in range(8):
                    sq = SQ2[c % 2]
                    sqk = ("SQ2", c % 2)
                    tt("dve", sq, MX[:, c, :], MX[:, c, :], ALU.mult, [mxk], [sqk])
                    for j in range(4):
                        mm(ps[4 + j][:, 0:4], sq[:, j * 128:(j + 1) * 128], IND[:, c, :], c == 0, c == 7,
                           [sqk, "const"], [("ps", 4 + j)])
                for j in range(4):
                    ts("dve", RS[:, j, :], ps[4 + j][:, 0:4], 1.0 / 256.0, ALU.mult, [("ps", 4 + j)], ["RS"],
                       s2=1e-6, op1=ALU.add)
                rsqrt(RS.rearrange("p a b -> p (a b)"), RS.rearrange("p a b -> p (a b)"), 0.0, ["RS"], ["RS"])
                if S2 <= 2:
                    continue
                for j in range(4):
                    blk = tile * 4 + j
                    dma("sp", XB, src_x[blk * 128:(blk + 1) * 128, :], [("xs", blk)], ["XB"])
                    for half in range(2):
                        ybs = []
                        for g in range(4):
                            b, bk = bank("Y", [0, 1, 2, 3])
                            ybs.append((b, bk))
                            for c2 in range(2):
                                c = 2 * g + c2
                                mm(ps[b], MX[:, c, j * 128:(j + 1) * 128], WOS[:, c, half * 512:(half + 1) * 512],
                                   c2 == 0, c2 == 1, [mxk, "WOS"], [bk])
                        acc = ACCY[:, half * 512:(half + 1) * 512]
                        ts("dve", acc, ps[ybs[0][0]], RS[:, j, 0:1], ALU.mult, [ybs[0][1], "RS"], ["ACCY"])
                        for g in range(1, 4):
                            stt(acc, ps[ybs[g][0]], RS[:, j, g:g + 1], acc, ALU.mult, ALU.add, [ybs[g][1], "RS", "ACCY"], ["ACCY"])
                    stt(U, XB, ALPHA, ACCY, ALU.mult, ALU.add, ["XB", "ACCY"], ["U"])
                    layernorm(U, "U", X1[:, j, :], ("X1", j), g1, b1)
                    transpose_to(X1[:, j, :], ("X1", j), X1T, j * 128, "X1T")
                if S2 <= 3:
                    continue
                g2 = load_vec("ln2_g")
                b2 = load_vec("ln2_b")
                for sc in range(11):
                    wi = sc % 2
                    dma("pool", WG[wi], wg_l[:, :, sc * 256:(sc + 1) * 256], [], [("WG", wi)])
                    dma("pool", WU[wi], wu_l[:, :, sc * 256:(sc + 1) * 256], [], [("WU", wi)])
                    for j2 in range(2):
                        fc = sc * 2 + j2
                        bg, bkg = bank("G", [0, 1])
                        bu, bku = bank("Uu", [2, 3])
                        for k in range(8):
                            mm(ps[bg], WG[wi][:, k, j2 * 128:(j2 + 1) * 128], X1T[:, k, :], k == 0, k == 7,
                               [("WG", wi), "X1T"], [bkg])
                        for k in range(8):
                            mm(ps[bu], WU[wi][:, k, j2 * 128:(j2 + 1) * 128], X1T[:, k, :], k == 0, k == 7,
                               [("WU", wi), "X1T"], [bku])
                        sg = SG[fc % 2]
                        sgk = ("SG", fc % 2)
                        act(sg, ps[bg], AF.Silu, [bkg], [sgk])
                        tt("dve", AT[:, fc, :], sg, ps[bu], ALU.mult, [sgk, bku], [("AT", fc)])
                if S2 <= 4:
                    continue
                for half in range(2):
                    fb = [4, 5, 6, 7]
                    for sc in range(11):
                        wi = (half * 11 + sc) % 2
                        dma("pool", WD[wi], wd_l[:, sc * 2:sc * 2 + 2, half * 512:(half + 1) * 512], [], [("WD", wi)])
                        for j2 in range(2):
                            fc = sc * 2 + j2
                            for j in range(4):
                                mm(ps[fb[j]], AT[:, fc, j * 128:(j + 1) * 128], WD[wi][:, j2, :], fc == 0, fc == NFC - 1,
                                   [("AT", fc), ("WD", wi)], [("ps", fb[j])])
                    for j in range(4):
                        stt(X1[:, j, half * 512:(half + 1) * 512], X1[:, j, half * 512:(half + 1) * 512], ALPHA,
                            ps[fb[j]], ALU.mult, ALU.add, [("ps", fb[j]), ("X1", j)], [("X1", j)])
                for j in range(4):
                    blk = tile * 4 + j
                    layernorm(X1[:, j, :], ("X1", j), U, "U", g2, b2)
                    o = dma("sp", dst_x[blk * 128:(blk + 1) * 128, :], U, ["U"], [("xs", blk)])
                    if last:
                        all_dma_out.append(o)
                    else:
                        transpose_to(U, "U", X1T, j * 128, "X1T")
                if not last:
                    dma("sp", xtm_f[tile].bitcast(BF16).rearrange("(c p) t -> p c t", p=128), X1T, ["X1T"], [("xtm", tile)])
                    P.cc((lambda jj: (lambda e: e.collective_compute("AllGather", ALU.bypass, replica_groups=rgroups,
                                                                     ins=[xtm_f[jj]], outs=[xta_f[jj]])))(tile),
                         [("xtm", tile)], [("xta", tile)])
                    xta_v = xta_f[tile].bitcast(BF16).rearrange("(r k p) t -> p k r t", r=2, k=8, p=128)
                    for r_ in range(2):
                        dma("sp", XT[:, :, (r_ * 4 + tile) * TS:(r_ * 4 + tile + 1) * TS], xta_v[:, :, r_, :],
                            [("xta", tile)], [("XT", r_ * 4 + tile)])

        build_xt0()
        for l in range(n_layers):
            P.barrier()
            phase1(l)
            P.barrier()
            if phase2:
                phase2_(l, l == n_layers - 1)
        if debug and not phase2:
            pass
        P.barrier()
        P.emit()
    return nc


def kernel(**inputs):
    x = np.ascontiguousarray(np.asarray(inputs["x"], dtype=np.float32))
    nc = build_program()
    base = {nm: np.ascontiguousarray(np.asarray(inputs[nm], dtype=np.float32)) for nm, _ in WNAMES}
    cst = [make_consts(0), make_consts(1)]
    in_maps = []
    for core in range(8):
        bq, r = core // 2, core % 2
        xt = x[bq].reshape(8, 512, D)
        m = dict(base)
        m.update(cst[r])
        m["x"] = np.ascontiguousarray(xt[[nat_tile(p) for p in range(8)]]).reshape(T, D)
        m["xq"] = np.ascontiguousarray(xt[[2 * j + r for j in range(4)]]).reshape(TQ, D)
        in_maps.append(m)
    res = run_bass_kernel_spmd(nc, in_maps, core_ids=list(range(8)))
    out = np.zeros((4, 8, 512, D), np.float32)
    for core in range(8):
        bq, r = core // 2, core % 2
        o = np.asarray(res.results[core]["out"], dtype=np.float32).reshape(4, 512, D)
        for j in range(4):
            out[bq, 2 * j + r] = o[j]
    return out.reshape(4, T, D)
```
